# Optimizing a Trainium2 kernel written in Bass

```python
import math
import jax, jax.numpy as jnp
from jax import lax
import numpy as np


D_MODEL = 1024
BATCH = 16
SEQ = 2048
DEPTH = 1

ATT_HEADS = 8
ATT_HEAD_DIM = 64
IDX_HEADS = 8
IDX_DIM = 64
TOPK_MAX = 256
Q_BLOCK = 128
ML_HEADS = 4
ML_HEAD_DIM = 128
ML_CHUNK = 64
CONV_WIDTH = 4
D_FF = 4 * D_MODEL
PLE_DIM = 256
ROPE_THETA = 10000.0
LN_EPS = 1e-5
DEEPNORM_ALPHA = (2.0 * DEPTH) ** 0.25
DEEPNORM_BETA = (8.0 * DEPTH) ** -0.25
IDX_W_SCALE = (IDX_HEADS ** -0.5) * (IDX_DIM ** -0.5)

ATT_W = ATT_HEADS * ATT_HEAD_DIM
IDX_QW = IDX_HEADS * IDX_DIM
ML_W = ML_HEADS * ML_HEAD_DIM
SPLIT_SPEC = (
    ('att_q', ATT_W), ('att_k', ATT_HEAD_DIM), ('att_v', ATT_HEAD_DIM),
    ('idx_q', IDX_QW), ('idx_k', IDX_DIM), ('idx_w', IDX_HEADS),
    ('ml_q', ML_W), ('ml_k', ML_W), ('ml_v', ML_W),
    ('ml_i', ML_HEADS), ('ml_f', ML_HEADS), ('ml_o', ML_W),
    ('gate_a', D_MODEL), ('gate_b', D_MODEL),
)
SPLIT_NAMES = tuple(n for n, _ in SPLIT_SPEC)
SPLIT_OFFSETS = tuple(int(o) for o in np.cumsum([w for _, w in SPLIT_SPEC])[:-1])
W_IN_COLS = sum(w for _, w in SPLIT_SPEC)

kernel_name = 'hybrid_dsa_mlstm_block'


def layer_norm(x, g, b):
    xf = x.astype(jnp.float32)
    mu = jnp.mean(xf, axis=-1, keepdims=True)
    var = jnp.mean(jnp.square(xf - mu), axis=-1, keepdims=True)
    y = (xf - mu) * lax.rsqrt(var + LN_EPS) * g.astype(jnp.float32) + b.astype(jnp.float32)
    return y.astype(x.dtype)


def rope_tables(positions, dim):
    inv_freq = 1.0 / (ROPE_THETA ** (jnp.arange(0, dim, 2, dtype=jnp.float32) / dim))
    ang = positions.astype(jnp.float32)[..., None] * inv_freq
    return jnp.cos(ang), jnp.sin(ang)


def apply_rope(x, cos, sin):
    xf = x.astype(jnp.float32)
    x1, x2 = jnp.split(xf, 2, axis=-1)
    c = cos[:, :, None, :]
    s = sin[:, :, None, :]
    return jnp.concatenate([x1 * c - x2 * s, x2 * c + x1 * s], axis=-1).astype(x.dtype)


def dsa_attention(q, k, v, qi, ki, wi):
    B, S = q.shape[0], q.shape[1]
    k_sel = min(TOPK_MAX, S // 4)
    nb = S // Q_BLOCK
    kv = jnp.concatenate([k, v], axis=-1)
    ki32 = ki.astype(jnp.float32)
    key_pos = jnp.arange(S)

    def to_blocks(a):
        return jnp.moveaxis(a.reshape((B, nb, Q_BLOCK) + a.shape[2:]), 1, 0)

    def block(args):
        qb, qib, wb, t0 = args
        tq = t0 + jnp.arange(Q_BLOCK)
        causal = key_pos[None, :] <= tq[:, None]
        logits = jnp.einsum('bthd,bsd->bths', qib.astype(jnp.float32), ki32)
        score = jnp.einsum('bths,bth->bts', jax.nn.relu(logits), wb.astype(jnp.float32) * IDX_W_SCALE)
        score = jnp.where(causal[None], score, -jnp.inf)
        _, idx = lax.top_k(score, k_sel)
        kvg = jax.vmap(lambda a, i: a[i])(kv, idx)
        kg, vg = jnp.split(kvg, 2, axis=-1)
        valid = idx <= tq[None, :, None]
        s = jnp.einsum('bthd,btkd->bthk', qb, kg).astype(jnp.float32) * (ATT_HEAD_DIM ** -0.5)
        s = jnp.where(valid[:, :, None, :], s, -jnp.inf)
        pr = jax.nn.softmax(s, axis=-1)
        return jnp.einsum('bthk,btkd->bthd', pr.astype(vg.dtype), vg)

    out = lax.map(block, (to_blocks(q), to_blocks(qi), to_blocks(wi), jnp.arange(nb) * Q_BLOCK))
    return jnp.moveaxis(out, 0, 1).reshape(B, S, ATT_W)


def causal_dwconv(x, w, b):
    C = x.shape[-1]
    y = lax.conv_general_dilated(x, w[:, None, :], window_strides=(1,), padding=[(CONV_WIDTH - 1, 0)],
                                 dimension_numbers=('NWC', 'WIO', 'NWC'), feature_group_count=C)
    return y + b


def mlstm(q, k, v, i_pre, f_pre):
    B, S, H, d = q.shape
    L = ML_CHUNK
    nc = S // L

    def chunks4(a):
        return a.astype(jnp.float32).reshape(B, nc, L, H, d).transpose(1, 0, 3, 2, 4)

    def chunks3(a):
        return a.astype(jnp.float32).reshape(B, nc, L, H).transpose(1, 0, 3, 2)

    log_f = jax.nn.log_sigmoid(f_pre.astype(jnp.float32))
    tril = jnp.tril(jnp.ones((L, L), dtype=bool))

    def step(carry, xs):
        C, n, m = carry
        qc, kc, vc, ic, lfc = xs
        b = jnp.cumsum(lfc, axis=-1)
        dmat = b[..., :, None] - b[..., None, :] + ic[..., None, :]
        dmat = jnp.where(tril, dmat, -jnp.inf)
        inter = b + m[..., None]
        m_row = jnp.maximum(jnp.max(dmat, axis=-1), inter)
        w_intra = jnp.exp(dmat - m_row[..., None])
        w_inter = jnp.exp(inter - m_row)
        s = jnp.einsum('bhld,bhsd->bhls', qc, kc) * w_intra
        num = w_inter[..., None] * jnp.einsum('bhld,bhde->bhle', qc, C) + jnp.einsum('bhls,bhse->bhle', s, vc)
        den = w_inter * jnp.einsum('bhld,bhd->bhl', qc, n) + jnp.sum(s, axis=-1)
        h = num / jnp.maximum(jnp.abs(den), jnp.exp(-m_row))[..., None]
        b_last = b[..., -1]
        g = b_last[..., None] - b + ic
        m_new = jnp.maximum(b_last + m, jnp.max(g, axis=-1))
        w_k = jnp.exp(g - m_new[..., None])
        decay = jnp.exp(b_last + m - m_new)
        C = decay[..., None, None] * C + jnp.einsum('bhs,bhsd,bhse->bhde', w_k, kc, vc)
        n = decay[..., None] * n + jnp.einsum('bhs,bhsd->bhd', w_k, kc)
        return (C, n, m_new), h

    init = (jnp.zeros((B, H, d, d), jnp.float32), jnp.zeros((B, H, d), jnp.float32), jnp.zeros((B, H), jnp.float32))
    _, hs = lax.scan(step, init, (chunks4(q), chunks4(k), chunks4(v), chunks3(i_pre), chunks3(log_f)))
    return hs.transpose(1, 0, 3, 2, 4).reshape(B, S, H, d)


def head_norm(h, g):
    mu = jnp.mean(h, axis=-1, keepdims=True)
    var = jnp.mean(jnp.square(h - mu), axis=-1, keepdims=True)
    return (h - mu) * lax.rsqrt(var + LN_EPS) * g.astype(jnp.float32).reshape(ML_HEADS, ML_HEAD_DIM)


def setup_inputs(seed: int = 0) -> dict:
    key = jax.random.key(seed)
    ks = jax.random.split(key, 24)
    f32 = jnp.float32
    nrm = lambda k, shape, scale: jax.random.normal(k, shape, f32) * scale
    offsets = jax.random.randint(ks[2], (BATCH,), 0, 4096, dtype=jnp.int32)
    positions = offsets[:, None] + jnp.arange(SEQ, dtype=jnp.int32)[None, :]
    b_f = jnp.broadcast_to(jnp.linspace(3.0, 6.0, ML_HEADS, dtype=f32), (DEPTH, ML_HEADS))
    return {
        'x': nrm(ks[0], (BATCH, SEQ, D_MODEL), 1.0),
        'p': nrm(ks[1], (DEPTH, BATCH, SEQ, PLE_DIM), 1.0),
        'positions': positions,
        'w_in': nrm(ks[3], (DEPTH, D_MODEL, W_IN_COLS), D_MODEL ** -0.5),
        'conv_w': nrm(ks[4], (DEPTH, CONV_WIDTH, 2 * ML_W), CONV_WIDTH ** -0.5),
        'conv_b': nrm(ks[5], (DEPTH, 2 * ML_W), 0.01),
        'b_igate': nrm(ks[6], (DEPTH, ML_HEADS), 0.1),
        'b_fgate': b_f + nrm(ks[7], (DEPTH, ML_HEADS), 0.1),
        'ml_norm_g': 1.0 + nrm(ks[8], (DEPTH, ML_W), 0.02),
        'w_up_a': nrm(ks[9], (DEPTH, ATT_W, D_MODEL), ATT_W ** -0.5),
        'w_up_b': nrm(ks[10], (DEPTH, ML_W, D_MODEL), ML_W ** -0.5),
        'w_out': nrm(ks[11], (DEPTH, D_MODEL, D_MODEL), D_MODEL ** -0.5 * DEEPNORM_BETA),
        'ln1_g': 1.0 + nrm(ks[12], (DEPTH, D_MODEL), 0.02),
        'ln1_b': nrm(ks[13], (DEPTH, D_MODEL), 0.02),
        'w_ff1': nrm(ks[14], (DEPTH, D_MODEL, D_FF), D_MODEL ** -0.5),
        'w_ff2': nrm(ks[15], (DEPTH, D_FF, D_MODEL), D_FF ** -0.5 * DEEPNORM_BETA),
        'w_ple_gate': nrm(ks[16], (DEPTH, D_MODEL, D_MODEL), D_MODEL ** -0.5),
        'w_ple_proj': nrm(ks[17], (DEPTH, PLE_DIM, D_MODEL), PLE_DIM ** -0.5 * DEEPNORM_BETA),
        'ln2_g': 1.0 + nrm(ks[18], (DEPTH, D_MODEL), 0.02),
        'ln2_b': nrm(ks[19], (DEPTH, D_MODEL), 0.02),
    }


def reference(x, p, positions, w_in, conv_w, conv_b, b_igate, b_fgate, ml_norm_g, w_up_a, w_up_b, w_out,
              ln1_g, ln1_b, w_ff1, w_ff2, w_ple_gate, w_ple_proj, ln2_g, ln2_b):
    B, S, _ = x.shape
    cos_a, sin_a = rope_tables(positions, ATT_HEAD_DIM)
    cos_i, sin_i = rope_tables(positions, IDX_DIM)
    h = x
    for l in range(DEPTH):
        parts = dict(zip(SPLIT_NAMES, jnp.split(h @ w_in[l], SPLIT_OFFSETS, axis=-1)))
        q_a = apply_rope(parts['att_q'].reshape(B, S, ATT_HEADS, ATT_HEAD_DIM), cos_a, sin_a)
        k_a = apply_rope(parts['att_k'][:, :, None, :], cos_a, sin_a)[:, :, 0]
        q_i = apply_rope(parts['idx_q'].reshape(B, S, IDX_HEADS, IDX_DIM), cos_i, sin_i)
        k_i = apply_rope(parts['idx_k'][:, :, None, :], cos_i, sin_i)[:, :, 0]
        y_a = dsa_attention(q_a, k_a, parts['att_v'], q_i, k_i, parts['idx_w'])
        qk = jax.nn.silu(causal_dwconv(jnp.concatenate([parts['ml_q'], parts['ml_k']], axis=-1), conv_w[l], conv_b[l]))
        mq, mk = jnp.split(qk, 2, axis=-1)
        mq = mq.reshape(B, S, ML_HEADS, ML_HEAD_DIM)
        mk = mk.reshape(B, S, ML_HEADS, ML_HEAD_DIM) * (ML_HEAD_DIM ** -0.5)
        mv = parts['ml_v'].reshape(B, S, ML_HEADS, ML_HEAD_DIM)
        hm = mlstm(mq, mk, mv, parts['ml_i'] + b_igate[l], parts['ml_f'] + b_fgate[l])
        hm = head_norm(hm, ml_norm_g[l]).reshape(B, S, ML_W).astype(x.dtype)
        y_b = jax.nn.sigmoid(parts['ml_o']) * hm
        merged = jax.nn.sigmoid(parts['gate_a']) * (y_a @ w_up_a[l]) + jax.nn.sigmoid(parts['gate_b']) * (y_b @ w_up_b[l])
        h = layer_norm(DEEPNORM_ALPHA * h + merged @ w_out[l], ln1_g[l], ln1_b[l])
        ff = jnp.square(jax.nn.relu(h @ w_ff1[l])) @ w_ff2[l]
        r = DEEPNORM_ALPHA * h + ff
        r = r + jax.nn.sigmoid(r @ w_ple_gate[l]) * (p[l] @ w_ple_proj[l])
        h = layer_norm(r, ln2_g[l], ln2_b[l])
    return h
```

```python
import math
from contextlib import ExitStack

import numpy as np
import concourse.bass as bass
import concourse.mybir as mybir
from concourse.bass_utils import run_bass_kernel_spmd

F32 = mybir.dt.float32
BF16 = mybir.dt.bfloat16
I32 = mybir.dt.int32
ALU = mybir.AluOpType
AF = mybir.ActivationFunctionType
AX = mybir.AxisListType

NCORES = 8
D = 1024
S = 2048
NT = S // 128
NSEQ = 2
DFF = 4096
PLE = 256
TOPK = 256
ALPHA = 2.0 ** 0.25
LN_EPS = 1e-5
BIG = 30000.0
NEG = -1.0e30
PI = math.pi
NBIS = 22
BIS_LO = -256.0 - 2.0 ** -14
BIS_HI = BIS_LO + 512.0

OFF = {}
_o = 0
for _n, _w in (('att_q', 512), ('att_k', 64), ('att_v', 64), ('idx_q', 512), ('idx_k', 64), ('idx_w', 8),
               ('ml_q', 512), ('ml_k', 512), ('ml_v', 512), ('ml_i', 4), ('ml_f', 4), ('ml_o', 512),
               ('gate_a', 1024), ('gate_b', 1024)):
    OFF[_n] = (_o, _w)
    _o += _w

G_QA, G_QAP, G_KA, G_KAP, G_QI, G_QIP, G_KI, G_KIP, G_MLQ, G_MLK = 0, 4, 8, 9, 10, 14, 18, 19, 20, 24
NG_FM = 28
TM_COLS = 80 + 512 + 512


class Buf:
    __slots__ = ("name", "w", "r")

    def __init__(self, name=""):
        self.name = name
        self.w = {}
        self.r = {}


class Sched:
    def __init__(self, nc, ctx):
        self.nc = nc
        self.ctx = ctx
        self.engs = {"pe": nc.tensor, "dve": nc.vector, "act": nc.scalar, "pool": nc.gpsimd, "sp": nc.sync}
        self.sem = {k: ctx.enter_context(nc.semaphore("s_" + k)) for k in ("pe", "dve", "act", "pool")}
        self.cnt = {k: 0 for k in self.sem}
        self.seen = {k: {} for k in self.engs}
        self.pending = {k: [] for k in self.sem}
        self.dsem = {}
        self.nops = {k: 0 for k in self.engs}

    def _wait(self, eng, deps):
        e = self.engs[eng]
        for key, (sem, val) in deps.items():
            if self.seen[eng].get(key, 0) >= val:
                continue
            e.wait_ge(sem, val)
            self.seen[eng][key] = val

    def _deps(self, eng, reads, writes):
        deps = {}

        def need(key, tok):
            if key not in deps or deps[key][1] < tok[1]:
                deps[key] = tok

        for b in reads:
            for key, tok in b.w.items():
                if key == eng and eng == "pe":
                    continue
                need(key, tok)
        for b in writes:
            for key, tok in b.w.items():
                if key == eng:
                    continue
                need(key, tok)
            for key, tok in b.r.items():
                if key == eng:
                    continue
                need(key, tok)
        return deps

    def op(self, eng, fn, reads=(), writes=(), inc=True):
        deps = self._deps(eng, reads, writes)
        self._wait(eng, deps)
        ins = fn(self.engs[eng])
        self.nops[eng] += 1
        self.pending[eng].append((tuple(reads), tuple(writes)))
        if inc:
            self.cnt[eng] += 1
            ins.then_inc(self.sem[eng], 1)
            tok = (self.sem[eng], self.cnt[eng])
            for rd, wr in self.pending[eng]:
                for b in rd:
                    b.r[eng] = tok
                for b in wr:
                    b.w = {eng: tok}
                    b.r = {}
            self.pending[eng] = []
        return ins

    def new_dma_sem(self, name):
        sem = self.ctx.enter_context(self.nc.semaphore("d_" + name))
        self.dsem[name] = [sem, 0]
        return name

    def dma(self, queue, semname, out, in_, reads=(), writes=()):
        deps = self._deps("dma:" + semname, reads, writes)
        self._wait(queue, deps)
        st = self.dsem[semname]
        st[1] += 16
        self.engs[queue].dma_start(out=out, in_=in_).then_inc(st[0], 16)
        self.nops[queue] += 1
        tok = (st[0], st[1])
        key = "dma:" + semname
        for b in reads:
            b.r[key] = tok
        for b in writes:
            b.w = {key: tok}
            b.r = {}

    def wait_all(self, eng, bufs):
        deps = {}
        for b in bufs:
            for key, tok in list(b.w.items()):
                if key not in deps or deps[key][1] < tok[1]:
                    deps[key] = tok
        self._wait(eng, deps)


def _bf16_round(a):
    return a


WFAM = [
    ("fm", NG_FM, 8, 128), ("tms", 1, 8, 80), ("mlv", 1, 8, 512), ("mlo", 1, 8, 512),
    ("gate", 16, 8, 128), ("upa", 8, 4, 128), ("upb", 8, 4, 128), ("wout", 8, 8, 128),
    ("ff1", 32, 8, 128), ("ff2", 8, 32, 128), ("pleg", 8, 8, 128), ("plep", 8, 2, 128),
]
WINFO = {}
_off = 0
for _n, _g, _kc, _c in WFAM:
    WINFO[_n] = (_off, _g, _kc, _c)
    _off += _g * _kc * _c
WTOT = _off

C_ID, C_TRI, C_CM, C_ONE, C_NEGI = 0, 128, 256, 384, 512
CST = 640
V_CW, V_CB, V_L1G, V_L1B, V_L2G, V_L2B, V_INVF, V_SGN, V_BI, V_BF, V_MLG = 0, 32, 40, 48, 56, 64, 72, 73, 74, 78, 82
VEC = 82 + 512


def _group_layout(wmat, kc, cols):
    return np.ascontiguousarray(wmat.reshape(kc, 128, cols).transpose(1, 0, 2).reshape(128, kc * cols))


def host_weights(w_in, w_up_a, w_up_b, w_out, w_ff1, w_ff2, w_ple_gate, w_ple_proj):
    wall = np.zeros((128, WTOT), np.float32)

    def put(fam, g, wmat):
        off, ng, kc, cols = WINFO[fam]
        o = off + g * kc * cols
        wall[:, o:o + kc * cols] = _group_layout(np.ascontiguousarray(wmat), kc, cols)

    def cols_of(name):
        o, w = OFF[name]
        return w_in[:, o:o + w]

    def perm64(m):
        K, N = m.shape
        mm = m.reshape(K, N // 64, 2, 32)
        return mm[:, :, ::-1, :].reshape(K, N)

    qa, ka, qi, ki = cols_of('att_q'), cols_of('att_k'), cols_of('idx_q'), cols_of('idx_k')
    qap, kap, qip, kip = perm64(qa), perm64(ka), perm64(qi), perm64(ki)
    for g in range(4):
        put("fm", G_QA + g, qa[:, g * 128:(g + 1) * 128])
        put("fm", G_QAP + g, qap[:, g * 128:(g + 1) * 128])
        put("fm", G_QI + g, qi[:, g * 128:(g + 1) * 128])
        put("fm", G_QIP + g, qip[:, g * 128:(g + 1) * 128])
        put("fm", G_MLQ + g, cols_of('ml_q')[:, g * 128:(g + 1) * 128])
        put("fm", G_MLK + g, cols_of('ml_k')[:, g * 128:(g + 1) * 128])
    put("fm", G_KA, np.concatenate([ka, ka], axis=1))
    put("fm", G_KAP, np.concatenate([kap, kap], axis=1))
    put("fm", G_KI, np.concatenate([ki, ki], axis=1))
    put("fm", G_KIP, np.concatenate([kip, kip], axis=1))
    put("tms", 0, np.concatenate([cols_of('att_v'), cols_of('idx_w'), cols_of('ml_i'), cols_of('ml_f')], axis=1))
    put("mlv", 0, cols_of('ml_v'))
    put("mlo", 0, cols_of('ml_o'))
    ga, gb = cols_of('gate_a'), cols_of('gate_b')
    for g in range(8):
        put("gate", g, ga[:, g * 128:(g + 1) * 128])
        put("gate", 8 + g, gb[:, g * 128:(g + 1) * 128])
        put("upa", g, w_up_a[:, g * 128:(g + 1) * 128])
        put("upb", g, w_up_b[:, g * 128:(g + 1) * 128])
        put("wout", g, w_out[:, g * 128:(g + 1) * 128])
        put("ff2", g, w_ff2[:, g * 128:(g + 1) * 128])
        put("pleg", g, w_ple_gate[:, g * 128:(g + 1) * 128])
        put("plep", g, w_ple_proj[:, g * 128:(g + 1) * 128])
    for g in range(32):
        put("ff1", g, w_ff1[:, g * 128:(g + 1) * 128])
    return wall


def host_consts(conv_w, conv_b, b_igate, b_fgate, ml_norm_g, ln1_g, ln1_b, ln2_g, ln2_b):
    cst = np.zeros((128, CST), np.float32)
    cst[:, C_ID:C_ID + 128] = np.eye(128, dtype=np.float32)
    r = np.arange(128)
    cst[:, C_TRI:C_TRI + 128] = (r[:, None] <= r[None, :]).astype(np.float32)
    cst[:, C_CM:C_CM + 128] = np.where(r[None, :] <= r[:, None], 0.0, NEG)
    cst[:, C_ONE:C_ONE + 128] = 1.0
    cst[:, C_NEGI:C_NEGI + 128] = -BIG * np.eye(128, dtype=np.float32)
    vec = np.zeros((128, VEC), np.float32)
    vec[:, V_CW:V_CW + 32] = conv_w.reshape(4, 8, 128).transpose(2, 1, 0).reshape(128, 32)
    vec[:, V_CB:V_CB + 8] = conv_b.reshape(8, 128).T
    for o, v in ((V_L1G, ln1_g), (V_L1B, ln1_b), (V_L2G, ln2_g), (V_L2B, ln2_b)):
        vec[:, o:o + 8] = v.reshape(8, 128).T
    invf = 1.0 / (np.float32(10000.0) ** (np.arange(0, 64, 2, dtype=np.float32) / np.float32(64)))
    p = np.arange(128)
    vec[:, V_INVF] = invf.astype(np.float32)[p % 32]
    vec[:, V_SGN] = np.where((p % 64) < 32, -1.0, 1.0)
    vec[:, V_BI:V_BI + 4] = b_igate.reshape(1, 4)
    vec[:, V_BF:V_BF + 4] = b_fgate.reshape(1, 4)
    vec[:, V_MLG:V_MLG + 512] = ml_norm_g.reshape(1, 512)
    return cst, vec


class K:
    def __init__(self, nseq=NSEQ, stop_after=None, dbg=None):
        self.nseq = nseq
        self.stop_after = stop_after
        self.dbg = dbg or {}
        self.nc = nc = bass.Bass("TRN2", target_bir_lowering=False)
        self.ctx = ExitStack()

    def sb(self, name, shape, dt, ctx=None):
        self._sbn = getattr(self, "_sbn", 0) + 1
        return (ctx or self.ctx).enter_context(self.nc.sbuf_tensor("sb%d_%s" % (self._sbn, name), shape, dt))

    def dram(self, name, shape, dt=F32, kind="ExternalInput"):
        return self.nc.dram_tensor(name, shape, dt, kind=kind).ap()

    def V(self, fn, reads=(), writes=()):
        return self.S.op("dve", fn, reads, writes)

    def A(self, fn, reads=(), writes=()):
        return self.S.op("act", fn, reads, writes)

    def P(self, fn, reads=(), writes=()):
        return self.S.op("pool", fn, reads, writes)

    def MM(self, out, lhsT, rhs, start, stop, reads, writes, inc, skip=False):
        return self.S.op("pe", lambda e: e.matmul(out, lhsT=lhsT, rhs=rhs, start=start, stop=stop,
                                                  skip_group_check=skip), reads, writes, inc=inc)

    def TR(self, out, in_, ident, reads, writes, inc):
        return self.S.op("pe", lambda e: e.transpose(out, in_, ident), reads, writes, inc=inc)

    def barrier(self):
        S = self.S
        for eng in ("pe", "dve", "act", "pool"):
            assert not S.pending[eng], eng
        deps = {k: (S.sem[k], S.cnt[k]) for k in S.sem if S.cnt[k] > 0}
        for name, (sem, val) in S.dsem.items():
            if val > 0:
                deps["dma:" + name] = (sem, val)
        for eng in S.engs:
            d = {k: v for k, v in deps.items() if k != eng}
            S._wait(eng, d)

    def wstream_begin(self, order):
        self.w_order = list(order)
        self.w_issued = 0
        self.w_handles = {}
        self.w_occ = {}

    def _w_issue(self):
        i = self.w_issued
        fam, g = self.w_order[i]
        off, ng, kc, cols = WINFO[fam]
        n = kc * cols
        o = off + g * n
        if n <= 1024:
            k = self.ws_small_i % len(self.ws_small)
            self.ws_small_i += 1
            t, b = self.ws_small[k]
        else:
            k = self.ws_big_i % len(self.ws_big)
            self.ws_big_i += 1
            t, b = self.ws_big[k]
        rd = [self.B_wsc] if o + n <= self.wsplit else [self.B_wsc, self.B_wscB]
        self.S.dma("sp", b.name, t[:, 0:n], self.wsc[:, o:o + n], reads=rd, writes=[b])
        self.w_handles[i] = (t[:, 0:n].rearrange("p (a b) -> p a b", a=kc), b)
        self.w_issued += 1

    def _w_next_slot(self):
        fam, g = self.w_order[self.w_issued]
        off, ng, kc, cols = WINFO[fam]
        if kc * cols <= 1024:
            return ("s", self.ws_small_i % len(self.ws_small))
        return ("b", self.ws_big_i % len(self.ws_big))

    def wget(self, i, la=2):
        while self.w_issued < min(len(self.w_order), i + 1 + la):
            slot = self._w_next_slot()
            occ = self.w_occ.get(slot)
            if occ is not None and occ >= i and self.w_issued > i:
                break
            self.w_occ[slot] = self.w_issued
            self._w_issue()
        h = self.w_handles.pop(i)
        return h

    def setup(self):
        nc, nseq = self.nc, self.nseq
        self.S = S = Sched(nc, self.ctx)
        self.x_d = self.dram("x", [nseq, 2048, D])
        self.p_d = self.dram("p", [nseq, 2048, PLE])
        self.pos_d = self.dram("pos", [nseq, 2048], I32)
        self.wall_d = self.dram("wall", [128, WTOT])
        self.cst_d = self.dram("cst", [128, CST])
        self.vec_d = self.dram("vec", [128, VEC])
        self.out_d = self.dram("out", [nseq, 2048, D], kind="ExternalOutput")
        self.wsc = self.dram("wsc", [128, WTOT], BF16, kind="Internal")
        self.dbg_d = {}
        for name, shape in self.dbg.items():
            if name.startswith("_"):
                continue
            self.dbg_d[name] = self.dram("dbg_" + name, shape, kind="ExternalOutput")
        for n in ("ld", "wc", "wl", "xl", "pl", "st", "dbg"):
            S.new_dma_sem(n)
        self.cst = self.sb("cst", [128, CST], F32)
        self.vec = self.sb("vec", [128, VEC], F32)
        self.B_cst = Buf("cst")
        self.B_wsc = Buf("wsc")
        S.dma("sp", "ld", self.cst[:], self.cst_d[:], writes=[self.B_cst])
        S.dma("sp", "ld", self.vec[:], self.vec_d[:], writes=[self.B_cst])
        CH = 8192
        self.B_wscB = Buf("wscB")
        S.new_dma_sem("wc2")
        self.wsplit = ((WINFO["gate"][0] + CH - 1) // CH) * CH
        for o in range(0, self.wsplit, CH):
            n = min(CH, WTOT - o)
            S.dma("pool", "wc", self.wsc[:, o:o + n], self.wall_d[:, o:o + n], writes=[self.B_wsc])
        self.ident = self.cst[:, C_ID:C_ID + 128]
        self.triT = self.cst[:, C_TRI:C_TRI + 128]
        self.cmask = self.cst[:, C_CM:C_CM + 128]
        self.ones = self.cst[:, C_ONE:C_ONE + 128]
        self.identb = self.sb("identb", [128, 128], BF16)
        self.negI4 = self.sb("negI4", [128, 512], BF16)
        self.zerob = self.sb("zerob", [128, 512], BF16)
        self.B_k = Buf("kconst")
        self.V(lambda e: e.tensor_copy(out=self.identb[:], in_=self.ident), reads=[self.B_cst], writes=[self.B_k])
        for r in range(4):
            self.V(lambda e, r=r: e.tensor_copy(out=self.negI4[:, r * 128:(r + 1) * 128], in_=self.cst[:, C_NEGI:C_NEGI + 128]),
                   reads=[self.B_cst], writes=[self.B_k])
        self.V(lambda e: e.memset(self.zerob[:], 0.0), writes=[self.B_k])
        self.ps = [self.ctx.enter_context(nc.psum_tensor("ps%d" % i, [128, 512], F32)) for i in range(8)]
        self.PB = [Buf("ps%d" % i) for i in range(8)]
        self.ws_small = [(self.sb("wss%d" % i, [128, 1024], BF16), Buf("wss%d" % i)) for i in range(4)]
        self.ws_big = [(self.sb("wsb%d" % i, [128, 4096], BF16), Buf("wsb%d" % i)) for i in range(2)]
        self.ws_small_i = 0
        self.ws_big_i = 0
        for t_, b_ in self.ws_small + self.ws_big:
            S.new_dma_sem(b_.name)
        self.xs = [(self.sb("xs%d" % i, [128, D], F32), Buf("xs%d" % i)) for i in range(2)]
        self.xs_i = 0
        for t_, b_ in self.xs:
            S.new_dma_sem(b_.name)

    def cast_rest(self, after_bufs):
        S = self.S
        S.wait_all("pool", after_bufs)
        CH = 8192
        for o in range(self.wsplit, WTOT, CH):
            n = min(CH, WTOT - o)
            S.dma("pool", "wc2", self.wsc[:, o:o + n], self.wall_d[:, o:o + n], writes=[self.B_wscB])

    def dump(self, name, ap, bufs):
        if name in self.dbg_d:
            self.S.dma("pool", "dbg", self.dbg_d[name], ap, reads=bufs)

    def finish(self):
        S = self.S
        for name in S.dsem:
            sem, val = S.dsem[name]
            if val > 0:
                self.nc.gpsimd.wait_ge(sem, val)
        self.ctx.close()
        return self.nc


def _phaseA(self, s, xT, B_xT):
    S = self.S
    for t in range(NT):
        xs, bx = self.xs[self.xs_i % 2]
        self.xs_i += 1
        S.dma("sp", bx.name, xs[:], self.x_d[s, t * 128:(t + 1) * 128, :], writes=[bx])
        pair = (t % 4) * 2
        for c in range(8):
            bk = pair + c // 4
            self.TR(self.ps[bk][:, (c % 4) * 128:(c % 4 + 1) * 128], xs[:, c * 128:(c + 1) * 128], self.ident,
                    reads=[bx, self.B_cst], writes=[self.PB[bk]], inc=(c % 4 == 3))
        self.V(lambda e: e.tensor_copy(out=xT[:, 0:4, t * 128:(t + 1) * 128],
                                       in_=self.ps[pair][:].rearrange("p (a b) -> p a b", a=4)),
               reads=[self.PB[pair]], writes=[B_xT])
        self.A(lambda e: e.copy(out=xT[:, 4:8, t * 128:(t + 1) * 128],
                                in_=self.ps[pair + 1][:].rearrange("p (a b) -> p a b", a=4)),
               reads=[self.PB[pair + 1]], writes=[B_xT])


K.phaseA = _phaseA


def _rope_tables(self, s, cosT, sinT, posi, ang, tmp, Bt):
    S = self.S
    MAGIC = 12582912.0
    C1 = 6.28125
    C2 = 2 * PI - C1
    vec = self.vec
    S.dma("sp", "ld", posi[:], self.pos_d[s:s + 1, :].partition_broadcast(128), writes=[Bt])
    self.V(lambda e: e.tensor_copy(out=ang[:], in_=posi[:]), reads=[Bt], writes=[Bt])
    self.V(lambda e: e.tensor_scalar(out=ang[:], in0=ang[:], scalar1=vec[:, V_INVF:V_INVF + 1], scalar2=None, op0=ALU.mult),
           reads=[Bt, self.B_cst], writes=[Bt])

    def red(dst):
        self.V(lambda e: e.tensor_scalar(out=dst[:], in0=ang[:], scalar1=1.0 / (2 * PI), scalar2=MAGIC, op0=ALU.mult, op1=ALU.add),
               reads=[Bt], writes=[Bt])
        self.V(lambda e: e.tensor_scalar(out=dst[:], in0=dst[:], scalar1=-MAGIC, scalar2=None, op0=ALU.add), reads=[Bt], writes=[Bt])
        self.V(lambda e: e.scalar_tensor_tensor(out=tmp[:], in0=dst[:], scalar=-C1, in1=ang[:], op0=ALU.mult, op1=ALU.add),
               reads=[Bt], writes=[Bt])
        self.V(lambda e: e.scalar_tensor_tensor(out=dst[:], in0=dst[:], scalar=-C2, in1=tmp[:], op0=ALU.mult, op1=ALU.add),
               reads=[Bt], writes=[Bt])
        self.V(lambda e: e.tensor_scalar(out=dst[:], in0=dst[:], scalar1=-3.1415925, scalar2=3.1415925, op0=ALU.max, op1=ALU.min),
               reads=[Bt], writes=[Bt])

    red(sinT)
    self.A(lambda e: e.activation(out=sinT[:], in_=sinT[:], func=AF.Sin), reads=[Bt], writes=[Bt])
    self.V(lambda e: e.tensor_scalar(out=sinT[:], in0=sinT[:], scalar1=vec[:, V_SGN:V_SGN + 1], scalar2=None, op0=ALU.mult),
           reads=[Bt, self.B_cst], writes=[Bt])
    self.V(lambda e: e.tensor_scalar(out=ang[:], in0=ang[:], scalar1=PI / 2, scalar2=None, op0=ALU.add), reads=[Bt], writes=[Bt])
    red(cosT)
    self.A(lambda e: e.activation(out=cosT[:], in_=cosT[:], func=AF.Sin), reads=[Bt], writes=[Bt])


K.rope_tables = _rope_tables


def _proj_fm_banks(self, wh, xT, B_xT, tok0, banks):
    wt, bw = wh
    for j, bk in enumerate(banks):
        for kc in range(8):
            self.MM(self.ps[bk][:, :], lhsT=wt[:, kc, :], rhs=xT[:, kc, tok0 + j * 512: tok0 + (j + 1) * 512],
                    start=(kc == 0), stop=(kc == 7), reads=[bw, B_xT], writes=[self.PB[bk]], inc=(kc == 7))


K.proj_fm_banks = _proj_fm_banks


def _dsa_inproj(self, s, xT, B_xT, T):
    S = self.S
    cosT, sinT, Bt = T["cosT"], T["sinT"], T["Btab"]
    units = []
    for g in range(4):
        units.append((G_QA + g, G_QAP + g, lambda a, b, g=g: [(slice(0, 128), T["qaT"][:, g, a:b])]))
    units.append((G_KA, G_KAP, lambda a, b: [(slice(0, 64), T["kaT"][0:64, 0, a:b]), (slice(64, 128), T["kaT"][64:128, 1, a:b])]))
    for g in range(4):
        units.append((G_QI + g, G_QIP + g, lambda a, b, g=g: [(slice(0, 128), T["qiT"][:, g, a:b])]))
    units.append((G_KI, G_KIP, lambda a, b: [(slice(0, 64), T["kiT"][0:64, 0, a:b]), (slice(64, 128), T["kiT"][64:128, 1, a:b])]))
    order = []
    for (gm, gp, _) in units:
        order += [("fm", gm), ("fm", gp)]
    order.append(("tms", 0))
    self.wstream_begin(order)
    B_dst = T["B_qk"]
    self.P(lambda e: e.memset(T["kaT"][:], 0.0), writes=[B_dst])
    self.P(lambda e: e.memset(T["kiT"][:], 0.0), writes=[B_dst])
    ucount = 0
    for ui, (gm, gp, dst) in enumerate(units):
        whm = self.wget(2 * ui)
        whp = self.wget(2 * ui + 1)
        for half in range(2):
            bset = (ucount % 2) * 4
            ucount += 1
            tok0 = half * 1024
            self.proj_fm_banks(whm, xT, B_xT, tok0, [bset, bset + 1])
            self.proj_fm_banks(whp, xT, B_xT, tok0, [bset + 2, bset + 3])
            for j in range(2):
                a = tok0 + j * 512
                k = (half * 2 + j) % 2
                t1, b1 = T["rt1"][k]
                t2, b2 = T["rt2"][k]
                self.V(lambda e, j=j, a=a, t1=t1: e.tensor_tensor(out=t1[:], in0=self.ps[bset + j][:], in1=cosT[:, a:a + 512], op=ALU.mult),
                       reads=[self.PB[bset + j], Bt], writes=[b1])
                self.V(lambda e, j=j, a=a, t2=t2: e.tensor_tensor(out=t2[:], in0=self.ps[bset + 2 + j][:], in1=sinT[:, a:a + 512], op=ALU.mult),
                       reads=[self.PB[bset + 2 + j], Bt], writes=[b2])
                for psl, dap in dst(a, a + 512):
                    self.P(lambda e, t1=t1, t2=t2, psl=psl, dap=dap: e.tensor_tensor(out=dap, in0=t1[psl, :], in1=t2[psl, :], op=ALU.add),
                           reads=[b1, b2], writes=[B_dst])
    wt, bw = self.wget(len(order) - 1)
    va, wif, B_tm = T["va"], T["wif"], T["B_tm"]
    self.V(lambda e: e.memset(va[:, :, 64:128], 1.0), writes=[B_tm])
    for t in range(NT):
        bk = t % 4
        for kc in range(8):
            self.MM(self.ps[bk][:, 0:80], lhsT=xT[:, kc, t * 128:(t + 1) * 128], rhs=wt[:, kc, :],
                    start=(kc == 0), stop=(kc == 7), reads=[bw, B_xT], writes=[self.PB[bk]], inc=(kc == 7))
        self.A(lambda e, t=t, bk=bk: e.copy(out=va[:, t, 0:64], in_=self.ps[bk][:, 0:64]), reads=[self.PB[bk]], writes=[B_tm])
        self.V(lambda e, t=t, bk=bk: e.tensor_copy(out=wif[:, t, :], in_=self.ps[bk][:, 64:80]), reads=[self.PB[bk]], writes=[B_tm])


K.dsa_inproj = _dsa_inproj


def _dsa_score_units(self, j, T):
    n = 128 * (j + 1)
    sc, bscs = T["sc"][j % 2]
    wif = T["wif"]
    B_qk, B_tm = T["B_qk"], T["B_tm"]
    qiT, kiT = T["qiT"], T["kiT"]
    units = []
    nchunk = (n + 511) // 512
    for c in range(nchunk):
        w = min(512, n - c * 512)
        for h in range(8):
            def unit(c=c, w=w, h=h, last=(c == nchunk - 1 and h == 7)):
                bsc = bscs[c]
                cnt = T["scnt"][0]
                T["scnt"][0] += 1
                bk = cnt % 2
                self.MM(self.ps[bk][:, 0:w], lhsT=qiT[:, h // 2, j * 128:(j + 1) * 128], rhs=kiT[:, h % 2, c * 512:c * 512 + w],
                        start=True, stop=True, reads=[B_qk], writes=[self.PB[bk]], inc=True)
                dst = sc[:, c * 512:c * 512 + w]
                rl, brl = T["rl"][cnt % 4]
                if h == 0:
                    self.V(lambda e: e.tensor_scalar(out=dst, in0=self.ps[bk][:, 0:w], scalar1=0.0, scalar2=wif[:, j, 0:1],
                                                     op0=ALU.max, op1=ALU.mult), reads=[self.PB[bk], B_tm], writes=[bsc])
                elif h <= 3:
                    self.A(lambda e: e.activation(out=rl[:, 0:w], in_=self.ps[bk][:, 0:w], func=AF.Relu), reads=[self.PB[bk]], writes=[brl])
                    self.V(lambda e: e.scalar_tensor_tensor(out=dst, in0=rl[:, 0:w], scalar=wif[:, j, h:h + 1], in1=dst,
                                                            op0=ALU.mult, op1=ALU.add), reads=[brl, B_tm, bsc], writes=[bsc])
                else:
                    self.V(lambda e: e.tensor_scalar(out=rl[:, 0:w], in0=self.ps[bk][:, 0:w], scalar1=0.0, scalar2=wif[:, j, h:h + 1],
                                                     op0=ALU.max, op1=ALU.mult), reads=[self.PB[bk], B_tm], writes=[brl])
                    self.P(lambda e: e.tensor_tensor(out=dst, in0=dst, in1=rl[:, 0:w], op=ALU.add), reads=[brl, bsc], writes=[bsc])
                if last:
                    bd = bscs[(j * 128) // 512]
                    self.V(lambda e: e.tensor_tensor(out=sc[:, j * 128:(j + 1) * 128], in0=sc[:, j * 128:(j + 1) * 128], in1=self.cmask,
                                                     op=ALU.add), reads=[bd, self.B_cst], writes=[bd])
            units.append(unit)
    return units


def _dsa_score(self, j, T):
    for u in self.dsa_score_units(j, T):
        u()


K.dsa_score_units = _dsa_score_units
K.dsa_score = _dsa_score


def _dsa_select(self, j, T):
    n = 128 * (j + 1)
    sc, bscs = T["sc"][j % 2]
    nm, bnm = T["nm"][j % 4]
    sm, bsm = T["sm"][j % 2]
    junk = T["junk"]
    mid, cnt, a, thr = sm[:, 0:1], sm[:, 1:2], sm[:, 2:3], sm[:, 3:4]
    if n <= TOPK:
        self.V(lambda e: e.memset(thr, -1.0e29), writes=[bsm])
    else:
        W = BIS_HI - BIS_LO
        self.V(lambda e: e.memset(mid, BIS_LO + W / 2), writes=[bsm])
        for i in range(NBIS):
            self.V(lambda e: e.tensor_scalar(out=junk[:, 0:n], in0=sc[:, 0:n], scalar1=mid, scalar2=None, op0=ALU.is_ge, op1=ALU.add,
                                             accum_out=cnt), reads=bscs + [bsm], writes=[bsm, T["B_junk"]])
            if i < NBIS - 1:
                st = W / (2 ** (i + 2))
                self.V(lambda e, st=st: e.tensor_scalar(out=a, in0=cnt, scalar1=TOPK - 0.5, scalar2=2 * st, op0=ALU.is_ge, op1=ALU.mult),
                       reads=[bsm], writes=[bsm])
                self.V(lambda e, st=st: e.scalar_tensor_tensor(out=mid, in0=a, scalar=-st, in1=mid, op0=ALU.add, op1=ALU.add),
                       reads=[bsm], writes=[bsm])
            else:
                st = W / (2 ** (i + 1))
                self.V(lambda e, st=st: e.tensor_scalar(out=a, in0=cnt, scalar1=TOPK - 0.5, scalar2=st, op0=ALU.is_ge, op1=ALU.mult),
                       reads=[bsm], writes=[bsm])
                self.V(lambda e, st=st: e.scalar_tensor_tensor(out=thr, in0=a, scalar=-st, in1=mid, op0=ALU.add, op1=ALU.add),
                       reads=[bsm], writes=[bsm])
    self.V(lambda e: e.tensor_scalar(out=nm[:, 0:n], in0=sc[:, 0:n], scalar1=thr, scalar2=None, op0=ALU.is_lt),
           reads=bscs + [bsm], writes=[bnm])


K.dsa_select = _dsa_select


def _dsa_select_act(self, j, T):
    n = 128 * (j + 1)
    sc, bscs = T["sc"][j % 2]
    nm, bnm = T["nm"][j % 4]
    sm, bsm = T["sm"][j % 2]
    junk = T["junk2"]
    nmid, cnt, a, thr = sm[:, 0:1], sm[:, 1:2], sm[:, 2:3], sm[:, 3:4]
    cb = T["cb"][:, j:j + 1]
    W = BIS_HI - BIS_LO
    self.V(lambda e: e.memset(nmid, -(BIS_LO + W / 2)), writes=[bsm])
    for i in range(NBIS):
        self.A(lambda e: e.activation(out=junk[:, 0:n], in_=sc[:, 0:n], func=AF.Sign, bias=nmid, scale=1.0, accum_out=cnt),
               reads=bscs + [bsm], writes=[bsm, T["B_junk2"]])
        self.A(lambda e: e.activation(out=a, in_=cnt, func=AF.Sign, bias=cb, scale=1.0), reads=[bsm, T["B_cb"]], writes=[bsm])
        if i < NBIS - 1:
            st = W / (2 ** (i + 2))
            self.A(lambda e: e.activation(out=nmid, in_=a, func=AF.Identity, scale=-st, bias=nmid), reads=[bsm], writes=[bsm])


K.dsa_select_act = _dsa_select_act


def _dsa_select_act_fin(self, j, T):
    n = 128 * (j + 1)
    sc, bscs = T["sc"][j % 2]
    nm, bnm = T["nm"][j % 4]
    sm, bsm = T["sm"][j % 2]
    nmid, cnt, a, thr = sm[:, 0:1], sm[:, 1:2], sm[:, 2:3], sm[:, 3:4]
    W = BIS_HI - BIS_LO
    st = W / (2 ** NBIS)
    self.V(lambda e: e.scalar_tensor_tensor(out=thr, in0=a, scalar=st / 2, in1=nmid, op0=ALU.mult, op1=ALU.subtract),
           reads=[bsm], writes=[bsm])
    self.V(lambda e: e.tensor_scalar(out=thr, in0=thr, scalar1=-st / 2, scalar2=None, op0=ALU.add), reads=[bsm], writes=[bsm])
    self.V(lambda e: e.tensor_scalar(out=nm[:, 0:n], in0=sc[:, 0:n], scalar1=thr, scalar2=None, op0=ALU.is_lt),
           reads=bscs + [bsm], writes=[bnm])


K.dsa_select_act_fin = _dsa_select_act_fin


def _dsa_attn_units(self, j, T):
    nm, bnm = T["nm"][j % 4]
    B_qk, B_tm = T["B_qk"], T["B_tm"]
    qaT, kaT, va = T["qaT"], T["kaT"], T["va"]
    ob = [4, 5] if j % 2 == 0 else [6, 7]
    import os
    var = os.environ.get("DSA_VAR", "")
    def zero_mm():
        for half in range(2):
            self.MM(self.ps[ob[half]][:, 0:264], lhsT=self.zerob[:, 0:128], rhs=self.zerob[:, 0:264], start=True, stop=False,
                    reads=[self.B_k], writes=[self.PB[ob[half]]], inc=False, skip=True)
    pc = T["pcount"]
    steps = [(kt, half) for kt in range(j + 1) for half in range(2)]
    slots = []
    for _ in steps:
        slots.append(pc[0])
        pc[0] += 1

    def emitS(i):
        kt, half = steps[i]
        bk = 2 + (slots[i] % 2)
        pT, bpT = T["pT"][slots[i] % 4]
        self.MM(self.ps[bk][:, :], lhsT=nm[:, kt * 128:(kt + 1) * 128], rhs=self.negI4[:, :], start=True, stop=False,
                reads=[bnm, self.B_k], writes=[self.PB[bk]], inc=False, skip=True)
        for hh in range(4):
            h = half * 4 + hh
            self.MM(self.ps[bk][:, hh * 128:(hh + 1) * 128], lhsT=kaT[:, h % 2, kt * 128:(kt + 1) * 128],
                    rhs=qaT[:, h // 2, j * 128:(j + 1) * 128], start=False, stop=True,
                    reads=[B_qk], writes=[self.PB[bk]], inc=(hh == 3), skip=True)
        self.A(lambda e: e.activation(out=pT[:], in_=self.ps[bk][:], func=AF.Exp, scale=0.125),
               reads=[self.PB[bk]], writes=[bpT])

    def emitPV(i):
        kt, half = steps[i]
        pT, bpT = T["pT"][slots[i] % 4]
        for hh in range(4):
            self.MM(self.ps[ob[half]][:, hh * 66:(hh + 1) * 66], lhsT=pT[:, hh * 128:(hh + 1) * 128], rhs=va[:, kt, 0:66],
                    start=False, stop=(kt == j), reads=[bpT, B_tm], writes=[self.PB[ob[half]]], inc=(hh == 3), skip=True)

    units = []

    def first():
        zero_mm()
        emitS(0)
    units.append(first)
    for i in range(len(steps)):
        def unit(i=i):
            if i + 1 < len(steps):
                emitS(i + 1)
            emitPV(i)
        units.append(unit)
    return units


K.dsa_attn_units = _dsa_attn_units


def _dsa_attn(self, j, T):
    for u in self.dsa_attn_units(j, T):
        u()


K.dsa_attn = _dsa_attn


def interleave(la, lb):
    out = []
    na, nb = len(la), len(lb)
    ia = ib = 0
    while ia < na or ib < nb:
        if ib >= nb or (ia < na and ia * nb <= ib * na):
            out.append(la[ia])
            ia += 1
        else:
            out.append(lb[ib])
            ib += 1
    return out


def _dsa_norm(self, j, T, yaT, B_ya):
    ob = [4, 5] if j % 2 == 0 else [6, 7]
    ya, bya = T["ya"][j % 2]
    rd, brd = T["rden"][j % 2]
    for half in range(2):
        o3 = self.ps[ob[half]][:, 0:264].rearrange("p (h e) -> p h e", e=66)
        self.V(lambda e, o3=o3, half=half: e.reciprocal(out=rd[:, half * 4:(half + 1) * 4], in_=o3[:, :, 64]),
               reads=[self.PB[ob[half]]], writes=[brd])
        self.V(lambda e, o3=o3, half=half: e.tensor_tensor(out=ya[:, half * 256:(half + 1) * 256].rearrange("p (h e) -> p h e", e=64),
                                                          in0=o3[:, :, 0:64],
                                                          in1=rd[:, half * 4:(half + 1) * 4].unsqueeze(2).to_broadcast([128, 4, 64]),
                                                          op=ALU.mult),
               reads=[self.PB[ob[half]], brd], writes=[bya])
    tb = self.ps[0][:].bitcast(BF16)
    for g in range(4):
        self.TR(tb[:, g * 128:(g + 1) * 128], ya[:, g * 128:(g + 1) * 128], self.identb[:], reads=[bya, self.B_k],
                writes=[self.PB[0]], inc=(g == 3))
    self.A(lambda e: e.copy(out=yaT[:, :, j * 128:(j + 1) * 128], in_=tb[:, 0:512].rearrange("p (g t) -> p g t", g=4)),
           reads=[self.PB[0]], writes=[B_ya])


K.dsa_norm = _dsa_norm


def _dsa_phase(self, s, xT, B_xT, yaT, B_ya, wif_keep, B_wifk):
    with ExitStack() as c1:
        T = {}
        T["qaT"] = self.sb("qaT", [128, 4, 2048], BF16, c1)
        T["kaT"] = self.sb("kaT", [128, 2, 2048], BF16, c1)
        T["qiT"] = self.sb("qiT", [128, 4, 2048], BF16, c1)
        T["kiT"] = self.sb("kiT", [128, 2, 2048], BF16, c1)
        T["va"] = self.sb("va", [128, NT, 128], BF16, c1)
        T["wif"] = self.sb("wif", [128, NT, 16], F32, c1)
        T["B_qk"], T["B_tm"] = Buf("qk"), Buf("tm")
        with ExitStack() as c2:
            T["cosT"] = self.sb("cosT", [128, 2048], F32, c2)
            T["sinT"] = self.sb("sinT", [128, 2048], F32, c2)
            posi = self.sb("posi", [128, 2048], I32, c2)
            ang = self.sb("ang", [128, 2048], F32, c2)
            tmp = self.sb("rtmp", [128, 2048], F32, c2)
            T["Btab"] = Buf("tab")
            self.rope_tables(s, T["cosT"], T["sinT"], posi, ang, tmp, T["Btab"])
            self.barrier()
            T["rt1"] = [(ang[:, k * 512:(k + 1) * 512], Buf("rt1")) for k in range(2)]
            T["rt2"] = [(tmp[:, k * 512:(k + 1) * 512], Buf("rt2")) for k in range(2)]
            self.dsa_inproj(s, xT, B_xT, T)
            self.V(lambda e: e.tensor_copy(out=wif_keep[:], in_=T["wif"][:, :, 8:16]), reads=[T["B_tm"]], writes=[B_wifk])
            self.dump("qaT", T["qaT"][:, 0, :], [T["B_qk"]])
            self.dump("kaT", T["kaT"][:, 0, :], [T["B_qk"]])
            self.barrier()
        if self.stop_after == "dsa_inproj":
            return
        with ExitStack() as c3:
            T["sc"] = [(self.sb("sc%d" % i, [128, 2048], F32, c3), [Buf("sc%d_%d" % (i, c)) for c in range(4)]) for i in range(2)]
            T["nm"] = [(self.sb("nm%d" % i, [128, 2048], BF16, c3), Buf("nm")) for i in range(4)]
            rlf = self.ws_big[0][0][:].bitcast(F32)
            T["rl"] = [(rlf[:, i * 512:(i + 1) * 512], Buf("rl")) for i in range(4)]
            T["pT"] = [(self.sb("pT%d" % i, [128, 512], BF16, c3), Buf("pT")) for i in range(4)]
            T["ya"] = [(self.sb("ya%d" % i, [128, 512], BF16, c3), Buf("ya")) for i in range(2)]
            T["sm"] = [(self.sb("sm%d" % i, [128, 4], F32, c3), Buf("sm")) for i in range(2)]
            T["rden"] = [(self.sb("rden%d" % i, [128, 8], F32, c3), Buf("rden")) for i in range(2)]
            T["junk"] = self.sb("junk", [128, 2048], BF16, c3)
            T["B_junk"] = Buf("junk")
            T["junk2"] = T["junk"]
            T["B_junk2"] = Buf("junk2")
            T["cb"] = self.sb("cbias", [128, NT], F32, c3)
            T["B_cb"] = Buf("cb")
            for jj in range(NT):
                self.V(lambda e, jj=jj: e.memset(T["cb"][:, jj:jj + 1], float(128 * (jj + 1) - 511)), writes=[T["B_cb"]])
            T["pcount"] = [0]
            T["scnt"] = [0]
            ntile = self.dbg.get("_ntile", NT)
            def sel_pair(k2):
                j1, j2 = 2 * k2, 2 * k2 + 1
                if 128 * (j2 + 1) <= TOPK:
                    self.dsa_select(j1, T)
                    self.dsa_select(j2, T)
                else:
                    self.dsa_select_act(j2, T)
                    self.dsa_select(j1, T)
                    self.dsa_select_act_fin(j2, T)

            npair = ntile // 2
            self.dsa_score(0, T)
            self.dsa_score(1, T)
            sel_pair(0)
            for k2 in range(npair):
                j1, j2 = 2 * k2, 2 * k2 + 1
                ua = self.dsa_attn_units(j1, T) + self.dsa_attn_units(j2, T)
                us = (self.dsa_score_units(j1 + 2, T) + self.dsa_score_units(j2 + 2, T)) if k2 + 1 < npair else []
                for u in interleave(ua, us):
                    u()
                if k2 + 1 < npair:
                    sel_pair(k2 + 1)
                self.dsa_norm(j1, T, yaT, B_ya)
                self.dsa_norm(j2, T, yaT, B_ya)
            self.dump("sc", T["sc"][(ntile - 1) % 2][0][:, :], T["sc"][(ntile - 1) % 2][1])
            self.dump("sm", T["sm"][(ntile - 1) % 2][0][:, :], [T["sm"][(ntile - 1) % 2][1]])
            self.dump("yaT", yaT[:, 0, :], [B_ya])
            self.barrier()


K.dsa_phase = _dsa_phase


def _ml_phase(self, s, xT, B_xT, ybT, B_yb, wifk, B_wifk):
    S = self.S
    vec = self.vec
    with ExitStack() as c1:
        mlqT = self.sb("mlqT", [128, 4, 2048], BF16, c1)
        mlkT = self.sb("mlkT", [128, 4, 2048], BF16, c1)
        ktm = [(self.sb("ktm%d" % i, [128, 128], BF16, c1), Buf("ktm")) for i in range(2)]
        mlv = self.sb("mlv", [128, NT, 4, 130], BF16, c1)
        osig = self.sb("osig", [128, NT, 512], BF16, c1)
        G = self.sb("mlG", [128, NT, 12], F32, c1)
        xc = self.sb("xc", [128, 2052], F32, c1)
        acc = self.sb("cacc", [128, 2048], F32, c1)
        C32 = self.sb("C32", [128, 4, 130], F32, c1)
        Cbf = self.sb("Cbf", [128, 4, 130], BF16, c1)
        hm = self.sb("hm", [128, 512], F32, c1)
        yb = self.sb("ybtm", [128, 512], BF16, c1)
        pTm = [(self.sb("pTm%d" % i, [128, 128], BF16, c1), Buf("pTm")) for i in range(2)]
        sm = self.sb("mlsm", [128, 4, 8], F32, c1)
        g4 = self.sb("mlg4", [128, 8], F32, c1)
        junk = self.sb("mljunk", [128, 128], F32, c1)
        junk2 = self.sb("mljunk2", [128, 128], F32, c1)
        B_q, B_k, B_ktm, B_v, B_o, B_G = Buf("mlq"), Buf("mlk"), Buf("mlktm"), Buf("mlv"), Buf("osig"), Buf("G")
        B_xc, B_acc, B_C32, B_Cbf, B_hm, B_yb_tm, B_sm, B_g4, B_j, B_j2 = (Buf("xc"), Buf("acc"), Buf("C32"), Buf("Cbf"), Buf("hm"),
                                                                         Buf("ybtm"), Buf("mlsm"), Buf("g4"), Buf("j"), Buf("j2"))
        order = [("fm", G_MLQ + g) for g in range(4)] + [("fm", G_MLK + g) for g in range(4)] + [("mlv", 0), ("mlo", 0)]
        self.wstream_begin(order)
        self.V(lambda e: e.memset(xc[:, 0:4], 0.0), writes=[B_xc])
        self.P(lambda e: e.memset(mlv[:], 0.0), writes=[B_v])
        ucount = 0
        for gi in range(8):
            wh = self.wget(gi)
            for half in range(2):
                bset = (ucount % 4) * 2
                ucount += 1
                self.proj_fm_banks(wh, xT, B_xT, half * 1024, [bset, bset + 1])
                for j in range(2):
                    a = 4 + half * 1024 + j * 512
                    self.A(lambda e, a=a, bk=bset + j: e.copy(out=xc[:, a:a + 512], in_=self.ps[bk][:]), reads=[self.PB[bset + j]], writes=[B_xc])
            cw = lambda jj: vec[:, V_CW + gi * 4 + jj:V_CW + gi * 4 + jj + 1]
            self.V(lambda e: e.tensor_scalar(out=acc[:], in0=xc[:, 4:2052], scalar1=cw(3), scalar2=vec[:, V_CB + gi:V_CB + gi + 1],
                                             op0=ALU.mult, op1=ALU.add), reads=[B_xc, self.B_cst], writes=[B_acc])
            self.V(lambda e: e.scalar_tensor_tensor(out=acc[:], in0=xc[:, 3:2051], scalar=cw(2), in1=acc[:], op0=ALU.mult, op1=ALU.add),
                   reads=[B_xc, B_acc, self.B_cst], writes=[B_acc])
            self.V(lambda e: e.scalar_tensor_tensor(out=acc[:], in0=xc[:, 2:2050], scalar=cw(1), in1=acc[:], op0=ALU.mult, op1=ALU.add),
                   reads=[B_xc, B_acc, self.B_cst], writes=[B_acc])
            self.V(lambda e: e.scalar_tensor_tensor(out=acc[:], in0=xc[:, 1:2049], scalar=cw(0), in1=acc[:], op0=ALU.mult, op1=ALU.add),
                   reads=[B_xc, B_acc, self.B_cst], writes=[B_acc])
            dst, bd = (mlqT[:, gi, :], B_q) if gi < 4 else (mlkT[:, gi - 4, :], B_k)
            self.A(lambda e, dst=dst: e.activation(out=dst, in_=acc[:], func=AF.Silu), reads=[B_acc], writes=[bd])
        wv, bwv = self.wget(8)
        wo, bwo = self.wget(9)
        LNS = math.log(128.0 ** -0.5)
        ga = self.sb("mlga", [128, NT, 4], F32, c1)
        gb_ = self.sb("mlgb", [128, NT, 4], F32, c1)
        B_ga, B_gb = Buf("ga"), Buf("gb")
        bfb = vec[:, V_BF:V_BF + 4].unsqueeze(1).to_broadcast([128, NT, 4])
        bib = vec[:, V_BI:V_BI + 4].unsqueeze(1).to_broadcast([128, NT, 4])
        self.V(lambda e: e.tensor_tensor(out=ga[:], in0=wifk[:, :, 4:8], in1=bfb, op=ALU.add), reads=[B_wifk, self.B_cst], writes=[B_ga])
        self.A(lambda e: e.activation(out=ga[:], in_=ga[:], func=AF.Exp, scale=-1.0), reads=[B_ga], writes=[B_ga])
        self.V(lambda e: e.tensor_scalar(out=ga[:], in0=ga[:], scalar1=1.0, scalar2=None, op0=ALU.add), reads=[B_ga], writes=[B_ga])
        self.A(lambda e: e.activation(out=ga[:], in_=ga[:], func=AF.Ln), reads=[B_ga], writes=[B_ga])
        gaf = ga[:].rearrange("p t h -> p (t h)")
        self.MM(self.ps[2][:, 0:64], lhsT=self.triT, rhs=gaf, start=True, stop=True, reads=[B_ga, self.B_cst], writes=[self.PB[2]], inc=True)
        self.MM(self.ps[3][:, 0:64], lhsT=self.ones, rhs=gaf, start=True, stop=True, reads=[B_ga, self.B_cst], writes=[self.PB[3]], inc=True)
        cum3 = self.ps[2][:, 0:64].rearrange("p (t h) -> p t h", h=4)
        tot3 = self.ps[3][:, 0:64].rearrange("p (t h) -> p t h", h=4)
        self.V(lambda e: e.tensor_tensor(out=gb_[:], in0=wifk[:, :, 0:4], in1=bib, op=ALU.add), reads=[B_wifk, self.B_cst], writes=[B_gb])
        self.V(lambda e: e.scalar_tensor_tensor(out=gb_[:], in0=gb_[:], scalar=LNS, in1=cum3, op0=ALU.add, op1=ALU.add),
               reads=[B_gb, self.PB[2]], writes=[B_gb])
        self.A(lambda e: e.activation(out=G[:, :, 0:4], in_=gb_[:], func=AF.Exp), reads=[B_gb], writes=[B_G])
        self.A(lambda e: e.activation(out=G[:, :, 4:8], in_=cum3, func=AF.Exp, scale=-1.0), reads=[self.PB[2]], writes=[B_G])
        self.A(lambda e: e.activation(out=G[:, :, 8:12], in_=tot3, func=AF.Exp, scale=-1.0), reads=[self.PB[3]], writes=[B_G])
        for t in range(NT):
            vb = 4 + (t % 2)
            for kc in range(8):
                self.MM(self.ps[vb][:, :], lhsT=xT[:, kc, t * 128:(t + 1) * 128], rhs=wv[:, kc, :], start=(kc == 0), stop=(kc == 7),
                        reads=[bwv, B_xT], writes=[self.PB[vb]], inc=(kc == 7))
            self.V(lambda e, t=t, vb=vb: e.tensor_tensor(out=mlv[:, t, :, 0:128], in0=self.ps[vb][:].rearrange("p (h d) -> p h d", h=4),
                                                       in1=G[:, t, 0:4].unsqueeze(2).to_broadcast([128, 4, 128]), op=ALU.mult),
                   reads=[self.PB[vb], B_G], writes=[B_v])
            self.V(lambda e, t=t: e.tensor_copy(out=mlv[:, t, :, 128], in_=G[:, t, 0:4]), reads=[B_G], writes=[B_v])
            ob = 6 + (t % 2)
            for kc in range(8):
                self.MM(self.ps[ob][:, :], lhsT=xT[:, kc, t * 128:(t + 1) * 128], rhs=wo[:, kc, :], start=(kc == 0), stop=(kc == 7),
                        reads=[bwo, B_xT], writes=[self.PB[ob]], inc=(kc == 7))
            self.A(lambda e, t=t, ob=ob: e.activation(out=osig[:, t, :], in_=self.ps[ob][:], func=AF.Sigmoid), reads=[self.PB[ob]], writes=[B_o])
        pT4 = [(self.sb("pT4_%d" % i, [128, 4, 128], BF16, c1), Buf("pT4")) for i in range(2)]
        kt4 = [(self.sb("kt4_%d" % i, [128, 4, 128], BF16, c1), Buf("kt4")) for i in range(2)]
        st8 = self.sb("mlst8", [128, 10, 4], F32, c1)
        B_st = Buf("st8")
        B_s1, B_s2 = Buf("s1"), Buf("s2")
        B_hmh = [Buf("hm%d" % h) for h in range(4)]
        sD, sND, sR, sF, sS1, sS2, sM, sV, sRS, sT = [st8[:, i, :] for i in range(10)]
        for t in range(NT):
            tl = slice(t * 128, (t + 1) * 128)
            par = t % 2
            pT, bpT = pT4[par]
            kt_, bkt = kt4[par]
            bS = par
            bT = 6 + par
            tbk = self.ps[bT][:].bitcast(BF16)
            last = (t == NT - 1)
            for h in range(4):
                self.MM(self.ps[bS][:, h * 128:(h + 1) * 128], lhsT=mlkT[:, h, tl], rhs=mlqT[:, h, tl], start=True, stop=True,
                        reads=[B_k, B_q], writes=[self.PB[bS]], inc=(h == 3))
            if not last:
                for h in range(4):
                    self.TR(tbk[:, h * 128:(h + 1) * 128], mlkT[:, h, tl], self.identb[:], reads=[B_k, self.B_k], writes=[self.PB[bT]], inc=(h == 3))
            self.V(lambda e: e.tensor_tensor(out=pT[:], in0=self.ps[bS][:].rearrange("p (h s) -> p h s", h=4),
                                             in1=self.triT.unsqueeze(1).to_broadcast([128, 4, 128]), op=ALU.mult),
                   reads=[self.PB[bS], self.B_cst], writes=[bpT])
            if not last:
                self.A(lambda e: e.copy(out=kt_[:], in_=tbk[:, 0:512].rearrange("p (h s) -> p h s", h=4)), reads=[self.PB[bT]], writes=[bkt])
            for h in range(4):
                bn = 2 + h // 2
                c0 = (h % 2) * 130
                self.MM(self.ps[bn][:, c0:c0 + 130], lhsT=pT[:, h, :], rhs=mlv[:, t, h, :], start=True, stop=(t == 0), reads=[bpT, B_v],
                        writes=[self.PB[bn]], inc=(t == 0 and h % 2 == 1), skip=True)
                if t > 0:
                    self.MM(self.ps[bn][:, c0:c0 + 130], lhsT=mlqT[:, h, tl], rhs=Cbf[:, h, :], start=False, stop=True, reads=[B_q, B_Cbf],
                            writes=[self.PB[bn]], inc=(h % 2 == 1), skip=True)
            if not last:
                for h in range(4):
                    bc = 4 + h // 2
                    c0 = (h % 2) * 130
                    self.MM(self.ps[bc][:, c0:c0 + 130], lhsT=kt_[:, h, :], rhs=mlv[:, t, h, :], start=True, stop=True, reads=[bkt, B_v],
                            writes=[self.PB[bc]], inc=(h % 2 == 1), skip=True)
                for b2 in range(2):
                    cv = C32[:, 2 * b2:2 * b2 + 2, :].rearrange("p h e -> p (h e)")
                    if t == 0:
                        self.V(lambda e, b2=b2, cv=cv: e.tensor_copy(out=cv, in_=self.ps[4 + b2][:, 0:260]), reads=[self.PB[4 + b2]], writes=[B_C32])
                    else:
                        self.V(lambda e, b2=b2, cv=cv: e.tensor_tensor(out=cv, in0=self.ps[4 + b2][:, 0:260], in1=cv, op=ALU.add),
                               reads=[self.PB[4 + b2], B_C32], writes=[B_C32])
                self.V(lambda e, t=t: e.tensor_tensor(out=C32[:], in0=C32[:], in1=G[:, t, 8:12].unsqueeze(2).to_broadcast([128, 4, 130]), op=ALU.mult),
                       reads=[B_C32, B_G], writes=[B_C32])
                self.A(lambda e: e.copy(out=Cbf[:], in_=C32[:]), reads=[B_C32], writes=[B_Cbf])
            eb4 = G[:, t, 4:8]
            for b2 in range(2):
                den2 = self.ps[2 + b2][:, 0:260].rearrange("p (h e) -> p h e", e=130)[:, :, 128]
                self.V(lambda e, b2=b2, den2=den2: e.tensor_tensor(out=sD[:, 2 * b2:2 * b2 + 2], in0=den2, in1=eb4[:, 2 * b2:2 * b2 + 2], op=ALU.mult),
                       reads=[self.PB[2 + b2], B_G], writes=[B_st])
            self.V(lambda e: e.tensor_scalar(out=sND, in0=sD, scalar1=-1.0, scalar2=None, op0=ALU.mult), reads=[B_st], writes=[B_st])
            self.V(lambda e: e.tensor_tensor(out=sD, in0=sD, in1=sND, op=ALU.max), reads=[B_st], writes=[B_st])
            self.V(lambda e: e.tensor_scalar(out=sD, in0=sD, scalar1=1.0, scalar2=None, op0=ALU.max), reads=[B_st], writes=[B_st])
            self.V(lambda e: e.reciprocal(out=sR, in_=sD), reads=[B_st], writes=[B_st])
            self.V(lambda e: e.tensor_tensor(out=sF, in0=sR, in1=eb4, op=ALU.mult), reads=[B_st, B_G], writes=[B_st])
            for h in range(4):
                bn = 2 + h // 2
                c0 = (h % 2) * 130
                hs = hm[:, h * 128:(h + 1) * 128]
                self.V(lambda e, bn=bn, c0=c0, hs=hs, h=h: e.tensor_scalar(out=hs, in0=self.ps[bn][:, c0:c0 + 128], scalar1=sF[:, h:h + 1], scalar2=None,
                                                                        op0=ALU.mult, op1=ALU.add, accum_out=sS1[:, h:h + 1]),
                       reads=[self.PB[bn], B_st], writes=[B_hmh[h], B_s1])
                self.A(lambda e, hs=hs, h=h: e.activation(out=junk2[:], in_=hs, func=AF.Square, accum_out=sS2[:, h:h + 1]), reads=[B_hmh[h]], writes=[B_j2, B_s2])
            self.V(lambda e: e.tensor_scalar(out=sM, in0=sS1, scalar1=1.0 / 128, scalar2=None, op0=ALU.mult), reads=[B_st, B_s1], writes=[B_st])
            self.V(lambda e: e.tensor_tensor(out=sT, in0=sM, in1=sM, op=ALU.mult), reads=[B_st], writes=[B_st])
            self.V(lambda e: e.scalar_tensor_tensor(out=sV, in0=sS2, scalar=1.0 / 128, in1=sT, op0=ALU.mult, op1=ALU.subtract), reads=[B_st, B_s2], writes=[B_st])
            self.V(lambda e: e.tensor_scalar(out=sV, in0=sV, scalar1=LN_EPS, scalar2=None, op0=ALU.add), reads=[B_st], writes=[B_st])
            self.A(lambda e: e.activation(out=sRS, in_=sV, func=AF.Sqrt), reads=[B_st], writes=[B_st])
            self.V(lambda e: e.reciprocal(out=sRS, in_=sRS), reads=[B_st], writes=[B_st])
            hm3 = hm[:].rearrange("p (h d) -> p h d", h=4)
            self.V(lambda e: e.tensor_tensor(out=hm3, in0=hm3, in1=sM.unsqueeze(2).to_broadcast([128, 4, 128]), op=ALU.subtract), reads=B_hmh + [B_st], writes=B_hmh)
            self.V(lambda e: e.tensor_tensor(out=hm3, in0=hm3, in1=sRS.unsqueeze(2).to_broadcast([128, 4, 128]), op=ALU.mult), reads=B_hmh + [B_st], writes=B_hmh)
            self.V(lambda e: e.tensor_tensor(out=hm[:], in0=hm[:], in1=vec[:, V_MLG:V_MLG + 512], op=ALU.mult), reads=B_hmh + [self.B_cst], writes=B_hmh)
            self.V(lambda e, t=t: e.tensor_tensor(out=yb[:], in0=hm[:], in1=osig[:, t, :], op=ALU.mult), reads=B_hmh + [B_o], writes=[B_yb_tm])
            tb = self.ps[bT][:].bitcast(BF16)
            for g in range(4):
                self.TR(tb[:, g * 128:(g + 1) * 128], yb[:, g * 128:(g + 1) * 128], self.identb[:], reads=[B_yb_tm, self.B_k],
                        writes=[self.PB[bT]], inc=(g == 3))
            self.A(lambda e, t=t, tb=tb: e.copy(out=ybT[:, :, t * 128:(t + 1) * 128], in_=tb[:, 0:512].rearrange("p (g t) -> p g t", g=4)),
                   reads=[self.PB[bT]], writes=[B_yb])
        self.dump("ybT", ybT[:, 0, :], [B_yb])
        self.barrier()


K.ml_phase = _ml_phase


def _post_phase(self, s, xT, B_xT, yaT, B_ya, ybT, B_yb):
    S = self.S
    vec = self.vec
    with ExitStack() as c1:
        Rs = [self.sb("R%d" % i, [128, 8, 512], F32, c1) for i in range(2)]
        B_Rs = [[Buf("R%d_%d" % (i, c)) for c in range(8)] for i in range(2)]
        Rb = self.sb("Rb", [128, 8, 512], BF16, c1)
        mg = self.sb("mergedT", [128, 8, 512], BF16, c1)
        uT = self.sb("uT", [128, 32, 512], BF16, c1)
        pTb = self.sb("pTb", [128, 2, 512], BF16, c1)
        tmpA = [(self.sb("tmpA%d" % i, [128, 512], F32, c1), Buf("tmpA")) for i in range(4)]
        st = self.sb("lnst", [128, 4, 512], F32, c1)
        onesd = self.sb("onesd", [128, 128], F32, c1)
        pl = [(self.sb("pl%d" % i, [128, 256], F32, c1), Buf("pl%d" % i)) for i in range(4)]
        for t_, b_ in pl:
            if b_.name not in S.dsem:
                S.new_dma_sem(b_.name)
        B_Rb, B_mg, B_uT, B_pT, B_st, B_od = Buf("Rb"), Buf("mg"), Buf("uT"), Buf("pTb"), Buf("lnst"), Buf("onesd")
        self.V(lambda e: e.tensor_scalar(out=onesd[:], in0=self.ones, scalar1=1.0 / 1024, scalar2=None, op0=ALU.mult), reads=[self.B_cst], writes=[B_od])
        bank_i = [0]
        tmp_i = [0]

        def nb():
            b = bank_i[0] % 8
            bank_i[0] += 1
            return b

        def ntmp():
            t = tmpA[tmp_i[0] % 4]
            tmp_i[0] += 1
            return t

        def mmacc(bk, pairs, reads):
            n = len(pairs)
            for i, (l, r) in enumerate(pairs):
                self.MM(self.ps[bk][:, :], lhsT=l, rhs=r, start=(i == 0), stop=(i == n - 1), reads=reads, writes=[self.PB[bk]], inc=(i == n - 1))

        def merge_order():
            o = []
            for fc in range(8):
                o += [("gate", fc), ("upa", fc), ("gate", 8 + fc), ("upb", fc)]
            return o
        order = merge_order() + [("wout", g) for g in range(8)]
        for blk in range(4):
            if blk + 1 < 4:
                order += merge_order()
            order += [("ff1", g) for g in range(32)] + [("ff2", g) for g in range(8)]
            for fc in range(8):
                order += [("pleg", fc), ("plep", fc)]
            if blk + 1 < 4:
                order += [("wout", g) for g in range(8)]
        self.wstream_begin(order)
        wi = [0]

        def nextw():
            h = self.wget(wi[0])
            wi[0] += 1
            return h

        def layernorm(R, B_R, gcol, bcol, write_rb):
            b1, b2 = nb(), nb()
            mmacc(b1, [(onesd[:], R[:, c, :]) for c in range(8)], [B_od] + B_R)
            for c in range(8):
                sq, bsq = ntmp()
                self.A(lambda e: e.activation(out=sq[:], in_=R[:, c, :], func=AF.Square), reads=[B_R[c]], writes=[bsq])
                self.MM(self.ps[b2][:, :], lhsT=onesd[:], rhs=sq[:], start=(c == 0), stop=(c == 7), reads=[B_od, bsq], writes=[self.PB[b2]], inc=True)
            mean, rstd, nmr, t4 = st[:, 0, :], st[:, 1, :], st[:, 2, :], st[:, 3, :]
            self.V(lambda e: e.tensor_copy(out=mean, in_=self.ps[b1][:]), reads=[self.PB[b1]], writes=[B_st])
            self.V(lambda e: e.tensor_tensor(out=t4, in0=mean, in1=mean, op=ALU.mult), reads=[B_st], writes=[B_st])
            self.V(lambda e: e.tensor_tensor(out=t4, in0=self.ps[b2][:], in1=t4, op=ALU.subtract), reads=[B_st, self.PB[b2]], writes=[B_st])
            self.V(lambda e: e.tensor_scalar(out=t4, in0=t4, scalar1=LN_EPS, scalar2=None, op0=ALU.add), reads=[B_st], writes=[B_st])
            self.A(lambda e: e.activation(out=t4, in_=t4, func=AF.Sqrt), reads=[B_st], writes=[B_st])
            self.V(lambda e: e.reciprocal(out=rstd, in_=t4), reads=[B_st], writes=[B_st])
            self.V(lambda e: e.scalar_tensor_tensor(out=nmr, in0=mean, scalar=-1.0, in1=rstd, op0=ALU.mult, op1=ALU.mult), reads=[B_st], writes=[B_st])
            for c in range(8):
                self.V(lambda e: e.tensor_tensor(out=R[:, c, :], in0=R[:, c, :], in1=rstd, op=ALU.mult), reads=[B_R[c], B_st], writes=[B_R[c]])
                self.V(lambda e: e.tensor_tensor(out=R[:, c, :], in0=R[:, c, :], in1=nmr, op=ALU.add), reads=[B_R[c], B_st], writes=[B_R[c]])
                self.A(lambda e: e.activation(out=R[:, c, :], in_=R[:, c, :], func=AF.Identity, scale=vec[:, gcol + c:gcol + c + 1],
                                              bias=vec[:, bcol + c:bcol + c + 1]), reads=[B_R[c], self.B_cst], writes=[B_R[c]])
                if write_rb:
                    self.P(lambda e: e.tensor_copy(out=Rb[:, c, :], in_=R[:, c, :]), reads=[B_R[c]], writes=[B_Rb])

        def do_xT(blk):
            R, B_R = Rs[blk % 2], B_Rs[blk % 2]
            tok0 = blk * 512
            for tt in range(4):
                xs, bx = self.xs[self.xs_i % 2]
                self.xs_i += 1
                S.dma("sp", bx.name, xs[:], self.x_d[s, tok0 + tt * 128: tok0 + (tt + 1) * 128, :], writes=[bx])
                b1, b2 = nb(), nb()
                for c in range(8):
                    bk = b1 if c < 4 else b2
                    self.TR(self.ps[bk][:, (c % 4) * 128:(c % 4 + 1) * 128], xs[:, c * 128:(c + 1) * 128], self.ident,
                            reads=[bx, self.B_cst], writes=[self.PB[bk]], inc=(c % 4 == 3))
                self.V(lambda e: e.tensor_scalar(out=R[:, 0:4, tt * 128:(tt + 1) * 128], in0=self.ps[b1][:].rearrange("p (a b) -> p a b", a=4),
                                                 scalar1=ALPHA, scalar2=None, op0=ALU.mult), reads=[self.PB[b1]], writes=B_R[0:4])
                self.A(lambda e: e.activation(out=R[:, 4:8, tt * 128:(tt + 1) * 128], in_=self.ps[b2][:].rearrange("p (a b) -> p a b", a=4),
                                              func=AF.Copy, scale=ALPHA), reads=[self.PB[b2]], writes=B_R[4:8])

        def do_merge(blk):
            tsl = slice(blk * 512, blk * 512 + 512)
            for fc in range(8):
                ms = []
                for (src, B_src, nk) in ((yaT, B_ya, 4), (ybT, B_yb, 4)):
                    wg, bwg = nextw()
                    wu, bwu = nextw()
                    bg, bu = nb(), nb()
                    mmacc(bg, [(wg[:, kc, :], xT[:, kc, tsl]) for kc in range(8)], [bwg, B_xT])
                    mmacc(bu, [(wu[:, kc, :], src[:, kc, tsl]) for kc in range(nk)], [bwu, B_src])
                    sg, bsg = ntmp()
                    self.A(lambda e: e.activation(out=sg[:], in_=self.ps[bg][:], func=AF.Sigmoid), reads=[self.PB[bg]], writes=[bsg])
                    self.V(lambda e: e.tensor_tensor(out=sg[:], in0=sg[:], in1=self.ps[bu][:], op=ALU.mult), reads=[bsg, self.PB[bu]], writes=[bsg])
                    ms.append((sg, bsg))
                self.P(lambda e: e.tensor_tensor(out=mg[:, fc, :], in0=ms[0][0][:], in1=ms[1][0][:], op=ALU.add),
                       reads=[ms[0][1], ms[1][1]], writes=[B_mg])

        def do_wout(blk):
            R, B_R = Rs[blk % 2], B_Rs[blk % 2]
            for fc in range(8):
                ww, bww = nextw()
                bk = nb()
                mmacc(bk, [(ww[:, kc, :], mg[:, kc, :]) for kc in range(8)], [bww, B_mg])
                self.V(lambda e: e.tensor_tensor(out=R[:, fc, :], in0=self.ps[bk][:], in1=R[:, fc, :], op=ALU.add), reads=[self.PB[bk], B_R[fc]], writes=[B_R[fc]])

        def do_ffn(blk):
            R, B_R = Rs[blk % 2], B_Rs[blk % 2]
            for f in range(32):
                w1, bw1 = nextw()
                bk = nb()
                mmacc(bk, [(w1[:, kc, :], Rb[:, kc, :]) for kc in range(8)], [bw1, B_Rb])
                rl, brl = ntmp()
                self.A(lambda e: e.activation(out=rl[:], in_=self.ps[bk][:], func=AF.Relu), reads=[self.PB[bk]], writes=[brl])
                self.P(lambda e: e.tensor_tensor(out=uT[:, f, :], in0=rl[:], in1=rl[:], op=ALU.mult), reads=[brl], writes=[B_uT])
            for fc in range(8):
                w2, bw2 = nextw()
                bk = nb()
                mmacc(bk, [(w2[:, f, :], uT[:, f, :]) for f in range(32)], [bw2, B_uT])
                self.V(lambda e: e.scalar_tensor_tensor(out=R[:, fc, :], in0=R[:, fc, :], scalar=ALPHA, in1=self.ps[bk][:], op0=ALU.mult, op1=ALU.add),
                       reads=[B_R[fc], self.PB[bk]], writes=[B_R[fc]])
                self.P(lambda e: e.tensor_copy(out=Rb[:, fc, :], in_=R[:, fc, :]), reads=[B_R[fc]], writes=[B_Rb])

        def do_ple(blk):
            R, B_R = Rs[blk % 2], B_Rs[blk % 2]
            tok0 = blk * 512
            pb = [nb(), nb()]
            for tt in range(4):
                pt, bp = pl[tt]
                S.dma("sp", bp.name, pt[:], self.p_d[s, tok0 + tt * 128: tok0 + (tt + 1) * 128, :], writes=[bp])
                for c in range(2):
                    self.TR(self.ps[pb[c]][:, tt * 128:(tt + 1) * 128], pt[:, c * 128:(c + 1) * 128], self.ident, reads=[bp, self.B_cst],
                            writes=[self.PB[pb[c]]], inc=True)
            for c in range(2):
                self.V(lambda e: e.tensor_copy(out=pTb[:, c, :], in_=self.ps[pb[c]][:]), reads=[self.PB[pb[c]]], writes=[B_pT])
            for fc in range(8):
                wg, bwg = nextw()
                wp, bwp = nextw()
                bg, bp2 = nb(), nb()
                mmacc(bg, [(wg[:, kc, :], Rb[:, kc, :]) for kc in range(8)], [bwg, B_Rb])
                mmacc(bp2, [(wp[:, kc, :], pTb[:, kc, :]) for kc in range(2)], [bwp, B_pT])
                sg, bsg = ntmp()
                self.A(lambda e: e.activation(out=sg[:], in_=self.ps[bg][:], func=AF.Sigmoid), reads=[self.PB[bg]], writes=[bsg])
                self.V(lambda e: e.tensor_tensor(out=sg[:], in0=sg[:], in1=self.ps[bp2][:], op=ALU.mult), reads=[bsg, self.PB[bp2]], writes=[bsg])
                self.P(lambda e: e.tensor_tensor(out=R[:, fc, :], in0=R[:, fc, :], in1=sg[:], op=ALU.add), reads=[B_R[fc], bsg], writes=[B_R[fc]])

        def do_out(blk):
            R, B_R = Rs[blk % 2], B_Rs[blk % 2]
            tok0 = blk * 512
            for tt in range(4):
                o, bo = self.xs[self.xs_i % 2]
                self.xs_i += 1
                b1, b2 = nb(), nb()
                for c in range(8):
                    bk = b1 if c < 4 else b2
                    self.TR(self.ps[bk][:, (c % 4) * 128:(c % 4 + 1) * 128], R[:, c, tt * 128:(tt + 1) * 128], self.ident,
                            reads=[B_R[c], self.B_cst], writes=[self.PB[bk]], inc=(c % 4 == 3))
                self.V(lambda e: e.tensor_copy(out=o[:, 0:512], in_=self.ps[b1][:]), reads=[self.PB[b1]], writes=[bo])
                self.A(lambda e: e.copy(out=o[:, 512:1024], in_=self.ps[b2][:]), reads=[self.PB[b2]], writes=[bo])
                S.dma("pool", bo.name, self.out_d[s, tok0 + tt * 128: tok0 + (tt + 1) * 128, :], o[:], reads=[bo])

        do_xT(0)
        do_merge(0)
        do_wout(0)
        for blk in range(4):
            R, B_R = Rs[blk % 2], B_Rs[blk % 2]
            layernorm(R, B_R, V_L1G, V_L1B, True)
            if blk == 0:
                self.dump("h1T", R[:, 0, :], [B_R[0]])
            if blk + 1 < 4:
                do_merge(blk + 1)
            do_ffn(blk)
            do_ple(blk)
            layernorm(R, B_R, V_L2G, V_L2B, False)
            if blk + 1 < 4:
                do_xT(blk + 1)
                do_wout(blk + 1)
            do_out(blk)
        self.barrier()


K.post_phase = _post_phase


def build_program(nseq=NSEQ, dbg=None):
    k = K(nseq=nseq, dbg=dbg)
    k.setup()
    for s in range(nseq):
        with ExitStack() as c0:
            xT = k.sb("xT", [128, 8, 2048], BF16, c0)
            yaT = k.sb("yaT", [128, 4, 2048], BF16, c0)
            ybT = k.sb("ybT", [128, 4, 2048], BF16, c0)
            wifk = k.sb("wifk", [128, NT, 8], F32, c0)
            B_xT, B_ya, B_yb, B_w = Buf("xT"), Buf("yaT"), Buf("ybT"), Buf("wifk")
            k.phaseA(s, xT, B_xT)
            if s == 0:
                k.cast_rest([b for _, b in k.xs])
            k.dsa_phase(s, xT, B_xT, yaT, B_ya, wifk, B_w)
            k.ml_phase(s, xT, B_xT, ybT, B_yb, wifk, B_w)
            k.post_phase(s, xT, B_xT, yaT, B_ya, ybT, B_yb)
            k.barrier()
    return k.finish(), k


_CACHE = {}


def kernel(x, p, positions, w_in, conv_w, conv_b, b_igate, b_fgate, ml_norm_g, w_up_a, w_up_b, w_out,
           ln1_g, ln1_b, w_ff1, w_ff2, w_ple_gate, w_ple_proj, ln2_g, ln2_b):
    f = lambda a: np.asarray(a, dtype=np.float32)
    wall = host_weights(f(w_in)[0], f(w_up_a)[0], f(w_up_b)[0], f(w_out)[0], f(w_ff1)[0], f(w_ff2)[0], f(w_ple_gate)[0], f(w_ple_proj)[0])
    cst, vec = host_consts(f(conv_w)[0], f(conv_b)[0], f(b_igate)[0], f(b_fgate)[0], f(ml_norm_g)[0], f(ln1_g)[0], f(ln1_b)[0],
                           f(ln2_g)[0], f(ln2_b)[0])
    x = f(x)
    p = f(p)[0]
    pos = np.asarray(positions, dtype=np.int32)
    if "nc" not in _CACHE:
        _CACHE["nc"] = build_program(NSEQ)[0]
    nc = _CACHE["nc"]
    in_maps = []
    for c in range(NCORES):
        sl = slice(c * NSEQ, (c + 1) * NSEQ)
        in_maps.append(dict(x=np.ascontiguousarray(x[sl]), p=np.ascontiguousarray(p[sl]), pos=np.ascontiguousarray(pos[sl]),
                            wall=wall, cst=cst, vec=vec))
    res = run_bass_kernel_spmd(nc, in_maps, core_ids=list(range(NCORES)))
    out = np.concatenate([np.asarray(r["out"], dtype=np.float32) for r in res.results], axis=0)
    return out
```

```python
import math
from contextlib import ExitStack

import numpy as np
import concourse.bass as bass
import concourse.mybir as mybir
from concourse.bass_utils import run_bass_kernel_spmd

F32 = mybir.dt.float32
BF16 = mybir.dt.bfloat16
I32 = mybir.dt.int32
ALU = mybir.AluOpType
AF = mybir.ActivationFunctionType
AX = mybir.AxisListType

NCORES = 8
D = 1024
S = 2048
NT = S // 128
NSEQ = 2
DFF = 4096
PLE = 256
TOPK = 256
ALPHA = 2.0 ** 0.25
LN_EPS = 1e-5
BIG = 30000.0
NEG = -1.0e30
PI = math.pi
NBIS = 22
BIS_LO = -256.0 - 2.0 ** -14
BIS_HI = BIS_LO + 512.0

OFF = {}
_o = 0
for _n, _w in (('att_q', 512), ('att_k', 64), ('att_v', 64), ('idx_q', 512), ('idx_k', 64), ('idx_w', 8),
               ('ml_q', 512), ('ml_k', 512), ('ml_v', 512), ('ml_i', 4), ('ml_f', 4), ('ml_o', 512),
               ('gate_a', 1024), ('gate_b', 1024)):
    OFF[_n] = (_o, _w)
    _o += _w

G_QA, G_QAP, G_KA, G_KAP, G_QI, G_QIP, G_KI, G_KIP, G_MLQ, G_MLK = 0, 4, 8, 9, 10, 14, 18, 19, 20, 24
NG_FM = 28
TM_COLS = 80 + 512 + 512


class Buf:
    __slots__ = ("name", "w", "r")

    def __init__(self, name=""):
        self.name = name
        self.w = {}
        self.r = {}


class Sched:
    def __init__(self, nc, ctx):
        self.nc = nc
        self.ctx = ctx
        self.engs = {"pe": nc.tensor, "dve": nc.vector, "act": nc.scalar, "pool": nc.gpsimd, "sp": nc.sync}
        self.sem = {k: ctx.enter_context(nc.semaphore("s_" + k)) for k in ("pe", "dve", "act", "pool")}
        self.cnt = {k: 0 for k in self.sem}
        self.seen = {k: {} for k in self.engs}
        self.pending = {k: [] for k in self.sem}
        self.dsem = {}
        self.nops = {k: 0 for k in self.engs}

    def _wait(self, eng, deps):
        e = self.engs[eng]
        for key, (sem, val) in deps.items():
            if self.seen[eng].get(key, 0) >= val:
                continue
            e.wait_ge(sem, val)
            self.seen[eng][key] = val

    def _deps(self, eng, reads, writes):
        deps = {}

        def need(key, tok):
            if key not in deps or deps[key][1] < tok[1]:
                deps[key] = tok

        for b in reads:
            for key, tok in b.w.items():
                if key == eng and eng == "pe":
                    continue
                need(key, tok)
        for b in writes:
            for key, tok in b.w.items():
                if key == eng:
                    continue
                need(key, tok)
            for key, tok in b.r.items():
                if key == eng:
                    continue
                need(key, tok)
        return deps

    def op(self, eng, fn, reads=(), writes=(), inc=True):
        deps = self._deps(eng, reads, writes)
        self._wait(eng, deps)
        ins = fn(self.engs[eng])
        self.nops[eng] += 1
        self.pending[eng].append((tuple(reads), tuple(writes)))
        if inc:
            self.cnt[eng] += 1
            ins.then_inc(self.sem[eng], 1)
            tok = (self.sem[eng], self.cnt[eng])
            for rd, wr in self.pending[eng]:
                for b in rd:
                    b.r[eng] = tok
                for b in wr:
                    b.w = {eng: tok}
                    b.r = {}
            self.pending[eng] = []
        return ins

    def new_dma_sem(self, name):
        sem = self.ctx.enter_context(self.nc.semaphore("d_" + name))
        self.dsem[name] = [sem, 0]
        return name

    def dma(self, queue, semname, out, in_, reads=(), writes=()):
        deps = self._deps("dma:" + semname, reads, writes)
        self._wait(queue, deps)
        st = self.dsem[semname]
        st[1] += 16
        self.engs[queue].dma_start(out=out, in_=in_).then_inc(st[0], 16)
        self.nops[queue] += 1
        tok = (st[0], st[1])
        key = "dma:" + semname
        for b in reads:
            b.r[key] = tok
        for b in writes:
            b.w = {key: tok}
            b.r = {}

    def wait_all(self, eng, bufs):
        deps = {}
        for b in bufs:
            for key, tok in list(b.w.items()):
                if key not in deps or deps[key][1] < tok[1]:
                    deps[key] = tok
        self._wait(eng, deps)


def _bf16_round(a):
    return a


WFAM = [
    ("fm", NG_FM, 8, 128), ("tms", 1, 8, 80), ("mlv", 1, 8, 512), ("mlo", 1, 8, 512),
    ("gate", 16, 8, 128), ("upa", 8, 4, 128), ("upb", 8, 4, 128), ("wout", 8, 8, 128),
    ("ff1", 32, 8, 128), ("ff2", 8, 32, 128), ("pleg", 8, 8, 128), ("plep", 8, 2, 128),
]
WINFO = {}
_off = 0
for _n, _g, _kc, _c in WFAM:
    WINFO[_n] = (_off, _g, _kc, _c)
    _off += _g * _kc * _c
WTOT = _off

C_ID, C_TRI, C_CM, C_ONE, C_NEGI = 0, 128, 256, 384, 512
CST = 640
V_CW, V_CB, V_L1G, V_L1B, V_L2G, V_L2B, V_INVF, V_SGN, V_BI, V_BF, V_MLG = 0, 32, 40, 48, 56, 64, 72, 73, 74, 78, 82
VEC = 82 + 512


def _group_layout(wmat, kc, cols):
    return np.ascontiguousarray(wmat.reshape(kc, 128, cols).transpose(1, 0, 2).reshape(128, kc * cols))


def host_weights(w_in, w_up_a, w_up_b, w_out, w_ff1, w_ff2, w_ple_gate, w_ple_proj):
    wall = np.zeros((128, WTOT), np.float32)

    def put(fam, g, wmat):
        off, ng, kc, cols = WINFO[fam]
        o = off + g * kc * cols
        wall[:, o:o + kc * cols] = _group_layout(np.ascontiguousarray(wmat), kc, cols)

    def cols_of(name):
        o, w = OFF[name]
        return w_in[:, o:o + w]

    def perm64(m):
        K, N = m.shape
        mm = m.reshape(K, N // 64, 2, 32)
        return mm[:, :, ::-1, :].reshape(K, N)

    qa, ka, qi, ki = cols_of('att_q'), cols_of('att_k'), cols_of('idx_q'), cols_of('idx_k')
    qap, kap, qip, kip = perm64(qa), perm64(ka), perm64(qi), perm64(ki)
    for g in range(4):
        put("fm", G_QA + g, qa[:, g * 128:(g + 1) * 128])
        put("fm", G_QAP + g, qap[:, g * 128:(g + 1) * 128])
        put("fm", G_QI + g, qi[:, g * 128:(g + 1) * 128])
        put("fm", G_QIP + g, qip[:, g * 128:(g + 1) * 128])
        put("fm", G_MLQ + g, cols_of('ml_q')[:, g * 128:(g + 1) * 128])
        put("fm", G_MLK + g, cols_of('ml_k')[:, g * 128:(g + 1) * 128])
    put("fm", G_KA, np.concatenate([ka, ka], axis=1))
    put("fm", G_KAP, np.concatenate([kap, kap], axis=1))
    put("fm", G_KI, np.concatenate([ki, ki], axis=1))
    put("fm", G_KIP, np.concatenate([kip, kip], axis=1))
    put("tms", 0, np.concatenate([cols_of('att_v'), cols_of('idx_w'), cols_of('ml_i'), cols_of('ml_f')], axis=1))
    put("mlv", 0, cols_of('ml_v'))
    put("mlo", 0, cols_of('ml_o'))
    ga, gb = cols_of('gate_a'), cols_of('gate_b')
    for g in range(8):
        put("gate", g, ga[:, g * 128:(g + 1) * 128])
        put("gate", 8 + g, gb[:, g * 128:(g + 1) * 128])
        put("upa", g, w_up_a[:, g * 128:(g + 1) * 128])
        put("upb", g, w_up_b[:, g * 128:(g + 1) * 128])
        put("wout", g, w_out[:, g * 128:(g + 1) * 128])
        put("ff2", g, w_ff2[:, g * 128:(g + 1) * 128])
        put("pleg", g, w_ple_gate[:, g * 128:(g + 1) * 128])
        put("plep", g, w_ple_proj[:, g * 128:(g + 1) * 128])
    for g in range(32):
        put("ff1", g, w_ff1[:, g * 128:(g + 1) * 128])
    return wall


def host_consts(conv_w, conv_b, b_igate, b_fgate, ml_norm_g, ln1_g, ln1_b, ln2_g, ln2_b):
    cst = np.zeros((128, CST), np.float32)
    cst[:, C_ID:C_ID + 128] = np.eye(128, dtype=np.float32)
    r = np.arange(128)
    cst[:, C_TRI:C_TRI + 128] = (r[:, None] <= r[None, :]).astype(np.float32)
    cst[:, C_CM:C_CM + 128] = np.where(r[None, :] <= r[:, None], 0.0, NEG)
    cst[:, C_ONE:C_ONE + 128] = 1.0
    cst[:, C_NEGI:C_NEGI + 128] = -BIG * np.eye(128, dtype=np.float32)
    vec = np.zeros((128, VEC), np.float32)
    vec[:, V_CW:V_CW + 32] = conv_w.reshape(4, 8, 128).transpose(2, 1, 0).reshape(128, 32)
    vec[:, V_CB:V_CB + 8] = conv_b.reshape(8, 128).T
    for o, v in ((V_L1G, ln1_g), (V_L1B, ln1_b), (V_L2G, ln2_g), (V_L2B, ln2_b)):
        vec[:, o:o + 8] = v.reshape(8, 128).T
    invf = 1.0 / (np.float32(10000.0) ** (np.arange(0, 64, 2, dtype=np.float32) / np.float32(64)))
    p = np.arange(128)
    vec[:, V_INVF] = invf.astype(np.float32)[p % 32]
    vec[:, V_SGN] = np.where((p % 64) < 32, -1.0, 1.0)
    vec[:, V_BI:V_BI + 4] = b_igate.reshape(1, 4)
    vec[:, V_BF:V_BF + 4] = b_fgate.reshape(1, 4)
    vec[:, V_MLG:V_MLG + 512] = ml_norm_g.reshape(1, 512)
    return cst, vec


class K:
    def __init__(self, nseq=NSEQ, stop_after=None, dbg=None):
        self.nseq = nseq
        self.stop_after = stop_after
        self.dbg = dbg or {}
        self.nc = nc = bass.Bass("TRN2", target_bir_lowering=False)
        self.ctx = ExitStack()

    def sb(self, name, shape, dt, ctx=None):
        self._sbn = getattr(self, "_sbn", 0) + 1
        return (ctx or self.ctx).enter_context(self.nc.sbuf_tensor("sb%d_%s" % (self._sbn, name), shape, dt))

    def dram(self, name, shape, dt=F32, kind="ExternalInput"):
        return self.nc.dram_tensor(name, shape, dt, kind=kind).ap()

    def V(self, fn, reads=(), writes=()):
        return self.S.op("dve", fn, reads, writes)

    def A(self, fn, reads=(), writes=()):
        return self.S.op("act", fn, reads, writes)

    def P(self, fn, reads=(), writes=()):
        return self.S.op("pool", fn, reads, writes)

    def MM(self, out, lhsT, rhs, start, stop, reads, writes, inc, skip=False):
        return self.S.op("pe", lambda e: e.matmul(out, lhsT=lhsT, rhs=rhs, start=start, stop=stop,
                                                  skip_group_check=skip), reads, writes, inc=inc)

    def TR(self, out, in_, ident, reads, writes, inc):
        return self.S.op("pe", lambda e: e.transpose(out, in_, ident), reads, writes, inc=inc)

    def barrier(self):
        S = self.S
        for eng in ("pe", "dve", "act", "pool"):
            assert not S.pending[eng], eng
        deps = {k: (S.sem[k], S.cnt[k]) for k in S.sem if S.cnt[k] > 0}
        for name, (sem, val) in S.dsem.items():
            if val > 0 and not name.startswith("wc"):
                deps["dma:" + name] = (sem, val)
        for eng in S.engs:
            d = {k: v for k, v in deps.items() if k != eng}
            S._wait(eng, d)

    def wstream_begin(self, order):
        self.w_order = list(order)
        self.w_issued = 0
        self.w_handles = {}
        self.w_occ = {}

    def _w_issue(self):
        i = self.w_issued
        fam, g = self.w_order[i]
        off, ng, kc, cols = WINFO[fam]
        n = kc * cols
        o = off + g * n
        if n <= 1024:
            k = self.ws_small_i % len(self.ws_small)
            self.ws_small_i += 1
            t, b = self.ws_small[k]
        else:
            k = self.ws_big_i % len(self.ws_big)
            self.ws_big_i += 1
            t, b = self.ws_big[k]
        rd = [self.B_wsc] if o + n <= self.wsplit else [self.B_wsc, self.B_wscB]
        self.S.dma("sp", b.name, t[:, 0:n], self.wsc[:, o:o + n], reads=rd, writes=[b])
        self.w_handles[i] = (t[:, 0:n].rearrange("p (a b) -> p a b", a=kc), b)
        self.w_issued += 1

    def _w_next_slot(self):
        fam, g = self.w_order[self.w_issued]
        off, ng, kc, cols = WINFO[fam]
        if kc * cols <= 1024:
            return ("s", self.ws_small_i % len(self.ws_small))
        return ("b", self.ws_big_i % len(self.ws_big))

    def wget(self, i, la=2):
        while self.w_issued < min(len(self.w_order), i + 1 + la):
            slot = self._w_next_slot()
            occ = self.w_occ.get(slot)
            if occ is not None and occ >= i and self.w_issued > i:
                break
            self.w_occ[slot] = self.w_issued
            self._w_issue()
        h = self.w_handles.pop(i)
        return h

    def setup(self):
        nc, nseq = self.nc, self.nseq
        self.S = S = Sched(nc, self.ctx)
        self.x_d = self.dram("x", [nseq, 2048, D])
        self.p_d = self.dram("p", [nseq, 2048, PLE])
        self.pos_d = self.dram("pos", [nseq, 2048], I32)
        self.wall_d = self.dram("wall", [128, WTOT])
        self.cst_d = self.dram("cst", [128, CST])
        self.vec_d = self.dram("vec", [128, VEC])
        self.out_d = self.dram("out", [nseq, 2048, D], kind="ExternalOutput")
        self.wsc = self.dram("wsc", [128, WTOT], BF16, kind="Internal")
        self.dbg_d = {}
        for name, shape in self.dbg.items():
            if name.startswith("_"):
                continue
            self.dbg_d[name] = self.dram("dbg_" + name, shape, kind="ExternalOutput")
        for n in ("ld", "wc", "wl", "xl", "pl", "st", "dbg"):
            S.new_dma_sem(n)
        self.cst = self.sb("cst", [128, CST], F32)
        self.vec = self.sb("vec", [128, VEC], F32)
        self.B_cst = Buf("cst")
        self.B_wsc = Buf("wsc")
        S.dma("sp", "ld", self.cst[:], self.cst_d[:], writes=[self.B_cst])
        S.dma("sp", "ld", self.vec[:], self.vec_d[:], writes=[self.B_cst])
        CH = 8192
        self.B_wscB = Buf("wscB")
        S.new_dma_sem("wc2")
        self.wsplit = ((WINFO["gate"][0] + CH - 1) // CH) * CH
        for o in range(0, self.wsplit, CH):
            n = min(CH, WTOT - o)
            S.dma("pool", "wc", self.wsc[:, o:o + n], self.wall_d[:, o:o + n], writes=[self.B_wsc])
        self.ident = self.cst[:, C_ID:C_ID + 128]
        self.triT = self.cst[:, C_TRI:C_TRI + 128]
        self.cmask = self.cst[:, C_CM:C_CM + 128]
        self.ones = self.cst[:, C_ONE:C_ONE + 128]
        self.identb = self.sb("identb", [128, 128], BF16)
        self.negI4 = self.sb("negI4", [128, 512], BF16)
        self.zerob = self.sb("zerob", [128, 512], BF16)
        self.B_k = Buf("kconst")
        self.V(lambda e: e.tensor_copy(out=self.identb[:], in_=self.ident), reads=[self.B_cst], writes=[self.B_k])
        for r in range(4):
            self.V(lambda e, r=r: e.tensor_copy(out=self.negI4[:, r * 128:(r + 1) * 128], in_=self.cst[:, C_NEGI:C_NEGI + 128]),
                   reads=[self.B_cst], writes=[self.B_k])
        self.V(lambda e: e.memset(self.zerob[:], 0.0), writes=[self.B_k])
        self.ps = [self.ctx.enter_context(nc.psum_tensor("ps%d" % i, [128, 512], F32)) for i in range(8)]
        self.PB = [Buf("ps%d" % i) for i in range(8)]
        self.ws_small = [(self.sb("wss%d" % i, [128, 1024], BF16), Buf("wss%d" % i)) for i in range(4)]
        self.ws_big = [(self.sb("wsb%d" % i, [128, 4096], BF16), Buf("wsb%d" % i)) for i in range(2)]
        self.ws_small_i = 0
        self.ws_big_i = 0
        for t_, b_ in self.ws_small + self.ws_big:
            S.new_dma_sem(b_.name)
        self.xs = [(self.sb("xs%d" % i, [128, D], F32), Buf("xs%d" % i)) for i in range(2)]
        self.xs_i = 0
        for t_, b_ in self.xs:
            S.new_dma_sem(b_.name)

    def cast_rest(self, after_bufs):
        S = self.S
        S.wait_all("pool", after_bufs)
        CH = 8192
        for o in range(self.wsplit, WTOT, CH):
            n = min(CH, WTOT - o)
            S.dma("pool", "wc2", self.wsc[:, o:o + n], self.wall_d[:, o:o + n], writes=[self.B_wscB])

    def dump(self, name, ap, bufs):
        if name in self.dbg_d:
            self.S.dma("pool", "dbg", self.dbg_d[name], ap, reads=bufs)

    def finish(self):
        S = self.S
        for name in S.dsem:
            sem, val = S.dsem[name]
            if val > 0:
                self.nc.gpsimd.wait_ge(sem, val)
        self.ctx.close()
        return self.nc


def _phaseA(self, s, xT, B_xT):
    S = self.S
    for t in range(NT):
        xs, bx = self.xs[self.xs_i % 2]
        self.xs_i += 1
        S.dma("sp", bx.name, xs[:], self.x_d[s, t * 128:(t + 1) * 128, :], writes=[bx])
        pair = (t % 4) * 2
        for c in range(8):
            bk = pair + c // 4
            self.TR(self.ps[bk][:, (c % 4) * 128:(c % 4 + 1) * 128], xs[:, c * 128:(c + 1) * 128], self.ident,
                    reads=[bx, self.B_cst], writes=[self.PB[bk]], inc=(c % 4 == 3))
        self.V(lambda e: e.tensor_copy(out=xT[:, 0:4, t * 128:(t + 1) * 128],
                                       in_=self.ps[pair][:].rearrange("p (a b) -> p a b", a=4)),
               reads=[self.PB[pair]], writes=[B_xT])
        self.A(lambda e: e.copy(out=xT[:, 4:8, t * 128:(t + 1) * 128],
                                in_=self.ps[pair + 1][:].rearrange("p (a b) -> p a b", a=4)),
               reads=[self.PB[pair + 1]], writes=[B_xT])


K.phaseA = _phaseA


def _rope_tables(self, s, cosT, sinT, posi, ang, tmp, Bt):
    S = self.S
    MAGIC = 12582912.0
    C1 = 6.28125
    C2 = 2 * PI - C1
    vec = self.vec
    S.dma("sp", "ld", posi[:], self.pos_d[s:s + 1, :].partition_broadcast(128), writes=[Bt])
    self.V(lambda e: e.tensor_copy(out=ang[:], in_=posi[:]), reads=[Bt], writes=[Bt])
    self.V(lambda e: e.tensor_scalar(out=ang[:], in0=ang[:], scalar1=vec[:, V_INVF:V_INVF + 1], scalar2=None, op0=ALU.mult),
           reads=[Bt, self.B_cst], writes=[Bt])

    def red(dst):
        self.V(lambda e: e.tensor_scalar(out=dst[:], in0=ang[:], scalar1=1.0 / (2 * PI), scalar2=MAGIC, op0=ALU.mult, op1=ALU.add),
               reads=[Bt], writes=[Bt])
        self.V(lambda e: e.tensor_scalar(out=dst[:], in0=dst[:], scalar1=-MAGIC, scalar2=None, op0=ALU.add), reads=[Bt], writes=[Bt])
        self.V(lambda e: e.scalar_tensor_tensor(out=tmp[:], in0=dst[:], scalar=-C1, in1=ang[:], op0=ALU.mult, op1=ALU.add),
               reads=[Bt], writes=[Bt])
        self.V(lambda e: e.scalar_tensor_tensor(out=dst[:], in0=dst[:], scalar=-C2, in1=tmp[:], op0=ALU.mult, op1=ALU.add),
               reads=[Bt], writes=[Bt])
        self.V(lambda e: e.tensor_scalar(out=dst[:], in0=dst[:], scalar1=-3.1415925, scalar2=3.1415925, op0=ALU.max, op1=ALU.min),
               reads=[Bt], writes=[Bt])

    red(sinT)
    self.A(lambda e: e.activation(out=sinT[:], in_=sinT[:], func=AF.Sin), reads=[Bt], writes=[Bt])
    self.V(lambda e: e.tensor_scalar(out=sinT[:], in0=sinT[:], scalar1=vec[:, V_SGN:V_SGN + 1], scalar2=None, op0=ALU.mult),
           reads=[Bt, self.B_cst], writes=[Bt])
    self.V(lambda e: e.tensor_scalar(out=ang[:], in0=ang[:], scalar1=PI / 2, scalar2=None, op0=ALU.add), reads=[Bt], writes=[Bt])
    red(cosT)
    self.A(lambda e: e.activation(out=cosT[:], in_=cosT[:], func=AF.Sin), reads=[Bt], writes=[Bt])


K.rope_tables = _rope_tables


def _proj_fm_banks(self, wh, xT, B_xT, tok0, banks):
    wt, bw = wh
    for j, bk in enumerate(banks):
        for kc in range(8):
            self.MM(self.ps[bk][:, :], lhsT=wt[:, kc, :], rhs=xT[:, kc, tok0 + j * 512: tok0 + (j + 1) * 512],
                    start=(kc == 0), stop=(kc == 7), reads=[bw, B_xT], writes=[self.PB[bk]], inc=(kc == 7))


K.proj_fm_banks = _proj_fm_banks


def _dsa_inproj(self, s, xT, B_xT, T):
    S = self.S
    cosT, sinT, Bt = T["cosT"], T["sinT"], T["Btab"]
    units = []
    for g in range(4):
        units.append((G_QA + g, G_QAP + g, lambda a, b, g=g: [(slice(0, 128), T["qaT"][:, g, a:b])]))
    units.append((G_KA, G_KAP, lambda a, b: [(slice(0, 64), T["kaT"][0:64, 0, a:b]), (slice(64, 128), T["kaT"][64:128, 1, a:b])]))
    for g in range(4):
        units.append((G_QI + g, G_QIP + g, lambda a, b, g=g: [(slice(0, 128), T["qiT"][:, g, a:b])]))
    units.append((G_KI, G_KIP, lambda a, b: [(slice(0, 64), T["kiT"][0:64, 0, a:b]), (slice(64, 128), T["kiT"][64:128, 1, a:b])]))
    order = []
    for (gm, gp, _) in units:
        order += [("fm", gm), ("fm", gp)]
    order.append(("tms", 0))
    self.wstream_begin(order)
    B_dst = T["B_qk"]
    self.P(lambda e: e.memset(T["kaT"][:], 0.0), writes=[B_dst])
    self.P(lambda e: e.memset(T["kiT"][:], 0.0), writes=[B_dst])
    ucount = 0
    for ui, (gm, gp, dst) in enumerate(units):
        whm = self.wget(2 * ui)
        whp = self.wget(2 * ui + 1)
        for half in range(2):
            bset = (ucount % 2) * 4
            ucount += 1
            tok0 = half * 1024
            self.proj_fm_banks(whm, xT, B_xT, tok0, [bset, bset + 1])
            self.proj_fm_banks(whp, xT, B_xT, tok0, [bset + 2, bset + 3])
            for j in range(2):
                a = tok0 + j * 512
                k = (half * 2 + j) % 2
                t1, b1 = T["rt1"][k]
                t2, b2 = T["rt2"][k]
                self.V(lambda e, j=j, a=a, t1=t1: e.tensor_tensor(out=t1[:], in0=self.ps[bset + j][:], in1=cosT[:, a:a + 512], op=ALU.mult),
                       reads=[self.PB[bset + j], Bt], writes=[b1])
                self.V(lambda e, j=j, a=a, t2=t2: e.tensor_tensor(out=t2[:], in0=self.ps[bset + 2 + j][:], in1=sinT[:, a:a + 512], op=ALU.mult),
                       reads=[self.PB[bset + 2 + j], Bt], writes=[b2])
                for psl, dap in dst(a, a + 512):
                    self.P(lambda e, t1=t1, t2=t2, psl=psl, dap=dap: e.tensor_tensor(out=dap, in0=t1[psl, :], in1=t2[psl, :], op=ALU.add),
                           reads=[b1, b2], writes=[B_dst])
    wt, bw = self.wget(len(order) - 1)
    va, wif, B_tm = T["va"], T["wif"], T["B_tm"]
    self.V(lambda e: e.memset(va[:, :, 64:128], 1.0), writes=[B_tm])
    for t in range(NT):
        bk = t % 4
        for kc in range(8):
            self.MM(self.ps[bk][:, 0:80], lhsT=xT[:, kc, t * 128:(t + 1) * 128], rhs=wt[:, kc, :],
                    start=(kc == 0), stop=(kc == 7), reads=[bw, B_xT], writes=[self.PB[bk]], inc=(kc == 7))
        self.A(lambda e, t=t, bk=bk: e.copy(out=va[:, t, 0:64], in_=self.ps[bk][:, 0:64]), reads=[self.PB[bk]], writes=[B_tm])
        self.V(lambda e, t=t, bk=bk: e.tensor_copy(out=wif[:, t, :], in_=self.ps[bk][:, 64:80]), reads=[self.PB[bk]], writes=[B_tm])


K.dsa_inproj = _dsa_inproj


def _dsa_score_units(self, j, T):
    n = 128 * (j + 1)
    sc, bscs = T["sc"][j % 2]
    wif = T["wif"]
    B_qk, B_tm = T["B_qk"], T["B_tm"]
    qiT, kiT = T["qiT"], T["kiT"]
    units = []
    nchunk = (n + 511) // 512
    for c in range(nchunk):
        w = min(512, n - c * 512)
        for h in range(8):
            def unit(c=c, w=w, h=h, last=(c == nchunk - 1 and h == 7)):
                bsc = bscs[c]
                cnt = T["scnt"][0]
                T["scnt"][0] += 1
                bk = cnt % 2
                self.MM(self.ps[bk][:, 0:w], lhsT=qiT[:, h // 2, j * 128:(j + 1) * 128], rhs=kiT[:, h % 2, c * 512:c * 512 + w],
                        start=True, stop=True, reads=[B_qk], writes=[self.PB[bk]], inc=True)
                dst = sc[:, c * 512:c * 512 + w]
                rl, brl = T["rl"][cnt % 4]
                if h == 0:
                    self.V(lambda e: e.tensor_scalar(out=dst, in0=self.ps[bk][:, 0:w], scalar1=0.0, scalar2=wif[:, j, 0:1],
                                                     op0=ALU.max, op1=ALU.mult), reads=[self.PB[bk], B_tm], writes=[bsc])
                elif h <= 3:
                    self.A(lambda e: e.activation(out=rl[:, 0:w], in_=self.ps[bk][:, 0:w], func=AF.Relu), reads=[self.PB[bk]], writes=[brl])
                    self.V(lambda e: e.scalar_tensor_tensor(out=dst, in0=rl[:, 0:w], scalar=wif[:, j, h:h + 1], in1=dst,
                                                            op0=ALU.mult, op1=ALU.add), reads=[brl, B_tm, bsc], writes=[bsc])
                else:
                    self.V(lambda e: e.tensor_scalar(out=rl[:, 0:w], in0=self.ps[bk][:, 0:w], scalar1=0.0, scalar2=wif[:, j, h:h + 1],
                                                     op0=ALU.max, op1=ALU.mult), reads=[self.PB[bk], B_tm], writes=[brl])
                    self.P(lambda e: e.tensor_tensor(out=dst, in0=dst, in1=rl[:, 0:w], op=ALU.add), reads=[brl, bsc], writes=[bsc])
                if last:
                    bd = bscs[(j * 128) // 512]
                    self.V(lambda e: e.tensor_tensor(out=sc[:, j * 128:(j + 1) * 128], in0=sc[:, j * 128:(j + 1) * 128], in1=self.cmask,
                                                     op=ALU.add), reads=[bd, self.B_cst], writes=[bd])
            units.append(unit)
    return units


def _dsa_score(self, j, T):
    for u in self.dsa_score_units(j, T):
        u()


K.dsa_score_units = _dsa_score_units
K.dsa_score = _dsa_score


def _dsa_select(self, j, T):
    n = 128 * (j + 1)
    sc, bscs = T["sc"][j % 2]
    nm, bnm = T["nm"][j % 4]
    sm, bsm = T["sm"][j % 2]
    junk = T["junk"]
    mid, cnt, a, thr = sm[:, 0:1], sm[:, 1:2], sm[:, 2:3], sm[:, 3:4]
    if n <= TOPK:
        self.V(lambda e: e.memset(thr, -1.0e29), writes=[bsm])
    else:
        W = BIS_HI - BIS_LO
        self.V(lambda e: e.memset(mid, BIS_LO + W / 2), writes=[bsm])
        for i in range(NBIS):
            self.V(lambda e: e.tensor_scalar(out=junk[:, 0:n], in0=sc[:, 0:n], scalar1=mid, scalar2=None, op0=ALU.is_ge, op1=ALU.add,
                                             accum_out=cnt), reads=bscs + [bsm], writes=[bsm, T["B_junk"]])
            if i < NBIS - 1:
                st = W / (2 ** (i + 2))
                self.V(lambda e, st=st: e.tensor_scalar(out=a, in0=cnt, scalar1=TOPK - 0.5, scalar2=2 * st, op0=ALU.is_ge, op1=ALU.mult),
                       reads=[bsm], writes=[bsm])
                self.V(lambda e, st=st: e.scalar_tensor_tensor(out=mid, in0=a, scalar=-st, in1=mid, op0=ALU.add, op1=ALU.add),
                       reads=[bsm], writes=[bsm])
            else:
                st = W / (2 ** (i + 1))
                self.V(lambda e, st=st: e.tensor_scalar(out=a, in0=cnt, scalar1=TOPK - 0.5, scalar2=st, op0=ALU.is_ge, op1=ALU.mult),
                       reads=[bsm], writes=[bsm])
                self.V(lambda e, st=st: e.scalar_tensor_tensor(out=thr, in0=a, scalar=-st, in1=mid, op0=ALU.add, op1=ALU.add),
                       reads=[bsm], writes=[bsm])
    self.V(lambda e: e.tensor_scalar(out=nm[:, 0:n], in0=sc[:, 0:n], scalar1=thr, scalar2=None, op0=ALU.is_lt),
           reads=bscs + [bsm], writes=[bnm])


K.dsa_select = _dsa_select


def _dsa_select_act(self, j, T):
    n = 128 * (j + 1)
    sc, bscs = T["sc"][j % 2]
    nm, bnm = T["nm"][j % 4]
    sm, bsm = T["sm"][j % 2]
    junk = T["junk2"]
    nmid, cnt, a, thr = sm[:, 0:1], sm[:, 1:2], sm[:, 2:3], sm[:, 3:4]
    cb = T["cb"][:, j:j + 1]
    W = BIS_HI - BIS_LO
    self.V(lambda e: e.memset(nmid, -(BIS_LO + W / 2)), writes=[bsm])
    for i in range(NBIS):
        self.A(lambda e: e.activation(out=junk[:, 0:n], in_=sc[:, 0:n], func=AF.Sign, bias=nmid, scale=1.0, accum_out=cnt),
               reads=bscs + [bsm], writes=[bsm, T["B_junk2"]])
        self.A(lambda e: e.activation(out=a, in_=cnt, func=AF.Sign, bias=cb, scale=1.0), reads=[bsm, T["B_cb"]], writes=[bsm])
        if i < NBIS - 1:
            st = W / (2 ** (i + 2))
            self.A(lambda e: e.activation(out=nmid, in_=a, func=AF.Identity, scale=-st, bias=nmid), reads=[bsm], writes=[bsm])


K.dsa_select_act = _dsa_select_act


def _dsa_select_act_fin(self, j, T):
    n = 128 * (j + 1)
    sc, bscs = T["sc"][j % 2]
    nm, bnm = T["nm"][j % 4]
    sm, bsm = T["sm"][j % 2]
    nmid, cnt, a, thr = sm[:, 0:1], sm[:, 1:2], sm[:, 2:3], sm[:, 3:4]
    W = BIS_HI - BIS_LO
    st = W / (2 ** NBIS)
    self.V(lambda e: e.scalar_tensor_tensor(out=thr, in0=a, scalar=st / 2, in1=nmid, op0=ALU.mult, op1=ALU.subtract),
           reads=[bsm], writes=[bsm])
    self.V(lambda e: e.tensor_scalar(out=thr, in0=thr, scalar1=-st / 2, scalar2=None, op0=ALU.add), reads=[bsm], writes=[bsm])
    self.V(lambda e: e.tensor_scalar(out=nm[:, 0:n], in0=sc[:, 0:n], scalar1=thr, scalar2=None, op0=ALU.is_lt),
           reads=bscs + [bsm], writes=[bnm])


K.dsa_select_act_fin = _dsa_select_act_fin


def _dsa_attn_units(self, j, T):
    nm, bnm = T["nm"][j % 4]
    B_qk, B_tm = T["B_qk"], T["B_tm"]
    qaT, kaT, va = T["qaT"], T["kaT"], T["va"]
    ob = [4, 5] if j % 2 == 0 else [6, 7]
    import os
    var = os.environ.get("DSA_VAR", "")
    def zero_mm():
        for half in range(2):
            self.MM(self.ps[ob[half]][:, 0:264], lhsT=self.zerob[:, 0:128], rhs=self.zerob[:, 0:264], start=True, stop=False,
                    reads=[self.B_k], writes=[self.PB[ob[half]]], inc=False, skip=True)
    pc = T["pcount"]
    steps = [(kt, half) for kt in range(j + 1) for half in range(2)]
    slots = []
    for _ in steps:
        slots.append(pc[0])
        pc[0] += 1

    def emitS(i):
        kt, half = steps[i]
        bk = 2 + (slots[i] % 2)
        pT, bpT = T["pT"][slots[i] % 4]
        self.MM(self.ps[bk][:, :], lhsT=nm[:, kt * 128:(kt + 1) * 128], rhs=self.negI4[:, :], start=True, stop=False,
                reads=[bnm, self.B_k], writes=[self.PB[bk]], inc=False, skip=True)
        for hh in range(4):
            h = half * 4 + hh
            self.MM(self.ps[bk][:, hh * 128:(hh + 1) * 128], lhsT=kaT[:, h % 2, kt * 128:(kt + 1) * 128],
                    rhs=qaT[:, h // 2, j * 128:(j + 1) * 128], start=False, stop=True,
                    reads=[B_qk], writes=[self.PB[bk]], inc=(hh == 3), skip=True)
        self.A(lambda e: e.activation(out=pT[:], in_=self.ps[bk][:], func=AF.Exp, scale=0.125),
               reads=[self.PB[bk]], writes=[bpT])

    def emitPV(i):
        kt, half = steps[i]
        pT, bpT = T["pT"][slots[i] % 4]
        for hh in range(4):
            self.MM(self.ps[ob[half]][:, hh * 66:(hh + 1) * 66], lhsT=pT[:, hh * 128:(hh + 1) * 128], rhs=va[:, kt, 0:66],
                    start=False, stop=(kt == j), reads=[bpT, B_tm], writes=[self.PB[ob[half]]], inc=(hh == 3), skip=True)

    units = []

    def first():
        zero_mm()
        emitS(0)
    units.append(first)
    for i in range(len(steps)):
        def unit(i=i):
            if i + 1 < len(steps):
                emitS(i + 1)
            emitPV(i)
        units.append(unit)
    return units


K.dsa_attn_units = _dsa_attn_units


def _dsa_attn(self, j, T):
    for u in self.dsa_attn_units(j, T):
        u()


K.dsa_attn = _dsa_attn


def interleave(la, lb):
    out = []
    na, nb = len(la), len(lb)
    ia = ib = 0
    while ia < na or ib < nb:
        if ib >= nb or (ia < na and ia * nb <= ib * na):
            out.append(la[ia])
            ia += 1
        else:
            out.append(lb[ib])
            ib += 1
    return out


def _dsa_norm(self, j, T, yaT, B_ya):
    ob = [4, 5] if j % 2 == 0 else [6, 7]
    ya, bya = T["ya"][j % 2]
    rd, brd = T["rden"][j % 2]
    for half in range(2):
        o3 = self.ps[ob[half]][:, 0:264].rearrange("p (h e) -> p h e", e=66)
        self.V(lambda e, o3=o3, half=half: e.reciprocal(out=rd[:, half * 4:(half + 1) * 4], in_=o3[:, :, 64]),
               reads=[self.PB[ob[half]]], writes=[brd])
        self.V(lambda e, o3=o3, half=half: e.tensor_tensor(out=ya[:, half * 256:(half + 1) * 256].rearrange("p (h e) -> p h e", e=64),
                                                          in0=o3[:, :, 0:64],
                                                          in1=rd[:, half * 4:(half + 1) * 4].unsqueeze(2).to_broadcast([128, 4, 64]),
                                                          op=ALU.mult),
               reads=[self.PB[ob[half]], brd], writes=[bya])
    tb = self.ps[0][:].bitcast(BF16)
    for g in range(4):
        self.TR(tb[:, g * 128:(g + 1) * 128], ya[:, g * 128:(g + 1) * 128], self.identb[:], reads=[bya, self.B_k],
                writes=[self.PB[0]], inc=(g == 3))
    self.A(lambda e: e.copy(out=yaT[:, :, j * 128:(j + 1) * 128], in_=tb[:, 0:512].rearrange("p (g t) -> p g t", g=4)),
           reads=[self.PB[0]], writes=[B_ya])


K.dsa_norm = _dsa_norm


def _dsa_phase(self, s, xT, B_xT, yaT, B_ya, wif_keep, B_wifk):
    with ExitStack() as c1:
        T = {}
        T["qaT"] = self.sb("qaT", [128, 4, 2048], BF16, c1)
        T["kaT"] = self.sb("kaT", [128, 2, 2048], BF16, c1)
        T["qiT"] = self.sb("qiT", [128, 4, 2048], BF16, c1)
        T["kiT"] = self.sb("kiT", [128, 2, 2048], BF16, c1)
        T["va"] = self.sb("va", [128, NT, 128], BF16, c1)
        T["wif"] = self.sb("wif", [128, NT, 16], F32, c1)
        T["B_qk"], T["B_tm"] = Buf("qk"), Buf("tm")
        with ExitStack() as c2:
            T["cosT"] = self.sb("cosT", [128, 2048], F32, c2)
            T["sinT"] = self.sb("sinT", [128, 2048], F32, c2)
            posi = self.sb("posi", [128, 2048], I32, c2)
            ang = self.sb("ang", [128, 2048], F32, c2)
            tmp = self.sb("rtmp", [128, 2048], F32, c2)
            T["Btab"] = Buf("tab")
            self.rope_tables(s, T["cosT"], T["sinT"], posi, ang, tmp, T["Btab"])
            self.barrier()
            T["rt1"] = [(ang[:, k * 512:(k + 1) * 512], Buf("rt1")) for k in range(2)]
            T["rt2"] = [(tmp[:, k * 512:(k + 1) * 512], Buf("rt2")) for k in range(2)]
            self.dsa_inproj(s, xT, B_xT, T)
            self.V(lambda e: e.tensor_copy(out=wif_keep[:], in_=T["wif"][:, :, 8:16]), reads=[T["B_tm"]], writes=[B_wifk])
            self.dump("qaT", T["qaT"][:, 0, :], [T["B_qk"]])
            self.dump("kaT", T["kaT"][:, 0, :], [T["B_qk"]])
            self.barrier()
        if self.stop_after == "dsa_inproj":
            return
        with ExitStack() as c3:
            T["sc"] = [(self.sb("sc%d" % i, [128, 2048], F32, c3), [Buf("sc%d_%d" % (i, c)) for c in range(4)]) for i in range(2)]
            T["nm"] = [(self.sb("nm%d" % i, [128, 2048], BF16, c3), Buf("nm")) for i in range(4)]
            rlf = self.ws_big[0][0][:].bitcast(F32)
            T["rl"] = [(rlf[:, i * 512:(i + 1) * 512], Buf("rl")) for i in range(4)]
            T["pT"] = [(self.sb("pT%d" % i, [128, 512], BF16, c3), Buf("pT")) for i in range(4)]
            T["ya"] = [(self.sb("ya%d" % i, [128, 512], BF16, c3), Buf("ya")) for i in range(2)]
            T["sm"] = [(self.sb("sm%d" % i, [128, 4], F32, c3), Buf("sm")) for i in range(2)]
            T["rden"] = [(self.sb("rden%d" % i, [128, 8], F32, c3), Buf("rden")) for i in range(2)]
            T["junk"] = self.sb("junk", [128, 2048], BF16, c3)
            T["B_junk"] = Buf("junk")
            T["junk2"] = T["junk"]
            T["B_junk2"] = Buf("junk2")
            T["cb"] = self.sb("cbias", [128, NT], F32, c3)
            T["B_cb"] = Buf("cb")
            for jj in range(NT):
                self.V(lambda e, jj=jj: e.memset(T["cb"][:, jj:jj + 1], float(128 * (jj + 1) - 511)), writes=[T["B_cb"]])
            T["pcount"] = [0]
            T["scnt"] = [0]
            ntile = self.dbg.get("_ntile", NT)
            def sel_pair(k2):
                j1, j2 = 2 * k2, 2 * k2 + 1
                if 128 * (j2 + 1) <= TOPK:
                    self.dsa_select(j1, T)
                    self.dsa_select(j2, T)
                else:
                    self.dsa_select_act(j2, T)
                    self.dsa_select(j1, T)
                    self.dsa_select_act_fin(j2, T)

            npair = ntile // 2
            self.dsa_score(0, T)
            self.dsa_score(1, T)
            sel_pair(0)
            for k2 in range(npair):
                j1, j2 = 2 * k2, 2 * k2 + 1
                ua = self.dsa_attn_units(j1, T) + self.dsa_attn_units(j2, T)
                us = (self.dsa_score_units(j1 + 2, T) + self.dsa_score_units(j2 + 2, T)) if k2 + 1 < npair else []
                for u in interleave(ua, us):
                    u()
                if k2 + 1 < npair:
                    sel_pair(k2 + 1)
                self.dsa_norm(j1, T, yaT, B_ya)
                self.dsa_norm(j2, T, yaT, B_ya)
            self.dump("sc", T["sc"][(ntile - 1) % 2][0][:, :], T["sc"][(ntile - 1) % 2][1])
            self.dump("sm", T["sm"][(ntile - 1) % 2][0][:, :], [T["sm"][(ntile - 1) % 2][1]])
            self.dump("yaT", yaT[:, 0, :], [B_ya])
            self.barrier()


K.dsa_phase = _dsa_phase


def _ml_phase(self, s, xT, B_xT, ybT, B_yb, wifk, B_wifk):
    S = self.S
    vec = self.vec
    with ExitStack() as c1:
        mlqT = self.sb("mlqT", [128, 4, 2048], BF16, c1)
        mlkT = self.sb("mlkT", [128, 4, 2048], BF16, c1)
        ktm = [(self.sb("ktm%d" % i, [128, 128], BF16, c1), Buf("ktm")) for i in range(2)]
        mlv = self.sb("mlv", [128, NT, 4, 130], BF16, c1)
        osig = self.sb("osig", [128, NT, 512], BF16, c1)
        G = self.sb("mlG", [128, NT, 12], F32, c1)
        xc = self.sb("xc", [128, 2052], F32, c1)
        acc = self.sb("cacc", [128, 2048], F32, c1)
        C32 = self.sb("C32", [128, 4, 130], F32, c1)
        Cbf = self.sb("Cbf", [128, 4, 130], BF16, c1)
        hm = self.sb("hm", [128, 512], F32, c1)
        yb = self.sb("ybtm", [128, 512], BF16, c1)
        pTm = [(self.sb("pTm%d" % i, [128, 128], BF16, c1), Buf("pTm")) for i in range(2)]
        sm = self.sb("mlsm", [128, 4, 8], F32, c1)
        g4 = self.sb("mlg4", [128, 8], F32, c1)
        junk = self.sb("mljunk", [128, 128], F32, c1)
        junk2 = self.sb("mljunk2", [128, 128], F32, c1)
        B_q, B_k, B_ktm, B_v, B_o, B_G = Buf("mlq"), Buf("mlk"), Buf("mlktm"), Buf("mlv"), Buf("osig"), Buf("G")
        B_xc, B_acc, B_C32, B_Cbf, B_hm, B_yb_tm, B_sm, B_g4, B_j, B_j2 = (Buf("xc"), Buf("acc"), Buf("C32"), Buf("Cbf"), Buf("hm"),
                                                                         Buf("ybtm"), Buf("mlsm"), Buf("g4"), Buf("j"), Buf("j2"))
        order = [("fm", G_MLQ + g) for g in range(4)] + [("fm", G_MLK + g) for g in range(4)] + [("mlv", 0), ("mlo", 0)]
        self.wstream_begin(order)
        self.V(lambda e: e.memset(xc[:, 0:4], 0.0), writes=[B_xc])
        self.P(lambda e: e.memset(mlv[:], 0.0), writes=[B_v])
        ucount = 0
        for gi in range(8):
            wh = self.wget(gi)
            for half in range(2):
                bset = (ucount % 4) * 2
                ucount += 1
                self.proj_fm_banks(wh, xT, B_xT, half * 1024, [bset, bset + 1])
                for j in range(2):
                    a = 4 + half * 1024 + j * 512
                    self.A(lambda e, a=a, bk=bset + j: e.copy(out=xc[:, a:a + 512], in_=self.ps[bk][:]), reads=[self.PB[bset + j]], writes=[B_xc])
            cw = lambda jj: vec[:, V_CW + gi * 4 + jj:V_CW + gi * 4 + jj + 1]
            self.V(lambda e: e.tensor_scalar(out=acc[:], in0=xc[:, 4:2052], scalar1=cw(3), scalar2=vec[:, V_CB + gi:V_CB + gi + 1],
                                             op0=ALU.mult, op1=ALU.add), reads=[B_xc, self.B_cst], writes=[B_acc])
            self.V(lambda e: e.scalar_tensor_tensor(out=acc[:], in0=xc[:, 3:2051], scalar=cw(2), in1=acc[:], op0=ALU.mult, op1=ALU.add),
                   reads=[B_xc, B_acc, self.B_cst], writes=[B_acc])
            self.V(lambda e: e.scalar_tensor_tensor(out=acc[:], in0=xc[:, 2:2050], scalar=cw(1), in1=acc[:], op0=ALU.mult, op1=ALU.add),
                   reads=[B_xc, B_acc, self.B_cst], writes=[B_acc])
            self.V(lambda e: e.scalar_tensor_tensor(out=acc[:], in0=xc[:, 1:2049], scalar=cw(0), in1=acc[:], op0=ALU.mult, op1=ALU.add),
                   reads=[B_xc, B_acc, self.B_cst], writes=[B_acc])
            dst, bd = (mlqT[:, gi, :], B_q) if gi < 4 else (mlkT[:, gi - 4, :], B_k)
            self.A(lambda e, dst=dst: e.activation(out=dst, in_=acc[:], func=AF.Silu), reads=[B_acc], writes=[bd])
        wv, bwv = self.wget(8)
        wo, bwo = self.wget(9)
        LNS = math.log(128.0 ** -0.5)
        ga = self.sb("mlga", [128, NT, 4], F32, c1)
        gb_ = self.sb("mlgb", [128, NT, 4], F32, c1)
        B_ga, B_gb = Buf("ga"), Buf("gb")
        bfb = vec[:, V_BF:V_BF + 4].unsqueeze(1).to_broadcast([128, NT, 4])
        bib = vec[:, V_BI:V_BI + 4].unsqueeze(1).to_broadcast([128, NT, 4])
        self.V(lambda e: e.tensor_tensor(out=ga[:], in0=wifk[:, :, 4:8], in1=bfb, op=ALU.add), reads=[B_wifk, self.B_cst], writes=[B_ga])
        self.A(lambda e: e.activation(out=ga[:], in_=ga[:], func=AF.Exp, scale=-1.0), reads=[B_ga], writes=[B_ga])
        self.V(lambda e: e.tensor_scalar(out=ga[:], in0=ga[:], scalar1=1.0, scalar2=None, op0=ALU.add), reads=[B_ga], writes=[B_ga])
        self.A(lambda e: e.activation(out=ga[:], in_=ga[:], func=AF.Ln), reads=[B_ga], writes=[B_ga])
        gaf = ga[:].rearrange("p t h -> p (t h)")
        self.MM(self.ps[2][:, 0:64], lhsT=self.triT, rhs=gaf, start=True, stop=True, reads=[B_ga, self.B_cst], writes=[self.PB[2]], inc=True)
        self.MM(self.ps[3][:, 0:64], lhsT=self.ones, rhs=gaf, start=True, stop=True, reads=[B_ga, self.B_cst], writes=[self.PB[3]], inc=True)
        cum3 = self.ps[2][:, 0:64].rearrange("p (t h) -> p t h", h=4)
        tot3 = self.ps[3][:, 0:64].rearrange("p (t h) -> p t h", h=4)
        self.V(lambda e: e.tensor_tensor(out=gb_[:], in0=wifk[:, :, 0:4], in1=bib, op=ALU.add), reads=[B_wifk, self.B_cst], writes=[B_gb])
        self.V(lambda e: e.scalar_tensor_tensor(out=gb_[:], in0=gb_[:], scalar=LNS, in1=cum3, op0=ALU.add, op1=ALU.add),
               reads=[B_gb, self.PB[2]], writes=[B_gb])
        self.A(lambda e: e.activation(out=G[:, :, 0:4], in_=gb_[:], func=AF.Exp), reads=[B_gb], writes=[B_G])
        self.A(lambda e: e.activation(out=G[:, :, 4:8], in_=cum3, func=AF.Exp, scale=-1.0), reads=[self.PB[2]], writes=[B_G])
        self.A(lambda e: e.activation(out=G[:, :, 8:12], in_=tot3, func=AF.Exp, scale=-1.0), reads=[self.PB[3]], writes=[B_G])
        for t in range(NT):
            vb = 4 + (t % 2)
            for kc in range(8):
                self.MM(self.ps[vb][:, :], lhsT=xT[:, kc, t * 128:(t + 1) * 128], rhs=wv[:, kc, :], start=(kc == 0), stop=(kc == 7),
                        reads=[bwv, B_xT], writes=[self.PB[vb]], inc=(kc == 7))
            self.V(lambda e, t=t, vb=vb: e.tensor_tensor(out=mlv[:, t, :, 0:128], in0=self.ps[vb][:].rearrange("p (h d) -> p h d", h=4),
                                                       in1=G[:, t, 0:4].unsqueeze(2).to_broadcast([128, 4, 128]), op=ALU.mult),
                   reads=[self.PB[vb], B_G], writes=[B_v])
            self.V(lambda e, t=t: e.tensor_copy(out=mlv[:, t, :, 128], in_=G[:, t, 0:4]), reads=[B_G], writes=[B_v])
            ob = 6 + (t % 2)
            for kc in range(8):
                self.MM(self.ps[ob][:, :], lhsT=xT[:, kc, t * 128:(t + 1) * 128], rhs=wo[:, kc, :], start=(kc == 0), stop=(kc == 7),
                        reads=[bwo, B_xT], writes=[self.PB[ob]], inc=(kc == 7))
            self.A(lambda e, t=t, ob=ob: e.activation(out=osig[:, t, :], in_=self.ps[ob][:], func=AF.Sigmoid), reads=[self.PB[ob]], writes=[B_o])
        pT4 = [(self.sb("pT4_%d" % i, [128, 4, 128], BF16, c1), Buf("pT4")) for i in range(2)]
        kt4 = [(self.sb("kt4_%d" % i, [128, 4, 128], BF16, c1), Buf("kt4")) for i in range(2)]
        st8 = self.sb("mlst8", [128, 10, 4], F32, c1)
        B_st = Buf("st8")
        B_s1, B_s2 = Buf("s1"), Buf("s2")
        B_hmh = [Buf("hm%d" % h) for h in range(4)]
        sD, sND, sR, sF, sS1, sS2, sM, sV, sRS, sT = [st8[:, i, :] for i in range(10)]
        for t in range(NT):
            tl = slice(t * 128, (t + 1) * 128)
            par = t % 2
            pT, bpT = pT4[par]
            kt_, bkt = kt4[par]
            bS = par
            bT = 6 + par
            tbk = self.ps[bT][:].bitcast(BF16)
            last = (t == NT - 1)
            for h in range(4):
                self.MM(self.ps[bS][:, h * 128:(h + 1) * 128], lhsT=mlkT[:, h, tl], rhs=mlqT[:, h, tl], start=True, stop=True,
                        reads=[B_k, B_q], writes=[self.PB[bS]], inc=(h == 3))
            if not last:
                for h in range(4):
                    self.TR(tbk[:, h * 128:(h + 1) * 128], mlkT[:, h, tl], self.identb[:], reads=[B_k, self.B_k], writes=[self.PB[bT]], inc=(h == 3))
            self.V(lambda e: e.tensor_tensor(out=pT[:], in0=self.ps[bS][:].rearrange("p (h s) -> p h s", h=4),
                                             in1=self.triT.unsqueeze(1).to_broadcast([128, 4, 128]), op=ALU.mult),
                   reads=[self.PB[bS], self.B_cst], writes=[bpT])
            if not last:
                self.A(lambda e: e.copy(out=kt_[:], in_=tbk[:, 0:512].rearrange("p (h s) -> p h s", h=4)), reads=[self.PB[bT]], writes=[bkt])
            for h in range(4):
                bn = 2 + h // 2
                c0 = (h % 2) * 130
                self.MM(self.ps[bn][:, c0:c0 + 130], lhsT=pT[:, h, :], rhs=mlv[:, t, h, :], start=True, stop=(t == 0), reads=[bpT, B_v],
                        writes=[self.PB[bn]], inc=(t == 0 and h % 2 == 1), skip=True)
                if t > 0:
                    self.MM(self.ps[bn][:, c0:c0 + 130], lhsT=mlqT[:, h, tl], rhs=Cbf[:, h, :], start=False, stop=True, reads=[B_q, B_Cbf],
                            writes=[self.PB[bn]], inc=(h % 2 == 1), skip=True)
            if not last:
                for h in range(4):
                    bc = 4 + h // 2
                    c0 = (h % 2) * 130
                    self.MM(self.ps[bc][:, c0:c0 + 130], lhsT=kt_[:, h, :], rhs=mlv[:, t, h, :], start=True, stop=True, reads=[bkt, B_v],
                            writes=[self.PB[bc]], inc=(h % 2 == 1), skip=True)
                for b2 in range(2):
                    cv = C32[:, 2 * b2:2 * b2 + 2, :].rearrange("p h e -> p (h e)")
                    if t == 0:
                        self.V(lambda e, b2=b2, cv=cv: e.tensor_copy(out=cv, in_=self.ps[4 + b2][:, 0:260]), reads=[self.PB[4 + b2]], writes=[B_C32])
                    else:
                        self.V(lambda e, b2=b2, cv=cv: e.tensor_tensor(out=cv, in0=self.ps[4 + b2][:, 0:260], in1=cv, op=ALU.add),
                               reads=[self.PB[4 + b2], B_C32], writes=[B_C32])
                self.V(lambda e, t=t: e.tensor_tensor(out=C32[:], in0=C32[:], in1=G[:, t, 8:12].unsqueeze(2).to_broadcast([128, 4, 130]), op=ALU.mult),
                       reads=[B_C32, B_G], writes=[B_C32])
                self.A(lambda e: e.copy(out=Cbf[:], in_=C32[:]), reads=[B_C32], writes=[B_Cbf])
            eb4 = G[:, t, 4:8]
            for b2 in range(2):
                den2 = self.ps[2 + b2][:, 0:260].rearrange("p (h e) -> p h e", e=130)[:, :, 128]
                self.V(lambda e, b2=b2, den2=den2: e.tensor_tensor(out=sD[:, 2 * b2:2 * b2 + 2], in0=den2, in1=eb4[:, 2 * b2:2 * b2 + 2], op=ALU.mult),
                       reads=[self.PB[2 + b2], B_G], writes=[B_st])
            self.V(lambda e: e.tensor_scalar(out=sND, in0=sD, scalar1=-1.0, scalar2=None, op0=ALU.mult), reads=[B_st], writes=[B_st])
            self.V(lambda e: e.tensor_tensor(out=sD, in0=sD, in1=sND, op=ALU.max), reads=[B_st], writes=[B_st])
            self.V(lambda e: e.tensor_scalar(out=sD, in0=sD, scalar1=1.0, scalar2=None, op0=ALU.max), reads=[B_st], writes=[B_st])
            self.V(lambda e: e.reciprocal(out=sR, in_=sD), reads=[B_st], writes=[B_st])
            self.V(lambda e: e.tensor_tensor(out=sF, in0=sR, in1=eb4, op=ALU.mult), reads=[B_st, B_G], writes=[B_st])
            for h in range(4):
                bn = 2 + h // 2
                c0 = (h % 2) * 130
                hs = hm[:, h * 128:(h + 1) * 128]
                self.V(lambda e, bn=bn, c0=c0, hs=hs, h=h: e.tensor_scalar(out=hs, in0=self.ps[bn][:, c0:c0 + 128], scalar1=sF[:, h:h + 1], scalar2=None,
                                                                        op0=ALU.mult, op1=ALU.add, accum_out=sS1[:, h:h + 1]),
                       reads=[self.PB[bn], B_st], writes=[B_hmh[h], B_s1])
                self.A(lambda e, hs=hs, h=h: e.activation(out=junk2[:], in_=hs, func=AF.Square, accum_out=sS2[:, h:h + 1]), reads=[B_hmh[h]], writes=[B_j2, B_s2])
            self.V(lambda e: e.tensor_scalar(out=sM, in0=sS1, scalar1=1.0 / 128, scalar2=None, op0=ALU.mult), reads=[B_st, B_s1], writes=[B_st])
            self.V(lambda e: e.tensor_tensor(out=sT, in0=sM, in1=sM, op=ALU.mult), reads=[B_st], writes=[B_st])
            self.V(lambda e: e.scalar_tensor_tensor(out=sV, in0=sS2, scalar=1.0 / 128, in1=sT, op0=ALU.mult, op1=ALU.subtract), reads=[B_st, B_s2], writes=[B_st])
            self.V(lambda e: e.tensor_scalar(out=sV, in0=sV, scalar1=LN_EPS, scalar2=None, op0=ALU.add), reads=[B_st], writes=[B_st])
            self.A(lambda e: e.activation(out=sRS, in_=sV, func=AF.Sqrt), reads=[B_st], writes=[B_st])
            self.V(lambda e: e.reciprocal(out=sRS, in_=sRS), reads=[B_st], writes=[B_st])
            hm3 = hm[:].rearrange("p (h d) -> p h d", h=4)
            self.V(lambda e: e.tensor_tensor(out=hm3, in0=hm3, in1=sM.unsqueeze(2).to_broadcast([128, 4, 128]), op=ALU.subtract), reads=B_hmh + [B_st], writes=B_hmh)
            self.V(lambda e: e.tensor_tensor(out=hm3, in0=hm3, in1=sRS.unsqueeze(2).to_broadcast([128, 4, 128]), op=ALU.mult), reads=B_hmh + [B_st], writes=B_hmh)
            self.V(lambda e: e.tensor_tensor(out=hm[:], in0=hm[:], in1=vec[:, V_MLG:V_MLG + 512], op=ALU.mult), reads=B_hmh + [self.B_cst], writes=B_hmh)
            self.V(lambda e, t=t: e.tensor_tensor(out=yb[:], in0=hm[:], in1=osig[:, t, :], op=ALU.mult), reads=B_hmh + [B_o], writes=[B_yb_tm])
            tb = self.ps[bT][:].bitcast(BF16)
            for g in range(4):
                self.TR(tb[:, g * 128:(g + 1) * 128], yb[:, g * 128:(g + 1) * 128], self.identb[:], reads=[B_yb_tm, self.B_k],
                        writes=[self.PB[bT]], inc=(g == 3))
            self.A(lambda e, t=t, tb=tb: e.copy(out=ybT[:, :, t * 128:(t + 1) * 128], in_=tb[:, 0:512].rearrange("p (g t) -> p g t", g=4)),
                   reads=[self.PB[bT]], writes=[B_yb])
        self.dump("ybT", ybT[:, 0, :], [B_yb])
        self.barrier()


K.ml_phase = _ml_phase


def _post_phase(self, s, xT, B_xT, yaT, B_ya, ybT, B_yb):
    S = self.S
    vec = self.vec
    with ExitStack() as c1:
        Rs = [self.sb("R%d" % i, [128, 8, 512], F32, c1) for i in range(2)]
        B_Rs = [[Buf("R%d_%d" % (i, c)) for c in range(8)] for i in range(2)]
        Rb = self.sb("Rb", [128, 8, 512], BF16, c1)
        mg = self.sb("mergedT", [128, 8, 512], BF16, c1)
        uT = self.sb("uT", [128, 32, 512], BF16, c1)
        pTb = self.sb("pTb", [128, 2, 512], BF16, c1)
        tmpA = [(self.sb("tmpA%d" % i, [128, 512], F32, c1), Buf("tmpA")) for i in range(4)]
        st = self.sb("lnst", [128, 4, 512], F32, c1)
        onesd = self.sb("onesd", [128, 128], F32, c1)
        pl = [(self.sb("pl%d" % i, [128, 256], F32, c1), Buf("pl%d" % i)) for i in range(4)]
        for t_, b_ in pl:
            if b_.name not in S.dsem:
                S.new_dma_sem(b_.name)
        B_Rb, B_mg, B_uT, B_pT, B_st, B_od = Buf("Rb"), Buf("mg"), Buf("uT"), Buf("pTb"), Buf("lnst"), Buf("onesd")
        self.V(lambda e: e.tensor_scalar(out=onesd[:], in0=self.ones, scalar1=1.0 / 1024, scalar2=None, op0=ALU.mult), reads=[self.B_cst], writes=[B_od])
        bank_i = [0]
        tmp_i = [0]

        def nb():
            b = bank_i[0] % 8
            bank_i[0] += 1
            return b

        def ntmp():
            t = tmpA[tmp_i[0] % 4]
            tmp_i[0] += 1
            return t

        def mmacc(bk, pairs, reads):
            n = len(pairs)
            for i, (l, r) in enumerate(pairs):
                self.MM(self.ps[bk][:, :], lhsT=l, rhs=r, start=(i == 0), stop=(i == n - 1), reads=reads, writes=[self.PB[bk]], inc=(i == n - 1))

        def merge_order():
            o = []
            for fc in range(8):
                o += [("gate", fc), ("upa", fc), ("gate", 8 + fc), ("upb", fc)]
            return o
        order = merge_order() + [("wout", g) for g in range(8)]
        for blk in range(4):
            if blk + 1 < 4:
                order += merge_order()
            order += [("ff1", g) for g in range(32)] + [("ff2", g) for g in range(8)]
            for fc in range(8):
                order += [("pleg", fc), ("plep", fc)]
            if blk + 1 < 4:
                order += [("wout", g) for g in range(8)]
        self.wstream_begin(order)
        wi = [0]

        def nextw():
            h = self.wget(wi[0])
            wi[0] += 1
            return h

        def layernorm(R, B_R, gcol, bcol, write_rb):
            b1, b2 = nb(), nb()
            mmacc(b1, [(onesd[:], R[:, c, :]) for c in range(8)], [B_od] + B_R)
            for c in range(8):
                sq, bsq = ntmp()
                self.A(lambda e: e.activation(out=sq[:], in_=R[:, c, :], func=AF.Square), reads=[B_R[c]], writes=[bsq])
                self.MM(self.ps[b2][:, :], lhsT=onesd[:], rhs=sq[:], start=(c == 0), stop=(c == 7), reads=[B_od, bsq], writes=[self.PB[b2]], inc=True)
            mean, rstd, nmr, t4 = st[:, 0, :], st[:, 1, :], st[:, 2, :], st[:, 3, :]
            self.V(lambda e: e.tensor_copy(out=mean, in_=self.ps[b1][:]), reads=[self.PB[b1]], writes=[B_st])
            self.V(lambda e: e.tensor_tensor(out=t4, in0=mean, in1=mean, op=ALU.mult), reads=[B_st], writes=[B_st])
            self.V(lambda e: e.tensor_tensor(out=t4, in0=self.ps[b2][:], in1=t4, op=ALU.subtract), reads=[B_st, self.PB[b2]], writes=[B_st])
            self.V(lambda e: e.tensor_scalar(out=t4, in0=t4, scalar1=LN_EPS, scalar2=None, op0=ALU.add), reads=[B_st], writes=[B_st])
            self.A(lambda e: e.activation(out=t4, in_=t4, func=AF.Sqrt), reads=[B_st], writes=[B_st])
            self.V(lambda e: e.reciprocal(out=rstd, in_=t4), reads=[B_st], writes=[B_st])
            self.V(lambda e: e.scalar_tensor_tensor(out=nmr, in0=mean, scalar=-1.0, in1=rstd, op0=ALU.mult, op1=ALU.mult), reads=[B_st], writes=[B_st])
            for c in range(8):
                self.V(lambda e: e.tensor_tensor(out=R[:, c, :], in0=R[:, c, :], in1=rstd, op=ALU.mult), reads=[B_R[c], B_st], writes=[B_R[c]])
                self.V(lambda e: e.tensor_tensor(out=R[:, c, :], in0=R[:, c, :], in1=nmr, op=ALU.add), reads=[B_R[c], B_st], writes=[B_R[c]])
                self.A(lambda e: e.activation(out=R[:, c, :], in_=R[:, c, :], func=AF.Identity, scale=vec[:, gcol + c:gcol + c + 1],
                                              bias=vec[:, bcol + c:bcol + c + 1]), reads=[B_R[c], self.B_cst], writes=[B_R[c]])
                if write_rb:
                    self.P(lambda e: e.tensor_copy(out=Rb[:, c, :], in_=R[:, c, :]), reads=[B_R[c]], writes=[B_Rb])

        def do_xT(blk):
            R, B_R = Rs[blk % 2], B_Rs[blk % 2]
            tok0 = blk * 512
            for tt in range(4):
                xs, bx = self.xs[self.xs_i % 2]
                self.xs_i += 1
                S.dma("sp", bx.name, xs[:], self.x_d[s, tok0 + tt * 128: tok0 + (tt + 1) * 128, :], writes=[bx])
                b1, b2 = nb(), nb()
                for c in range(8):
                    bk = b1 if c < 4 else b2
                    self.TR(self.ps[bk][:, (c % 4) * 128:(c % 4 + 1) * 128], xs[:, c * 128:(c + 1) * 128], self.ident,
                            reads=[bx, self.B_cst], writes=[self.PB[bk]], inc=(c % 4 == 3))
                self.V(lambda e: e.tensor_scalar(out=R[:, 0:4, tt * 128:(tt + 1) * 128], in0=self.ps[b1][:].rearrange("p (a b) -> p a b", a=4),
                                                 scalar1=ALPHA, scalar2=None, op0=ALU.mult), reads=[self.PB[b1]], writes=B_R[0:4])
                self.A(lambda e: e.activation(out=R[:, 4:8, tt * 128:(tt + 1) * 128], in_=self.ps[b2][:].rearrange("p (a b) -> p a b", a=4),
                                              func=AF.Copy, scale=ALPHA), reads=[self.PB[b2]], writes=B_R[4:8])

        def do_merge(blk):
            tsl = slice(blk * 512, blk * 512 + 512)
            for fc in range(8):
                ms = []
                for (src, B_src, nk) in ((yaT, B_ya, 4), (ybT, B_yb, 4)):
                    wg, bwg = nextw()
                    wu, bwu = nextw()
                    bg, bu = nb(), nb()
                    mmacc(bg, [(wg[:, kc, :], xT[:, kc, tsl]) for kc in range(8)], [bwg, B_xT])
                    mmacc(bu, [(wu[:, kc, :], src[:, kc, tsl]) for kc in range(nk)], [bwu, B_src])
                    sg, bsg = ntmp()
                    self.A(lambda e: e.activation(out=sg[:], in_=self.ps[bg][:], func=AF.Sigmoid), reads=[self.PB[bg]], writes=[bsg])
                    self.V(lambda e: e.tensor_tensor(out=sg[:], in0=sg[:], in1=self.ps[bu][:], op=ALU.mult), reads=[bsg, self.PB[bu]], writes=[bsg])
                    ms.append((sg, bsg))
                self.P(lambda e: e.tensor_tensor(out=mg[:, fc, :], in0=ms[0][0][:], in1=ms[1][0][:], op=ALU.add),
                       reads=[ms[0][1], ms[1][1]], writes=[B_mg])

        def do_wout(blk):
            R, B_R = Rs[blk % 2], B_Rs[blk % 2]
            for fc in range(8):
                ww, bww = nextw()
                bk = nb()
                mmacc(bk, [(ww[:, kc, :], mg[:, kc, :]) for kc in range(8)], [bww, B_mg])
                self.V(lambda e: e.tensor_tensor(out=R[:, fc, :], in0=self.ps[bk][:], in1=R[:, fc, :], op=ALU.add), reads=[self.PB[bk], B_R[fc]], writes=[B_R[fc]])

        def do_ffn(blk):
            R, B_R = Rs[blk % 2], B_Rs[blk % 2]
            for f in range(32):
                w1, bw1 = nextw()
                bk = nb()
                mmacc(bk, [(w1[:, kc, :], Rb[:, kc, :]) for kc in range(8)], [bw1, B_Rb])
                rl, brl = ntmp()
                self.A(lambda e: e.activation(out=rl[:], in_=self.ps[bk][:], func=AF.Relu), reads=[self.PB[bk]], writes=[brl])
                self.P(lambda e: e.tensor_tensor(out=uT[:, f, :], in0=rl[:], in1=rl[:], op=ALU.mult), reads=[brl], writes=[B_uT])
            for fc in range(8):
                w2, bw2 = nextw()
                bk = nb()
                mmacc(bk, [(w2[:, f, :], uT[:, f, :]) for f in range(32)], [bw2, B_uT])
                self.V(lambda e: e.scalar_tensor_tensor(out=R[:, fc, :], in0=R[:, fc, :], scalar=ALPHA, in1=self.ps[bk][:], op0=ALU.mult, op1=ALU.add),
                       reads=[B_R[fc], self.PB[bk]], writes=[B_R[fc]])
                self.P(lambda e: e.tensor_copy(out=Rb[:, fc, :], in_=R[:, fc, :]), reads=[B_R[fc]], writes=[B_Rb])

        def do_ple(blk):
            R, B_R = Rs[blk % 2], B_Rs[blk % 2]
            tok0 = blk * 512
            pb = [nb(), nb()]
            for tt in range(4):
                pt, bp = pl[tt]
                S.dma("sp", bp.name, pt[:], self.p_d[s, tok0 + tt * 128: tok0 + (tt + 1) * 128, :], writes=[bp])
                for c in range(2):
                    self.TR(self.ps[pb[c]][:, tt * 128:(tt + 1) * 128], pt[:, c * 128:(c + 1) * 128], self.ident, reads=[bp, self.B_cst],
                            writes=[self.PB[pb[c]]], inc=True)
            for c in range(2):
                self.V(lambda e: e.tensor_copy(out=pTb[:, c, :], in_=self.ps[pb[c]][:]), reads=[self.PB[pb[c]]], writes=[B_pT])
            for fc in range(8):
                wg, bwg = nextw()
                wp, bwp = nextw()
                bg, bp2 = nb(), nb()
                mmacc(bg, [(wg[:, kc, :], Rb[:, kc, :]) for kc in range(8)], [bwg, B_Rb])
                mmacc(bp2, [(wp[:, kc, :], pTb[:, kc, :]) for kc in range(2)], [bwp, B_pT])
                sg, bsg = ntmp()
                self.A(lambda e: e.activation(out=sg[:], in_=self.ps[bg][:], func=AF.Sigmoid), reads=[self.PB[bg]], writes=[bsg])
                self.V(lambda e: e.tensor_tensor(out=sg[:], in0=sg[:], in1=self.ps[bp2][:], op=ALU.mult), reads=[bsg, self.PB[bp2]], writes=[bsg])
                self.P(lambda e: e.tensor_tensor(out=R[:, fc, :], in0=R[:, fc, :], in1=sg[:], op=ALU.add), reads=[B_R[fc], bsg], writes=[B_R[fc]])

        def do_out(blk):
            R, B_R = Rs[blk % 2], B_Rs[blk % 2]
            tok0 = blk * 512
            for tt in range(4):
                o, bo = self.xs[self.xs_i % 2]
                self.xs_i += 1
                b1, b2 = nb(), nb()
                for c in range(8):
                    bk = b1 if c < 4 else b2
                    self.TR(self.ps[bk][:, (c % 4) * 128:(c % 4 + 1) * 128], R[:, c, tt * 128:(tt + 1) * 128], self.ident,
                            reads=[B_R[c], self.B_cst], writes=[self.PB[bk]], inc=(c % 4 == 3))
                self.V(lambda e: e.tensor_copy(out=o[:, 0:512], in_=self.ps[b1][:]), reads=[self.PB[b1]], writes=[bo])
                self.A(lambda e: e.copy(out=o[:, 512:1024], in_=self.ps[b2][:]), reads=[self.PB[b2]], writes=[bo])
                S.dma("pool", bo.name, self.out_d[s, tok0 + tt * 128: tok0 + (tt + 1) * 128, :], o[:], reads=[bo])

        do_xT(0)
        do_merge(0)
        do_wout(0)
        for blk in range(4):
            R, B_R = Rs[blk % 2], B_Rs[blk % 2]
            layernorm(R, B_R, V_L1G, V_L1B, True)
            if blk == 0:
                self.dump("h1T", R[:, 0, :], [B_R[0]])
            if blk + 1 < 4:
                do_merge(blk + 1)
            do_ffn(blk)
            do_ple(blk)
            layernorm(R, B_R, V_L2G, V_L2B, False)
            if blk + 1 < 4:
                do_xT(blk + 1)
                do_wout(blk + 1)
            do_out(blk)
        self.barrier()


K.post_phase = _post_phase


def build_program(nseq=NSEQ, dbg=None):
    k = K(nseq=nseq, dbg=dbg)
    k.setup()
    for s in range(nseq):
        with ExitStack() as c0:
            xT = k.sb("xT", [128, 8, 2048], BF16, c0)
            yaT = k.sb("yaT", [128, 4, 2048], BF16, c0)
            ybT = k.sb("ybT", [128, 4, 2048], BF16, c0)
            wifk = k.sb("wifk", [128, NT, 8], F32, c0)
            B_xT, B_ya, B_yb, B_w = Buf("xT"), Buf("yaT"), Buf("ybT"), Buf("wifk")
            k.phaseA(s, xT, B_xT)
            if s == 0:
                k.cast_rest([b for _, b in k.xs])
            k.dsa_phase(s, xT, B_xT, yaT, B_ya, wifk, B_w)
            k.ml_phase(s, xT, B_xT, ybT, B_yb, wifk, B_w)
            k.post_phase(s, xT, B_xT, yaT, B_ya, ybT, B_yb)
            k.barrier()
    return k.finish(), k


_CACHE = {}


def kernel(x, p, positions, w_in, conv_w, conv_b, b_igate, b_fgate, ml_norm_g, w_up_a, w_up_b, w_out,
           ln1_g, ln1_b, w_ff1, w_ff2, w_ple_gate, w_ple_proj, ln2_g, ln2_b):
    f = lambda a: np.asarray(a, dtype=np.float32)
    wall = host_weights(f(w_in)[0], f(w_up_a)[0], f(w_up_b)[0], f(w_out)[0], f(w_ff1)[0], f(w_ff2)[0], f(w_ple_gate)[0], f(w_ple_proj)[0])
    cst, vec = host_consts(f(conv_w)[0], f(conv_b)[0], f(b_igate)[0], f(b_fgate)[0], f(ml_norm_g)[0], f(ln1_g)[0], f(ln1_b)[0],
                           f(ln2_g)[0], f(ln2_b)[0])
    x = f(x)
    p = f(p)[0]
    pos = np.asarray(positions, dtype=np.int32)
    if "nc" not in _CACHE:
        _CACHE["nc"] = build_program(NSEQ)[0]
    nc = _CACHE["nc"]
    in_maps = []
    for c in range(NCORES):
        sl = slice(c * NSEQ, (c + 1) * NSEQ)
        in_maps.append(dict(x=np.ascontiguousarray(x[sl]), p=np.ascontiguousarray(p[sl]), pos=np.ascontiguousarray(pos[sl]),
                            wall=wall, cst=cst, vec=vec))
    res = run_bass_kernel_spmd(nc, in_maps, core_ids=list(range(NCORES)))
    out = np.concatenate([np.asarray(r["out"], dtype=np.float32) for r in res.results], axis=0)
    return out
```

```python
import math
from contextlib import ExitStack

import numpy as np
import concourse.bass as bass
import concourse.mybir as mybir
from concourse.bass_utils import run_bass_kernel_spmd

F32 = mybir.dt.float32
BF16 = mybir.dt.bfloat16
I32 = mybir.dt.int32
ALU = mybir.AluOpType
AF = mybir.ActivationFunctionType
AX = mybir.AxisListType

NCORES = 8
D = 1024
S = 2048
NT = S // 128
NSEQ = 2
DFF = 4096
PLE = 256
TOPK = 256
ALPHA = 2.0 ** 0.25
LN_EPS = 1e-5
BIG = 30000.0
NEG = -1.0e30
PI = math.pi
NBIS = 22
BIS_LO = -256.0 - 2.0 ** -14
BIS_HI = BIS_LO + 512.0

OFF = {}
_o = 0
for _n, _w in (('att_q', 512), ('att_k', 64), ('att_v', 64), ('idx_q', 512), ('idx_k', 64), ('idx_w', 8),
               ('ml_q', 512), ('ml_k', 512), ('ml_v', 512), ('ml_i', 4), ('ml_f', 4), ('ml_o', 512),
               ('gate_a', 1024), ('gate_b', 1024)):
    OFF[_n] = (_o, _w)
    _o += _w

G_QA, G_QAP, G_KA, G_KAP, G_QI, G_QIP, G_KI, G_KIP, G_MLQ, G_MLK = 0, 4, 8, 9, 10, 14, 18, 19, 20, 24
NG_FM = 28
TM_COLS = 80 + 512 + 512


class Buf:
    __slots__ = ("name", "w", "r")

    def __init__(self, name=""):
        self.name = name
        self.w = {}
        self.r = {}


class Sched:
    def __init__(self, nc, ctx):
        self.nc = nc
        self.ctx = ctx
        self.engs = {"pe": nc.tensor, "dve": nc.vector, "act": nc.scalar, "pool": nc.gpsimd, "sp": nc.sync}
        self.sem = {k: ctx.enter_context(nc.semaphore("s_" + k)) for k in ("pe", "dve", "act", "pool")}
        self.cnt = {k: 0 for k in self.sem}
        self.seen = {k: {} for k in self.engs}
        self.pending = {k: [] for k in self.sem}
        self.dsem = {}
        self.nops = {k: 0 for k in self.engs}

    def _wait(self, eng, deps):
        e = self.engs[eng]
        for key, (sem, val) in deps.items():
            if self.seen[eng].get(key, 0) >= val:
                continue
            e.wait_ge(sem, val)
            self.seen[eng][key] = val

    def _deps(self, eng, reads, writes):
        deps = {}

        def need(key, tok):
            if key not in deps or deps[key][1] < tok[1]:
                deps[key] = tok

        for b in reads:
            for key, tok in b.w.items():
                if key == eng and eng == "pe":
                    continue
                need(key, tok)
        for b in writes:
            for key, tok in b.w.items():
                if key == eng:
                    continue
                need(key, tok)
            for key, tok in b.r.items():
                if key == eng:
                    continue
                need(key, tok)
        return deps

    def op(self, eng, fn, reads=(), writes=(), inc=True):
        deps = self._deps(eng, reads, writes)
        self._wait(eng, deps)
        ins = fn(self.engs[eng])
        self.nops[eng] += 1
        self.pending[eng].append((tuple(reads), tuple(writes)))
        if inc:
            self.cnt[eng] += 1
            ins.then_inc(self.sem[eng], 1)
            tok = (self.sem[eng], self.cnt[eng])
            for rd, wr in self.pending[eng]:
                for b in rd:
                    b.r[eng] = tok
                for b in wr:
                    b.w = {eng: tok}
                    b.r = {}
            self.pending[eng] = []
        return ins

    def new_dma_sem(self, name):
        sem = self.ctx.enter_context(self.nc.semaphore("d_" + name))
        self.dsem[name] = [sem, 0]
        return name

    def dma(self, queue, semname, out, in_, reads=(), writes=()):
        deps = self._deps("dma:" + semname, reads, writes)
        self._wait(queue, deps)
        st = self.dsem[semname]
        st[1] += 16
        self.engs[queue].dma_start(out=out, in_=in_).then_inc(st[0], 16)
        self.nops[queue] += 1
        tok = (st[0], st[1])
        key = "dma:" + semname
        for b in reads:
            b.r[key] = tok
        for b in writes:
            b.w = {key: tok}
            b.r = {}

    def wait_all(self, eng, bufs):
        deps = {}
        for b in bufs:
            for key, tok in list(b.w.items()):
                if key not in deps or deps[key][1] < tok[1]:
                    deps[key] = tok
        self._wait(eng, deps)


def _bf16_round(a):
    return a


WFAM = [
    ("fm", NG_FM, 8, 128), ("tms", 1, 8, 80), ("mlv", 1, 8, 512), ("mlo", 1, 8, 512),
    ("gate", 16, 8, 128), ("upa", 8, 4, 128), ("upb", 8, 4, 128), ("wout", 8, 8, 128),
    ("ff1", 32, 8, 128), ("ff2", 8, 32, 128), ("pleg", 8, 8, 128), ("plep", 8, 2, 128),
]
WINFO = {}
_off = 0
for _n, _g, _kc, _c in WFAM:
    WINFO[_n] = (_off, _g, _kc, _c)
    _off += _g * _kc * _c
WTOT = _off

C_ID, C_TRI, C_CM, C_ONE, C_NEGI = 0, 128, 256, 384, 512
CST = 640
V_CW, V_CB, V_L1G, V_L1B, V_L2G, V_L2B, V_INVF, V_SGN, V_BI, V_BF, V_MLG = 0, 32, 40, 48, 56, 64, 72, 73, 74, 78, 82
VEC = 82 + 512


def _group_layout(wmat, kc, cols):
    return np.ascontiguousarray(wmat.reshape(kc, 128, cols).transpose(1, 0, 2).reshape(128, kc * cols))


def host_weights(w_in, w_up_a, w_up_b, w_out, w_ff1, w_ff2, w_ple_gate, w_ple_proj):
    wall = np.zeros((128, WTOT), np.float32)

    def put(fam, g, wmat):
        off, ng, kc, cols = WINFO[fam]
        o = off + g * kc * cols
        wall[:, o:o + kc * cols] = _group_layout(np.ascontiguousarray(wmat), kc, cols)

    def cols_of(name):
        o, w = OFF[name]
        return w_in[:, o:o + w]

    def perm64(m):
        K, N = m.shape
        mm = m.reshape(K, N // 64, 2, 32)
        return mm[:, :, ::-1, :].reshape(K, N)

    qa, ka, qi, ki = cols_of('att_q'), cols_of('att_k'), cols_of('idx_q'), cols_of('idx_k')
    qap, kap, qip, kip = perm64(qa), perm64(ka), perm64(qi), perm64(ki)
    for g in range(4):
        put("fm", G_QA + g, qa[:, g * 128:(g + 1) * 128])
        put("fm", G_QAP + g, qap[:, g * 128:(g + 1) * 128])
        put("fm", G_QI + g, qi[:, g * 128:(g + 1) * 128])
        put("fm", G_QIP + g, qip[:, g * 128:(g + 1) * 128])
        put("fm", G_MLQ + g, cols_of('ml_q')[:, g * 128:(g + 1) * 128])
        put("fm", G_MLK + g, cols_of('ml_k')[:, g * 128:(g + 1) * 128])
    put("fm", G_KA, np.concatenate([ka, ka], axis=1))
    put("fm", G_KAP, np.concatenate([kap, kap], axis=1))
    put("fm", G_KI, np.concatenate([ki, ki], axis=1))
    put("fm", G_KIP, np.concatenate([kip, kip], axis=1))
    put("tms", 0, np.concatenate([cols_of('att_v'), cols_of('idx_w'), cols_of('ml_i'), cols_of('ml_f')], axis=1))
    put("mlv", 0, cols_of('ml_v'))
    put("mlo", 0, cols_of('ml_o'))
    ga, gb = cols_of('gate_a'), cols_of('gate_b')
    for g in range(8):
        put("gate", g, ga[:, g * 128:(g + 1) * 128])
        put("gate", 8 + g, gb[:, g * 128:(g + 1) * 128])
        put("upa", g, w_up_a[:, g * 128:(g + 1) * 128])
        put("upb", g, w_up_b[:, g * 128:(g + 1) * 128])
        put("wout", g, w_out[:, g * 128:(g + 1) * 128])
        put("ff2", g, w_ff2[:, g * 128:(g + 1) * 128])
        put("pleg", g, w_ple_gate[:, g * 128:(g + 1) * 128])
        put("plep", g, w_ple_proj[:, g * 128:(g + 1) * 128])
    for g in range(32):
        put("ff1", g, w_ff1[:, g * 128:(g + 1) * 128])
    return wall


def host_consts(conv_w, conv_b, b_igate, b_fgate, ml_norm_g, ln1_g, ln1_b, ln2_g, ln2_b):
    cst = np.zeros((128, CST), np.float32)
    cst[:, C_ID:C_ID + 128] = np.eye(128, dtype=np.float32)
    r = np.arange(128)
    cst[:, C_TRI:C_TRI + 128] = (r[:, None] <= r[None, :]).astype(np.float32)
    cst[:, C_CM:C_CM + 128] = np.where(r[None, :] <= r[:, None], 0.0, NEG)
    cst[:, C_ONE:C_ONE + 128] = 1.0
    cst[:, C_NEGI:C_NEGI + 128] = -BIG * np.eye(128, dtype=np.float32)
    vec = np.zeros((128, VEC), np.float32)
    vec[:, V_CW:V_CW + 32] = conv_w.reshape(4, 8, 128).transpose(2, 1, 0).reshape(128, 32)
    vec[:, V_CB:V_CB + 8] = conv_b.reshape(8, 128).T
    for o, v in ((V_L1G, ln1_g), (V_L1B, ln1_b), (V_L2G, ln2_g), (V_L2B, ln2_b)):
        vec[:, o:o + 8] = v.reshape(8, 128).T
    invf = 1.0 / (np.float32(10000.0) ** (np.arange(0, 64, 2, dtype=np.float32) / np.float32(64)))
    p = np.arange(128)
    vec[:, V_INVF] = invf.astype(np.float32)[p % 32]
    vec[:, V_SGN] = np.where((p % 64) < 32, -1.0, 1.0)
    vec[:, V_BI:V_BI + 4] = b_igate.reshape(1, 4)
    vec[:, V_BF:V_BF + 4] = b_fgate.reshape(1, 4)
    vec[:, V_MLG:V_MLG + 512] = ml_norm_g.reshape(1, 512)
    return cst, vec


class K:
    def __init__(self, nseq=NSEQ, stop_after=None, dbg=None):
        self.nseq = nseq
        self.stop_after = stop_after
        self.dbg = dbg or {}
        self.nc = nc = bass.Bass("TRN2", target_bir_lowering=False)
        self.ctx = ExitStack()

    def sb(self, name, shape, dt, ctx=None):
        self._sbn = getattr(self, "_sbn", 0) + 1
        return (ctx or self.ctx).enter_context(self.nc.sbuf_tensor("sb%d_%s" % (self._sbn, name), shape, dt))

    def dram(self, name, shape, dt=F32, kind="ExternalInput"):
        return self.nc.dram_tensor(name, shape, dt, kind=kind).ap()

    def V(self, fn, reads=(), writes=()):
        return self.S.op("dve", fn, reads, writes)

    def A(self, fn, reads=(), writes=()):
        return self.S.op("act", fn, reads, writes)

    def P(self, fn, reads=(), writes=()):
        return self.S.op("pool", fn, reads, writes)

    def MM(self, out, lhsT, rhs, start, stop, reads, writes, inc, skip=False):
        return self.S.op("pe", lambda e: e.matmul(out, lhsT=lhsT, rhs=rhs, start=start, stop=stop,
                                                  skip_group_check=skip), reads, writes, inc=inc)

    def TR(self, out, in_, ident, reads, writes, inc):
        return self.S.op("pe", lambda e: e.transpose(out, in_, ident), reads, writes, inc=inc)

    def barrier(self):
        S = self.S
        for eng in ("pe", "dve", "act", "pool"):
            assert not S.pending[eng], eng
        deps = {k: (S.sem[k], S.cnt[k]) for k in S.sem if S.cnt[k] > 0}
        for name, (sem, val) in S.dsem.items():
            if val > 0 and not name.startswith("wc"):
                deps["dma:" + name] = (sem, val)
        for eng in S.engs:
            d = {k: v for k, v in deps.items() if k != eng}
            S._wait(eng, d)

    def wstream_begin(self, order):
        self.w_order = list(order)
        self.w_issued = 0
        self.w_handles = {}
        self.w_occ = {}

    def _w_issue(self):
        i = self.w_issued
        fam, g = self.w_order[i]
        off, ng, kc, cols = WINFO[fam]
        n = kc * cols
        o = off + g * n
        if n <= 1024:
            k = self.ws_small_i % len(self.ws_small)
            self.ws_small_i += 1
            t, b = self.ws_small[k]
        else:
            k = self.ws_big_i % len(self.ws_big)
            self.ws_big_i += 1
            t, b = self.ws_big[k]
        rd = [self.B_wsc] if o + n <= self.wsplit else [self.B_wsc, self.B_wscB]
        self.S.dma("sp", b.name, t[:, 0:n], self.wsc[:, o:o + n], reads=rd, writes=[b])
        self.w_handles[i] = (t[:, 0:n].rearrange("p (a b) -> p a b", a=kc), b)
        self.w_issued += 1

    def _w_next_slot(self):
        fam, g = self.w_order[self.w_issued]
        off, ng, kc, cols = WINFO[fam]
        if kc * cols <= 1024:
            return ("s", self.ws_small_i % len(self.ws_small))
        return ("b", self.ws_big_i % len(self.ws_big))

    def wget(self, i, la=2):
        while self.w_issued < min(len(self.w_order), i + 1 + la):
            slot = self._w_next_slot()
            occ = self.w_occ.get(slot)
            if occ is not None and occ >= i and self.w_issued > i:
                break
            self.w_occ[slot] = self.w_issued
            self._w_issue()
        h = self.w_handles.pop(i)
        return h

    def setup(self):
        nc, nseq = self.nc, self.nseq
        self.S = S = Sched(nc, self.ctx)
        self.x_d = self.dram("x", [nseq, 2048, D])
        self.p_d = self.dram("p", [nseq, 2048, PLE])
        self.pos_d = self.dram("pos", [nseq, 2048], I32)
        self.wall_d = self.dram("wall", [128, WTOT])
        self.cst_d = self.dram("cst", [128, CST])
        self.vec_d = self.dram("vec", [128, VEC])
        self.out_d = self.dram("out", [nseq, 2048, D], kind="ExternalOutput")
        self.wsc = self.dram("wsc", [128, WTOT], BF16, kind="Internal")
        self.dbg_d = {}
        for name, shape in self.dbg.items():
            if name.startswith("_"):
                continue
            self.dbg_d[name] = self.dram("dbg_" + name, shape, kind="ExternalOutput")
        for n in ("ld", "wc", "wl", "xl", "pl", "st", "dbg"):
            S.new_dma_sem(n)
        self.cst = self.sb("cst", [128, CST], F32)
        self.vec = self.sb("vec", [128, VEC], F32)
        self.B_cst = Buf("cst")
        self.B_wsc = Buf("wsc")
        S.dma("sp", "ld", self.cst[:], self.cst_d[:], writes=[self.B_cst])
        S.dma("sp", "ld", self.vec[:], self.vec_d[:], writes=[self.B_cst])
        CH = 8192
        self.B_wscB = Buf("wscB")
        S.new_dma_sem("wc2")
        self.wsplit = ((WINFO["gate"][0] + CH - 1) // CH) * CH
        for o in range(0, self.wsplit, CH):
            n = min(CH, WTOT - o)
            S.dma("pool", "wc", self.wsc[:, o:o + n], self.wall_d[:, o:o + n], writes=[self.B_wsc])
        self.ident = self.cst[:, C_ID:C_ID + 128]
        self.triT = self.cst[:, C_TRI:C_TRI + 128]
        self.cmask = self.cst[:, C_CM:C_CM + 128]
        self.ones = self.cst[:, C_ONE:C_ONE + 128]
        self.identb = self.sb("identb", [128, 128], BF16)
        self.negI4 = self.sb("negI4", [128, 512], BF16)
        self.zerob = self.sb("zerob", [128, 512], BF16)
        self.B_k = Buf("kconst")
        self.V(lambda e: e.tensor_copy(out=self.identb[:], in_=self.ident), reads=[self.B_cst], writes=[self.B_k])
        for r in range(4):
            self.V(lambda e, r=r: e.tensor_copy(out=self.negI4[:, r * 128:(r + 1) * 128], in_=self.cst[:, C_NEGI:C_NEGI + 128]),
                   reads=[self.B_cst], writes=[self.B_k])
        self.V(lambda e: e.memset(self.zerob[:], 0.0), writes=[self.B_k])
        self.ps = [self.ctx.enter_context(nc.psum_tensor("ps%d" % i, [128, 512], F32)) for i in range(8)]
        self.PB = [Buf("ps%d" % i) for i in range(8)]
        self.ws_small = [(self.sb("wss%d" % i, [128, 1024], BF16), Buf("wss%d" % i)) for i in range(4)]
        self.ws_big = [(self.sb("wsb%d" % i, [128, 4096], BF16), Buf("wsb%d" % i)) for i in range(2)]
        self.ws_small_i = 0
        self.ws_big_i = 0
        for t_, b_ in self.ws_small + self.ws_big:
            S.new_dma_sem(b_.name)
        self.xs = [(self.sb("xs%d" % i, [128, D], F32), Buf("xs%d" % i)) for i in range(2)]
        self.xs_i = 0
        for t_, b_ in self.xs:
            S.new_dma_sem(b_.name)

    def cast_rest(self, after_bufs):
        S = self.S
        S.wait_all("pool", after_bufs)
        CH = 8192
        for o in range(self.wsplit, WTOT, CH):
            n = min(CH, WTOT - o)
            S.dma("pool", "wc2", self.wsc[:, o:o + n], self.wall_d[:, o:o + n], writes=[self.B_wscB])

    def dump(self, name, ap, bufs):
        if name in self.dbg_d:
            self.S.dma("pool", "dbg", self.dbg_d[name], ap, reads=bufs)

    def finish(self):
        S = self.S
        for name in S.dsem:
            sem, val = S.dsem[name]
            if val > 0:
                self.nc.gpsimd.wait_ge(sem, val)
        self.ctx.close()
        return self.nc


def _phaseA(self, s, xT, B_xT):
    S = self.S
    for t in range(NT):
        xs, bx = self.xs[self.xs_i % 2]
        self.xs_i += 1
        S.dma("sp", bx.name, xs[:], self.x_d[s, t * 128:(t + 1) * 128, :], writes=[bx])
        pair = (t % 4) * 2
        for c in range(8):
            bk = pair + c // 4
            self.TR(self.ps[bk][:, (c % 4) * 128:(c % 4 + 1) * 128], xs[:, c * 128:(c + 1) * 128], self.ident,
                    reads=[bx, self.B_cst], writes=[self.PB[bk]], inc=(c % 4 == 3))
        self.V(lambda e: e.tensor_copy(out=xT[:, 0:4, t * 128:(t + 1) * 128],
                                       in_=self.ps[pair][:].rearrange("p (a b) -> p a b", a=4)),
               reads=[self.PB[pair]], writes=[B_xT])
        self.A(lambda e: e.copy(out=xT[:, 4:8, t * 128:(t + 1) * 128],
                                in_=self.ps[pair + 1][:].rearrange("p (a b) -> p a b", a=4)),
               reads=[self.PB[pair + 1]], writes=[B_xT])


K.phaseA = _phaseA


def _rope_tables(self, s, cosT, sinT, posi, ang, tmp, Bt):
    S = self.S
    MAGIC = 12582912.0
    C1 = 6.28125
    C2 = 2 * PI - C1
    vec = self.vec
    S.dma("sp", "ld", posi[:], self.pos_d[s:s + 1, :].partition_broadcast(128), writes=[Bt])
    self.V(lambda e: e.tensor_copy(out=ang[:], in_=posi[:]), reads=[Bt], writes=[Bt])
    self.V(lambda e: e.tensor_scalar(out=ang[:], in0=ang[:], scalar1=vec[:, V_INVF:V_INVF + 1], scalar2=None, op0=ALU.mult),
           reads=[Bt, self.B_cst], writes=[Bt])

    def red(dst):
        self.V(lambda e: e.tensor_scalar(out=dst[:], in0=ang[:], scalar1=1.0 / (2 * PI), scalar2=MAGIC, op0=ALU.mult, op1=ALU.add),
               reads=[Bt], writes=[Bt])
        self.V(lambda e: e.tensor_scalar(out=dst[:], in0=dst[:], scalar1=-MAGIC, scalar2=None, op0=ALU.add), reads=[Bt], writes=[Bt])
        self.V(lambda e: e.scalar_tensor_tensor(out=tmp[:], in0=dst[:], scalar=-C1, in1=ang[:], op0=ALU.mult, op1=ALU.add),
               reads=[Bt], writes=[Bt])
        self.V(lambda e: e.scalar_tensor_tensor(out=dst[:], in0=dst[:], scalar=-C2, in1=tmp[:], op0=ALU.mult, op1=ALU.add),
               reads=[Bt], writes=[Bt])
        self.V(lambda e: e.tensor_scalar(out=dst[:], in0=dst[:], scalar1=-3.1415925, scalar2=3.1415925, op0=ALU.max, op1=ALU.min),
               reads=[Bt], writes=[Bt])

    red(sinT)
    self.A(lambda e: e.activation(out=sinT[:], in_=sinT[:], func=AF.Sin), reads=[Bt], writes=[Bt])
    self.V(lambda e: e.tensor_scalar(out=sinT[:], in0=sinT[:], scalar1=vec[:, V_SGN:V_SGN + 1], scalar2=None, op0=ALU.mult),
           reads=[Bt, self.B_cst], writes=[Bt])
    self.V(lambda e: e.tensor_scalar(out=ang[:], in0=ang[:], scalar1=PI / 2, scalar2=None, op0=ALU.add), reads=[Bt], writes=[Bt])
    red(cosT)
    self.A(lambda e: e.activation(out=cosT[:], in_=cosT[:], func=AF.Sin), reads=[Bt], writes=[Bt])


K.rope_tables = _rope_tables


def _proj_fm_banks(self, wh, xT, B_xT, tok0, banks):
    wt, bw = wh
    for j, bk in enumerate(banks):
        for kc in range(8):
            self.MM(self.ps[bk][:, :], lhsT=wt[:, kc, :], rhs=xT[:, kc, tok0 + j * 512: tok0 + (j + 1) * 512],
                    start=(kc == 0), stop=(kc == 7), reads=[bw, B_xT], writes=[self.PB[bk]], inc=(kc == 7))


K.proj_fm_banks = _proj_fm_banks


def _dsa_inproj(self, s, xT, B_xT, T):
    S = self.S
    cosT, sinT, Bt = T["cosT"], T["sinT"], T["Btab"]
    units = []
    for g in range(4):
        units.append((G_QA + g, G_QAP + g, lambda a, b, g=g: [(slice(0, 128), T["qaT"][:, g, a:b])]))
    units.append((G_KA, G_KAP, lambda a, b: [(slice(0, 64), T["kaT"][0:64, 0, a:b]), (slice(64, 128), T["kaT"][64:128, 1, a:b])]))
    for g in range(4):
        units.append((G_QI + g, G_QIP + g, lambda a, b, g=g: [(slice(0, 128), T["qiT"][:, g, a:b])]))
    units.append((G_KI, G_KIP, lambda a, b: [(slice(0, 64), T["kiT"][0:64, 0, a:b]), (slice(64, 128), T["kiT"][64:128, 1, a:b])]))
    order = []
    for (gm, gp, _) in units:
        order += [("fm", gm), ("fm", gp)]
    order.append(("tms", 0))
    self.wstream_begin(order)
    B_dst = T["B_qk"]
    self.P(lambda e: e.memset(T["kaT"][:], 0.0), writes=[B_dst])
    self.P(lambda e: e.memset(T["kiT"][:], 0.0), writes=[B_dst])
    ucount = 0
    for ui, (gm, gp, dst) in enumerate(units):
        whm = self.wget(2 * ui)
        whp = self.wget(2 * ui + 1)
        for half in range(2):
            bset = (ucount % 2) * 4
            ucount += 1
            tok0 = half * 1024
            self.proj_fm_banks(whm, xT, B_xT, tok0, [bset, bset + 1])
            self.proj_fm_banks(whp, xT, B_xT, tok0, [bset + 2, bset + 3])
            for j in range(2):
                a = tok0 + j * 512
                k = (half * 2 + j) % 2
                t1, b1 = T["rt1"][k]
                t2, b2 = T["rt2"][k]
                self.V(lambda e, j=j, a=a, t1=t1: e.tensor_tensor(out=t1[:], in0=self.ps[bset + j][:], in1=cosT[:, a:a + 512], op=ALU.mult),
                       reads=[self.PB[bset + j], Bt], writes=[b1])
                self.V(lambda e, j=j, a=a, t2=t2: e.tensor_tensor(out=t2[:], in0=self.ps[bset + 2 + j][:], in1=sinT[:, a:a + 512], op=ALU.mult),
                       reads=[self.PB[bset + 2 + j], Bt], writes=[b2])
                for psl, dap in dst(a, a + 512):
                    self.P(lambda e, t1=t1, t2=t2, psl=psl, dap=dap: e.tensor_tensor(out=dap, in0=t1[psl, :], in1=t2[psl, :], op=ALU.add),
                           reads=[b1, b2], writes=[B_dst])
    wt, bw = self.wget(len(order) - 1)
    va, wif, B_tm = T["va"], T["wif"], T["B_tm"]
    self.V(lambda e: e.memset(va[:, :, 64:128], 1.0), writes=[B_tm])
    for t in range(NT):
        bk = t % 4
        for kc in range(8):
            self.MM(self.ps[bk][:, 0:80], lhsT=xT[:, kc, t * 128:(t + 1) * 128], rhs=wt[:, kc, :],
                    start=(kc == 0), stop=(kc == 7), reads=[bw, B_xT], writes=[self.PB[bk]], inc=(kc == 7))
        self.A(lambda e, t=t, bk=bk: e.copy(out=va[:, t, 0:64], in_=self.ps[bk][:, 0:64]), reads=[self.PB[bk]], writes=[B_tm])
        self.V(lambda e, t=t, bk=bk: e.tensor_copy(out=wif[:, t, :], in_=self.ps[bk][:, 64:80]), reads=[self.PB[bk]], writes=[B_tm])


K.dsa_inproj = _dsa_inproj


def _dsa_score_units(self, j, T):
    n = 128 * (j + 1)
    sc, bscs = T["sc"][j % 2]
    wif = T["wif"]
    B_qk, B_tm = T["B_qk"], T["B_tm"]
    qiT, kiT = T["qiT"], T["kiT"]
    units = []
    nchunk = (n + 511) // 512
    for c in range(nchunk):
        w = min(512, n - c * 512)
        for h in range(8):
            def unit(c=c, w=w, h=h, last=(c == nchunk - 1 and h == 7)):
                bsc = bscs[c]
                cnt = T["scnt"][0]
                T["scnt"][0] += 1
                bk = cnt % 2
                self.MM(self.ps[bk][:, 0:w], lhsT=qiT[:, h // 2, j * 128:(j + 1) * 128], rhs=kiT[:, h % 2, c * 512:c * 512 + w],
                        start=True, stop=True, reads=[B_qk], writes=[self.PB[bk]], inc=True)
                dst = sc[:, c * 512:c * 512 + w]
                rl, brl = T["rl"][cnt % 4]
                if h == 0:
                    self.V(lambda e: e.tensor_scalar(out=dst, in0=self.ps[bk][:, 0:w], scalar1=0.0, scalar2=wif[:, j, 0:1],
                                                     op0=ALU.max, op1=ALU.mult), reads=[self.PB[bk], B_tm], writes=[bsc])
                elif h <= 3:
                    self.A(lambda e: e.activation(out=rl[:, 0:w], in_=self.ps[bk][:, 0:w], func=AF.Relu), reads=[self.PB[bk]], writes=[brl])
                    self.V(lambda e: e.scalar_tensor_tensor(out=dst, in0=rl[:, 0:w], scalar=wif[:, j, h:h + 1], in1=dst,
                                                            op0=ALU.mult, op1=ALU.add), reads=[brl, B_tm, bsc], writes=[bsc])
                else:
                    self.V(lambda e: e.tensor_scalar(out=rl[:, 0:w], in0=self.ps[bk][:, 0:w], scalar1=0.0, scalar2=wif[:, j, h:h + 1],
                                                     op0=ALU.max, op1=ALU.mult), reads=[self.PB[bk], B_tm], writes=[brl])
                    self.P(lambda e: e.tensor_tensor(out=dst, in0=dst, in1=rl[:, 0:w], op=ALU.add), reads=[brl, bsc], writes=[bsc])
                if last:
                    bd = bscs[(j * 128) // 512]
                    self.V(lambda e: e.tensor_tensor(out=sc[:, j * 128:(j + 1) * 128], in0=sc[:, j * 128:(j + 1) * 128], in1=self.cmask,
                                                     op=ALU.add), reads=[bd, self.B_cst], writes=[bd])
            units.append(unit)
    return units


def _dsa_score(self, j, T):
    for u in self.dsa_score_units(j, T):
        u()


K.dsa_score_units = _dsa_score_units
K.dsa_score = _dsa_score


def _dsa_select(self, j, T):
    n = 128 * (j + 1)
    sc, bscs = T["sc"][j % 2]
    nm, bnm = T["nm"][j % 4]
    sm, bsm = T["sm"][j % 2]
    junk = T["junk"]
    mid, cnt, a, thr = sm[:, 0:1], sm[:, 1:2], sm[:, 2:3], sm[:, 3:4]
    if n <= TOPK:
        self.V(lambda e: e.memset(thr, -1.0e29), writes=[bsm])
    else:
        W = BIS_HI - BIS_LO
        self.V(lambda e: e.memset(mid, BIS_LO + W / 2), writes=[bsm])
        for i in range(NBIS):
            self.V(lambda e: e.tensor_scalar(out=junk[:, 0:n], in0=sc[:, 0:n], scalar1=mid, scalar2=None, op0=ALU.is_ge, op1=ALU.add,
                                             accum_out=cnt), reads=bscs + [bsm], writes=[bsm, T["B_junk"]])
            if i < NBIS - 1:
                st = W / (2 ** (i + 2))
                self.V(lambda e, st=st: e.tensor_scalar(out=a, in0=cnt, scalar1=TOPK - 0.5, scalar2=2 * st, op0=ALU.is_ge, op1=ALU.mult),
                       reads=[bsm], writes=[bsm])
                self.V(lambda e, st=st: e.scalar_tensor_tensor(out=mid, in0=a, scalar=-st, in1=mid, op0=ALU.add, op1=ALU.add),
                       reads=[bsm], writes=[bsm])
            else:
                st = W / (2 ** (i + 1))
                self.V(lambda e, st=st: e.tensor_scalar(out=a, in0=cnt, scalar1=TOPK - 0.5, scalar2=st, op0=ALU.is_ge, op1=ALU.mult),
                       reads=[bsm], writes=[bsm])
                self.V(lambda e, st=st: e.scalar_tensor_tensor(out=thr, in0=a, scalar=-st, in1=mid, op0=ALU.add, op1=ALU.add),
                       reads=[bsm], writes=[bsm])
    self.V(lambda e: e.tensor_scalar(out=nm[:, 0:n], in0=sc[:, 0:n], scalar1=thr, scalar2=None, op0=ALU.is_lt),
           reads=bscs + [bsm], writes=[bnm])


K.dsa_select = _dsa_select


def _dsa_select_act(self, j, T):
    n = 128 * (j + 1)
    sc, bscs = T["sc"][j % 2]
    nm, bnm = T["nm"][j % 4]
    sm, bsm = T["sm"][j % 2]
    junk = T["junk2"]
    nmid, cnt, a, thr = sm[:, 0:1], sm[:, 1:2], sm[:, 2:3], sm[:, 3:4]
    cb = T["cb"][:, j:j + 1]
    W = BIS_HI - BIS_LO
    self.V(lambda e: e.memset(nmid, -(BIS_LO + W / 2)), writes=[bsm])
    for i in range(NBIS):
        self.A(lambda e: e.activation(out=junk[:, 0:n], in_=sc[:, 0:n], func=AF.Sign, bias=nmid, scale=1.0, accum_out=cnt),
               reads=bscs + [bsm], writes=[bsm, T["B_junk2"]])
        self.A(lambda e: e.activation(out=a, in_=cnt, func=AF.Sign, bias=cb, scale=1.0), reads=[bsm, T["B_cb"]], writes=[bsm])
        if i < NBIS - 1:
            st = W / (2 ** (i + 2))
            self.A(lambda e: e.activation(out=nmid, in_=a, func=AF.Identity, scale=-st, bias=nmid), reads=[bsm], writes=[bsm])


K.dsa_select_act = _dsa_select_act


def _dsa_select_act_fin(self, j, T):
    n = 128 * (j + 1)
    sc, bscs = T["sc"][j % 2]
    nm, bnm = T["nm"][j % 4]
    sm, bsm = T["sm"][j % 2]
    nmid, cnt, a, thr = sm[:, 0:1], sm[:, 1:2], sm[:, 2:3], sm[:, 3:4]
    W = BIS_HI - BIS_LO
    st = W / (2 ** NBIS)
    self.V(lambda e: e.scalar_tensor_tensor(out=thr, in0=a, scalar=st / 2, in1=nmid, op0=ALU.mult, op1=ALU.subtract),
           reads=[bsm], writes=[bsm])
    self.V(lambda e: e.tensor_scalar(out=thr, in0=thr, scalar1=-st / 2, scalar2=None, op0=ALU.add), reads=[bsm], writes=[bsm])
    self.V(lambda e: e.tensor_scalar(out=nm[:, 0:n], in0=sc[:, 0:n], scalar1=thr, scalar2=None, op0=ALU.is_lt),
           reads=bscs + [bsm], writes=[bnm])


K.dsa_select_act_fin = _dsa_select_act_fin


def _dsa_attn_units(self, j, T):
    nm, bnm = T["nm"][j % 4]
    B_qk, B_tm = T["B_qk"], T["B_tm"]
    qaT, kaT, va = T["qaT"], T["kaT"], T["va"]
    ob = [4, 5] if j % 2 == 0 else [6, 7]
    import os
    var = os.environ.get("DSA_VAR", "")
    def zero_mm():
        for half in range(2):
            self.MM(self.ps[ob[half]][:, 0:264], lhsT=self.zerob[:, 0:128], rhs=self.zerob[:, 0:264], start=True, stop=False,
                    reads=[self.B_k], writes=[self.PB[ob[half]]], inc=False, skip=True)
    pc = T["pcount"]
    steps = [(kt, half) for kt in range(j + 1) for half in range(2)]
    slots = []
    for _ in steps:
        slots.append(pc[0])
        pc[0] += 1

    def emitS(i):
        kt, half = steps[i]
        bk = 2 + (slots[i] % 2)
        pT, bpT = T["pT"][slots[i] % 4]
        self.MM(self.ps[bk][:, :], lhsT=nm[:, kt * 128:(kt + 1) * 128], rhs=self.negI4[:, :], start=True, stop=False,
                reads=[bnm, self.B_k], writes=[self.PB[bk]], inc=False, skip=True)
        for hh in range(4):
            h = half * 4 + hh
            self.MM(self.ps[bk][:, hh * 128:(hh + 1) * 128], lhsT=kaT[:, h % 2, kt * 128:(kt + 1) * 128],
                    rhs=qaT[:, h // 2, j * 128:(j + 1) * 128], start=False, stop=True,
                    reads=[B_qk], writes=[self.PB[bk]], inc=(hh == 3), skip=True)
        self.A(lambda e: e.activation(out=pT[:], in_=self.ps[bk][:], func=AF.Exp, scale=0.125),
               reads=[self.PB[bk]], writes=[bpT])

    def emitPV(i):
        kt, half = steps[i]
        pT, bpT = T["pT"][slots[i] % 4]
        for hh in range(4):
            self.MM(self.ps[ob[half]][:, hh * 66:(hh + 1) * 66], lhsT=pT[:, hh * 128:(hh + 1) * 128], rhs=va[:, kt, 0:66],
                    start=False, stop=(kt == j), reads=[bpT, B_tm], writes=[self.PB[ob[half]]], inc=(hh == 3), skip=True)

    units = []

    def first():
        zero_mm()
        emitS(0)
    units.append(first)
    for i in range(len(steps)):
        def unit(i=i):
            if i + 1 < len(steps):
                emitS(i + 1)
            emitPV(i)
        units.append(unit)
    return units


K.dsa_attn_units = _dsa_attn_units


def _dsa_attn(self, j, T):
    for u in self.dsa_attn_units(j, T):
        u()


K.dsa_attn = _dsa_attn


def interleave(la, lb):
    out = []
    na, nb = len(la), len(lb)
    ia = ib = 0
    while ia < na or ib < nb:
        if ib >= nb or (ia < na and ia * nb <= ib * na):
            out.append(la[ia])
            ia += 1
        else:
            out.append(lb[ib])
            ib += 1
    return out


def _dsa_norm(self, j, T, yaT, B_ya):
    ob = [4, 5] if j % 2 == 0 else [6, 7]
    ya, bya = T["ya"][j % 2]
    rd, brd = T["rden"][j % 2]
    for half in range(2):
        o3 = self.ps[ob[half]][:, 0:264].rearrange("p (h e) -> p h e", e=66)
        self.V(lambda e, o3=o3, half=half: e.reciprocal(out=rd[:, half * 4:(half + 1) * 4], in_=o3[:, :, 64]),
               reads=[self.PB[ob[half]]], writes=[brd])
        self.V(lambda e, o3=o3, half=half: e.tensor_tensor(out=ya[:, half * 256:(half + 1) * 256].rearrange("p (h e) -> p h e", e=64),
                                                          in0=o3[:, :, 0:64],
                                                          in1=rd[:, half * 4:(half + 1) * 4].unsqueeze(2).to_broadcast([128, 4, 64]),
                                                          op=ALU.mult),
               reads=[self.PB[ob[half]], brd], writes=[bya])
    tb = self.ps[0][:].bitcast(BF16)
    for g in range(4):
        self.TR(tb[:, g * 128:(g + 1) * 128], ya[:, g * 128:(g + 1) * 128], self.identb[:], reads=[bya, self.B_k],
                writes=[self.PB[0]], inc=(g == 3))
    self.A(lambda e: e.copy(out=yaT[:, :, j * 128:(j + 1) * 128], in_=tb[:, 0:512].rearrange("p (g t) -> p g t", g=4)),
           reads=[self.PB[0]], writes=[B_ya])


K.dsa_norm = _dsa_norm


def _dsa_phase(self, s, xT, B_xT, yaT, B_ya, wif_keep, B_wifk):
    with ExitStack() as c1:
        T = {}
        T["qaT"] = self.sb("qaT", [128, 4, 2048], BF16, c1)
        T["kaT"] = self.sb("kaT", [128, 2, 2048], BF16, c1)
        T["qiT"] = self.sb("qiT", [128, 4, 2048], BF16, c1)
        T["kiT"] = self.sb("kiT", [128, 2, 2048], BF16, c1)
        T["va"] = self.sb("va", [128, NT, 128], BF16, c1)
        T["wif"] = self.sb("wif", [128, NT, 16], F32, c1)
        T["B_qk"], T["B_tm"] = Buf("qk"), Buf("tm")
        with ExitStack() as c2:
            T["cosT"] = self.sb("cosT", [128, 2048], F32, c2)
            T["sinT"] = self.sb("sinT", [128, 2048], F32, c2)
            posi = self.sb("posi", [128, 2048], I32, c2)
            ang = self.sb("ang", [128, 2048], F32, c2)
            tmp = self.sb("rtmp", [128, 2048], F32, c2)
            T["Btab"] = Buf("tab")
            self.rope_tables(s, T["cosT"], T["sinT"], posi, ang, tmp, T["Btab"])
            self.barrier()
            T["rt1"] = [(ang[:, k * 512:(k + 1) * 512], Buf("rt1")) for k in range(2)]
            T["rt2"] = [(tmp[:, k * 512:(k + 1) * 512], Buf("rt2")) for k in range(2)]
            self.dsa_inproj(s, xT, B_xT, T)
            self.V(lambda e: e.tensor_copy(out=wif_keep[:], in_=T["wif"][:, :, 8:16]), reads=[T["B_tm"]], writes=[B_wifk])
            self.dump("qaT", T["qaT"][:, 0, :], [T["B_qk"]])
            self.dump("kaT", T["kaT"][:, 0, :], [T["B_qk"]])
            self.barrier()
        if self.stop_after == "dsa_inproj":
            return
        if s == 0:
            self.cast_rest([])
        with ExitStack() as c3:
            T["sc"] = [(self.sb("sc%d" % i, [128, 2048], F32, c3), [Buf("sc%d_%d" % (i, c)) for c in range(4)]) for i in range(2)]
            T["nm"] = [(self.sb("nm%d" % i, [128, 2048], BF16, c3), Buf("nm")) for i in range(4)]
            rlf = self.ws_big[0][0][:].bitcast(F32)
            T["rl"] = [(rlf[:, i * 512:(i + 1) * 512], Buf("rl")) for i in range(4)]
            T["pT"] = [(self.sb("pT%d" % i, [128, 512], BF16, c3), Buf("pT")) for i in range(4)]
            T["ya"] = [(self.sb("ya%d" % i, [128, 512], BF16, c3), Buf("ya")) for i in range(2)]
            T["sm"] = [(self.sb("sm%d" % i, [128, 4], F32, c3), Buf("sm")) for i in range(2)]
            T["rden"] = [(self.sb("rden%d" % i, [128, 8], F32, c3), Buf("rden")) for i in range(2)]
            T["junk"] = self.sb("junk", [128, 2048], BF16, c3)
            T["B_junk"] = Buf("junk")
            T["junk2"] = T["junk"]
            T["B_junk2"] = Buf("junk2")
            T["cb"] = self.sb("cbias", [128, NT], F32, c3)
            T["B_cb"] = Buf("cb")
            for jj in range(NT):
                self.V(lambda e, jj=jj: e.memset(T["cb"][:, jj:jj + 1], float(128 * (jj + 1) - 511)), writes=[T["B_cb"]])
            T["pcount"] = [0]
            T["scnt"] = [0]
            ntile = self.dbg.get("_ntile", NT)
            def sel_pair(k2):
                j1, j2 = 2 * k2, 2 * k2 + 1
                if 128 * (j2 + 1) <= TOPK:
                    self.dsa_select(j1, T)
                    self.dsa_select(j2, T)
                else:
                    self.dsa_select_act(j2, T)
                    self.dsa_select(j1, T)
                    self.dsa_select_act_fin(j2, T)

            npair = ntile // 2
            self.dsa_score(0, T)
            self.dsa_score(1, T)
            sel_pair(0)
            for k2 in range(npair):
                j1, j2 = 2 * k2, 2 * k2 + 1
                ua = self.dsa_attn_units(j1, T) + self.dsa_attn_units(j2, T)
                us = (self.dsa_score_units(j1 + 2, T) + self.dsa_score_units(j2 + 2, T)) if k2 + 1 < npair else []
                for u in interleave(ua, us):
                    u()
                if k2 + 1 < npair:
                    sel_pair(k2 + 1)
                self.dsa_norm(j1, T, yaT, B_ya)
                self.dsa_norm(j2, T, yaT, B_ya)
            self.dump("sc", T["sc"][(ntile - 1) % 2][0][:, :], T["sc"][(ntile - 1) % 2][1])
            self.dump("sm", T["sm"][(ntile - 1) % 2][0][:, :], [T["sm"][(ntile - 1) % 2][1]])
            self.dump("yaT", yaT[:, 0, :], [B_ya])
            self.barrier()


K.dsa_phase = _dsa_phase


def _ml_phase(self, s, xT, B_xT, ybT, B_yb, wifk, B_wifk):
    S = self.S
    vec = self.vec
    with ExitStack() as c1:
        mlqT = self.sb("mlqT", [128, 4, 2048], BF16, c1)
        mlkT = self.sb("mlkT", [128, 4, 2048], BF16, c1)
        ktm = [(self.sb("ktm%d" % i, [128, 128], BF16, c1), Buf("ktm")) for i in range(2)]
        mlv = self.sb("mlv", [128, NT, 4, 130], BF16, c1)
        osig = self.sb("osig", [128, NT, 512], BF16, c1)
        G = self.sb("mlG", [128, NT, 12], F32, c1)
        xc = self.sb("xc", [128, 2052], F32, c1)
        acc = self.sb("cacc", [128, 2048], F32, c1)
        C32 = self.sb("C32", [128, 4, 130], F32, c1)
        Cbf = self.sb("Cbf", [128, 4, 130], BF16, c1)
        hm = self.sb("hm", [128, 512], F32, c1)
        yb = self.sb("ybtm", [128, 512], BF16, c1)
        pTm = [(self.sb("pTm%d" % i, [128, 128], BF16, c1), Buf("pTm")) for i in range(2)]
        sm = self.sb("mlsm", [128, 4, 8], F32, c1)
        g4 = self.sb("mlg4", [128, 8], F32, c1)
        junk = self.sb("mljunk", [128, 128], F32, c1)
        junk2 = self.sb("mljunk2", [128, 128], F32, c1)
        B_q, B_k, B_ktm, B_v, B_o, B_G = Buf("mlq"), Buf("mlk"), Buf("mlktm"), Buf("mlv"), Buf("osig"), Buf("G")
        B_xc, B_acc, B_C32, B_Cbf, B_hm, B_yb_tm, B_sm, B_g4, B_j, B_j2 = (Buf("xc"), Buf("acc"), Buf("C32"), Buf("Cbf"), Buf("hm"),
                                                                         Buf("ybtm"), Buf("mlsm"), Buf("g4"), Buf("j"), Buf("j2"))
        order = [("fm", G_MLQ + g) for g in range(4)] + [("fm", G_MLK + g) for g in range(4)] + [("mlv", 0), ("mlo", 0)]
        self.wstream_begin(order)
        self.V(lambda e: e.memset(xc[:, 0:4], 0.0), writes=[B_xc])
        self.P(lambda e: e.memset(mlv[:], 0.0), writes=[B_v])
        ucount = 0
        for gi in range(8):
            wh = self.wget(gi)
            for half in range(2):
                bset = (ucount % 4) * 2
                ucount += 1
                self.proj_fm_banks(wh, xT, B_xT, half * 1024, [bset, bset + 1])
                for j in range(2):
                    a = 4 + half * 1024 + j * 512
                    self.A(lambda e, a=a, bk=bset + j: e.copy(out=xc[:, a:a + 512], in_=self.ps[bk][:]), reads=[self.PB[bset + j]], writes=[B_xc])
            cw = lambda jj: vec[:, V_CW + gi * 4 + jj:V_CW + gi * 4 + jj + 1]
            self.V(lambda e: e.tensor_scalar(out=acc[:], in0=xc[:, 4:2052], scalar1=cw(3), scalar2=vec[:, V_CB + gi:V_CB + gi + 1],
                                             op0=ALU.mult, op1=ALU.add), reads=[B_xc, self.B_cst], writes=[B_acc])
            self.V(lambda e: e.scalar_tensor_tensor(out=acc[:], in0=xc[:, 3:2051], scalar=cw(2), in1=acc[:], op0=ALU.mult, op1=ALU.add),
                   reads=[B_xc, B_acc, self.B_cst], writes=[B_acc])
            self.V(lambda e: e.scalar_tensor_tensor(out=acc[:], in0=xc[:, 2:2050], scalar=cw(1), in1=acc[:], op0=ALU.mult, op1=ALU.add),
                   reads=[B_xc, B_acc, self.B_cst], writes=[B_acc])
            self.V(lambda e: e.scalar_tensor_tensor(out=acc[:], in0=xc[:, 1:2049], scalar=cw(0), in1=acc[:], op0=ALU.mult, op1=ALU.add),
                   reads=[B_xc, B_acc, self.B_cst], writes=[B_acc])
            dst, bd = (mlqT[:, gi, :], B_q) if gi < 4 else (mlkT[:, gi - 4, :], B_k)
            self.A(lambda e, dst=dst: e.activation(out=dst, in_=acc[:], func=AF.Silu), reads=[B_acc], writes=[bd])
        wv, bwv = self.wget(8)
        wo, bwo = self.wget(9)
        LNS = math.log(128.0 ** -0.5)
        ga = self.sb("mlga", [128, NT, 4], F32, c1)
        gb_ = self.sb("mlgb", [128, NT, 4], F32, c1)
        B_ga, B_gb = Buf("ga"), Buf("gb")
        bfb = vec[:, V_BF:V_BF + 4].unsqueeze(1).to_broadcast([128, NT, 4])
        bib = vec[:, V_BI:V_BI + 4].unsqueeze(1).to_broadcast([128, NT, 4])
        self.V(lambda e: e.tensor_tensor(out=ga[:], in0=wifk[:, :, 4:8], in1=bfb, op=ALU.add), reads=[B_wifk, self.B_cst], writes=[B_ga])
        self.A(lambda e: e.activation(out=ga[:], in_=ga[:], func=AF.Exp, scale=-1.0), reads=[B_ga], writes=[B_ga])
        self.V(lambda e: e.tensor_scalar(out=ga[:], in0=ga[:], scalar1=1.0, scalar2=None, op0=ALU.add), reads=[B_ga], writes=[B_ga])
        self.A(lambda e: e.activation(out=ga[:], in_=ga[:], func=AF.Ln), reads=[B_ga], writes=[B_ga])
        gaf = ga[:].rearrange("p t h -> p (t h)")
        self.MM(self.ps[2][:, 0:64], lhsT=self.triT, rhs=gaf, start=True, stop=True, reads=[B_ga, self.B_cst], writes=[self.PB[2]], inc=True)
        self.MM(self.ps[3][:, 0:64], lhsT=self.ones, rhs=gaf, start=True, stop=True, reads=[B_ga, self.B_cst], writes=[self.PB[3]], inc=True)
        cum3 = self.ps[2][:, 0:64].rearrange("p (t h) -> p t h", h=4)
        tot3 = self.ps[3][:, 0:64].rearrange("p (t h) -> p t h", h=4)
        self.V(lambda e: e.tensor_tensor(out=gb_[:], in0=wifk[:, :, 0:4], in1=bib, op=ALU.add), reads=[B_wifk, self.B_cst], writes=[B_gb])
        self.V(lambda e: e.scalar_tensor_tensor(out=gb_[:], in0=gb_[:], scalar=LNS, in1=cum3, op0=ALU.add, op1=ALU.add),
               reads=[B_gb, self.PB[2]], writes=[B_gb])
        self.A(lambda e: e.activation(out=G[:, :, 0:4], in_=gb_[:], func=AF.Exp), reads=[B_gb], writes=[B_G])
        self.A(lambda e: e.activation(out=G[:, :, 4:8], in_=cum3, func=AF.Exp, scale=-1.0), reads=[self.PB[2]], writes=[B_G])
        self.A(lambda e: e.activation(out=G[:, :, 8:12], in_=tot3, func=AF.Exp, scale=-1.0), reads=[self.PB[3]], writes=[B_G])
        for t in range(NT):
            vb = 4 + (t % 2)
            for kc in range(8):
                self.MM(self.ps[vb][:, :], lhsT=xT[:, kc, t * 128:(t + 1) * 128], rhs=wv[:, kc, :], start=(kc == 0), stop=(kc == 7),
                        reads=[bwv, B_xT], writes=[self.PB[vb]], inc=(kc == 7))
            self.V(lambda e, t=t, vb=vb: e.tensor_tensor(out=mlv[:, t, :, 0:128], in0=self.ps[vb][:].rearrange("p (h d) -> p h d", h=4),
                                                       in1=G[:, t, 0:4].unsqueeze(2).to_broadcast([128, 4, 128]), op=ALU.mult),
                   reads=[self.PB[vb], B_G], writes=[B_v])
            self.V(lambda e, t=t: e.tensor_copy(out=mlv[:, t, :, 128], in_=G[:, t, 0:4]), reads=[B_G], writes=[B_v])
            ob = 6 + (t % 2)
            for kc in range(8):
                self.MM(self.ps[ob][:, :], lhsT=xT[:, kc, t * 128:(t + 1) * 128], rhs=wo[:, kc, :], start=(kc == 0), stop=(kc == 7),
                        reads=[bwo, B_xT], writes=[self.PB[ob]], inc=(kc == 7))
            self.A(lambda e, t=t, ob=ob: e.activation(out=osig[:, t, :], in_=self.ps[ob][:], func=AF.Sigmoid), reads=[self.PB[ob]], writes=[B_o])
        pT4 = [(self.sb("pT4_%d" % i, [128, 4, 128], BF16, c1), Buf("pT4")) for i in range(2)]
        kt4 = [(self.sb("kt4_%d" % i, [128, 4, 128], BF16, c1), Buf("kt4")) for i in range(2)]
        st8 = self.sb("mlst8", [128, 10, 4], F32, c1)
        B_st = Buf("st8")
        B_s1, B_s2 = Buf("s1"), Buf("s2")
        B_hmh = [Buf("hm%d" % h) for h in range(4)]
        sD, sND, sR, sF, sS1, sS2, sM, sV, sRS, sT = [st8[:, i, :] for i in range(10)]
        for t in range(NT):
            tl = slice(t * 128, (t + 1) * 128)
            par = t % 2
            pT, bpT = pT4[par]
            kt_, bkt = kt4[par]
            bS = par
            bT = 6 + par
            tbk = self.ps[bT][:].bitcast(BF16)
            last = (t == NT - 1)
            for h in range(4):
                self.MM(self.ps[bS][:, h * 128:(h + 1) * 128], lhsT=mlkT[:, h, tl], rhs=mlqT[:, h, tl], start=True, stop=True,
                        reads=[B_k, B_q], writes=[self.PB[bS]], inc=(h == 3))
            if not last:
                for h in range(4):
                    self.TR(tbk[:, h * 128:(h + 1) * 128], mlkT[:, h, tl], self.identb[:], reads=[B_k, self.B_k], writes=[self.PB[bT]], inc=(h == 3))
            self.V(lambda e: e.tensor_tensor(out=pT[:], in0=self.ps[bS][:].rearrange("p (h s) -> p h s", h=4),
                                             in1=self.triT.unsqueeze(1).to_broadcast([128, 4, 128]), op=ALU.mult),
                   reads=[self.PB[bS], self.B_cst], writes=[bpT])
            if not last:
                self.A(lambda e: e.copy(out=kt_[:], in_=tbk[:, 0:512].rearrange("p (h s) -> p h s", h=4)), reads=[self.PB[bT]], writes=[bkt])
            for h in range(4):
                bn = 2 + h // 2
                c0 = (h % 2) * 130
                self.MM(self.ps[bn][:, c0:c0 + 130], lhsT=pT[:, h, :], rhs=mlv[:, t, h, :], start=True, stop=(t == 0), reads=[bpT, B_v],
                        writes=[self.PB[bn]], inc=(t == 0 and h % 2 == 1), skip=True)
                if t > 0:
                    self.MM(self.ps[bn][:, c0:c0 + 130], lhsT=mlqT[:, h, tl], rhs=Cbf[:, h, :], start=False, stop=True, reads=[B_q, B_Cbf],
                            writes=[self.PB[bn]], inc=(h % 2 == 1), skip=True)
            if not last:
                for h in range(4):
                    bc = 4 + h // 2
                    c0 = (h % 2) * 130
                    self.MM(self.ps[bc][:, c0:c0 + 130], lhsT=kt_[:, h, :], rhs=mlv[:, t, h, :], start=True, stop=True, reads=[bkt, B_v],
                            writes=[self.PB[bc]], inc=(h % 2 == 1), skip=True)
                for b2 in range(2):
                    cv = C32[:, 2 * b2:2 * b2 + 2, :].rearrange("p h e -> p (h e)")
                    if t == 0:
                        self.V(lambda e, b2=b2, cv=cv: e.tensor_copy(out=cv, in_=self.ps[4 + b2][:, 0:260]), reads=[self.PB[4 + b2]], writes=[B_C32])
                    else:
                        self.V(lambda e, b2=b2, cv=cv: e.tensor_tensor(out=cv, in0=self.ps[4 + b2][:, 0:260], in1=cv, op=ALU.add),
                               reads=[self.PB[4 + b2], B_C32], writes=[B_C32])
                self.V(lambda e, t=t: e.tensor_tensor(out=C32[:], in0=C32[:], in1=G[:, t, 8:12].unsqueeze(2).to_broadcast([128, 4, 130]), op=ALU.mult),
                       reads=[B_C32, B_G], writes=[B_C32])
                self.A(lambda e: e.copy(out=Cbf[:], in_=C32[:]), reads=[B_C32], writes=[B_Cbf])
            eb4 = G[:, t, 4:8]
            for b2 in range(2):
                den2 = self.ps[2 + b2][:, 0:260].rearrange("p (h e) -> p h e", e=130)[:, :, 128]
                self.V(lambda e, b2=b2, den2=den2: e.tensor_tensor(out=sD[:, 2 * b2:2 * b2 + 2], in0=den2, in1=eb4[:, 2 * b2:2 * b2 + 2], op=ALU.mult),
                       reads=[self.PB[2 + b2], B_G], writes=[B_st])
            self.V(lambda e: e.tensor_scalar(out=sND, in0=sD, scalar1=-1.0, scalar2=None, op0=ALU.mult), reads=[B_st], writes=[B_st])
            self.V(lambda e: e.tensor_tensor(out=sD, in0=sD, in1=sND, op=ALU.max), reads=[B_st], writes=[B_st])
            self.V(lambda e: e.tensor_scalar(out=sD, in0=sD, scalar1=1.0, scalar2=None, op0=ALU.max), reads=[B_st], writes=[B_st])
            self.V(lambda e: e.reciprocal(out=sR, in_=sD), reads=[B_st], writes=[B_st])
            self.V(lambda e: e.tensor_tensor(out=sF, in0=sR, in1=eb4, op=ALU.mult), reads=[B_st, B_G], writes=[B_st])
            for h in range(4):
                bn = 2 + h // 2
                c0 = (h % 2) * 130
                hs = hm[:, h * 128:(h + 1) * 128]
                self.V(lambda e, bn=bn, c0=c0, hs=hs, h=h: e.tensor_scalar(out=hs, in0=self.ps[bn][:, c0:c0 + 128], scalar1=sF[:, h:h + 1], scalar2=None,
                                                                        op0=ALU.mult, op1=ALU.add, accum_out=sS1[:, h:h + 1]),
                       reads=[self.PB[bn], B_st], writes=[B_hmh[h], B_s1])
                self.A(lambda e, hs=hs, h=h: e.activation(out=junk2[:], in_=hs, func=AF.Square, accum_out=sS2[:, h:h + 1]), reads=[B_hmh[h]], writes=[B_j2, B_s2])
            self.V(lambda e: e.tensor_scalar(out=sM, in0=sS1, scalar1=1.0 / 128, scalar2=None, op0=ALU.mult), reads=[B_st, B_s1], writes=[B_st])
            self.V(lambda e: e.tensor_tensor(out=sT, in0=sM, in1=sM, op=ALU.mult), reads=[B_st], writes=[B_st])
            self.V(lambda e: e.scalar_tensor_tensor(out=sV, in0=sS2, scalar=1.0 / 128, in1=sT, op0=ALU.mult, op1=ALU.subtract), reads=[B_st, B_s2], writes=[B_st])
            self.V(lambda e: e.tensor_scalar(out=sV, in0=sV, scalar1=LN_EPS, scalar2=None, op0=ALU.add), reads=[B_st], writes=[B_st])
            self.A(lambda e: e.activation(out=sRS, in_=sV, func=AF.Sqrt), reads=[B_st], writes=[B_st])
            self.V(lambda e: e.reciprocal(out=sRS, in_=sRS), reads=[B_st], writes=[B_st])
            hm3 = hm[:].rearrange("p (h d) -> p h d", h=4)
            self.V(lambda e: e.tensor_tensor(out=hm3, in0=hm3, in1=sM.unsqueeze(2).to_broadcast([128, 4, 128]), op=ALU.subtract), reads=B_hmh + [B_st], writes=B_hmh)
            self.V(lambda e: e.tensor_tensor(out=hm3, in0=hm3, in1=sRS.unsqueeze(2).to_broadcast([128, 4, 128]), op=ALU.mult), reads=B_hmh + [B_st], writes=B_hmh)
            self.V(lambda e: e.tensor_tensor(out=hm[:], in0=hm[:], in1=vec[:, V_MLG:V_MLG + 512], op=ALU.mult), reads=B_hmh + [self.B_cst], writes=B_hmh)
            self.V(lambda e, t=t: e.tensor_tensor(out=yb[:], in0=hm[:], in1=osig[:, t, :], op=ALU.mult), reads=B_hmh + [B_o], writes=[B_yb_tm])
            tb = self.ps[bT][:].bitcast(BF16)
            for g in range(4):
                self.TR(tb[:, g * 128:(g + 1) * 128], yb[:, g * 128:(g + 1) * 128], self.identb[:], reads=[B_yb_tm, self.B_k],
                        writes=[self.PB[bT]], inc=(g == 3))
            self.A(lambda e, t=t, tb=tb: e.copy(out=ybT[:, :, t * 128:(t + 1) * 128], in_=tb[:, 0:512].rearrange("p (g t) -> p g t", g=4)),
                   reads=[self.PB[bT]], writes=[B_yb])
        self.dump("ybT", ybT[:, 0, :], [B_yb])
        self.barrier()


K.ml_phase = _ml_phase


def _post_phase(self, s, xT, B_xT, yaT, B_ya, ybT, B_yb):
    S = self.S
    vec = self.vec
    with ExitStack() as c1:
        Rs = [self.sb("R%d" % i, [128, 8, 512], F32, c1) for i in range(2)]
        B_Rs = [[Buf("R%d_%d" % (i, c)) for c in range(8)] for i in range(2)]
        Rb = self.sb("Rb", [128, 8, 512], BF16, c1)
        mg = self.sb("mergedT", [128, 8, 512], BF16, c1)
        uT = self.sb("uT", [128, 32, 512], BF16, c1)
        pTb = self.sb("pTb", [128, 2, 512], BF16, c1)
        tmpA = [(self.sb("tmpA%d" % i, [128, 512], F32, c1), Buf("tmpA")) for i in range(4)]
        st = self.sb("lnst", [128, 4, 512], F32, c1)
        onesd = self.sb("onesd", [128, 128], F32, c1)
        pl = [(self.sb("pl%d" % i, [128, 256], F32, c1), Buf("pl%d" % i)) for i in range(4)]
        for t_, b_ in pl:
            if b_.name not in S.dsem:
                S.new_dma_sem(b_.name)
        B_Rb, B_mg, B_uT, B_pT, B_st, B_od = Buf("Rb"), Buf("mg"), Buf("uT"), Buf("pTb"), Buf("lnst"), Buf("onesd")
        self.V(lambda e: e.tensor_scalar(out=onesd[:], in0=self.ones, scalar1=1.0 / 1024, scalar2=None, op0=ALU.mult), reads=[self.B_cst], writes=[B_od])
        bank_i = [0]
        tmp_i = [0]

        def nb():
            b = bank_i[0] % 8
            bank_i[0] += 1
            return b

        def ntmp():
            t = tmpA[tmp_i[0] % 4]
            tmp_i[0] += 1
            return t

        def mmacc(bk, pairs, reads):
            n = len(pairs)
            for i, (l, r) in enumerate(pairs):
                self.MM(self.ps[bk][:, :], lhsT=l, rhs=r, start=(i == 0), stop=(i == n - 1), reads=reads, writes=[self.PB[bk]], inc=(i == n - 1))

        def merge_order():
            o = []
            for fc in range(8):
                o += [("gate", fc), ("upa", fc), ("gate", 8 + fc), ("upb", fc)]
            return o
        order = merge_order() + [("wout", g) for g in range(8)]
        for blk in range(4):
            if blk + 1 < 4:
                order += merge_order()
            order += [("ff1", g) for g in range(32)] + [("ff2", g) for g in range(8)]
            for fc in range(8):
                order += [("pleg", fc), ("plep", fc)]
            if blk + 1 < 4:
                order += [("wout", g) for g in range(8)]
        self.wstream_begin(order)
        wi = [0]

        def nextw():
            h = self.wget(wi[0])
            wi[0] += 1
            return h

        def layernorm(R, B_R, gcol, bcol, write_rb):
            b1, b2 = nb(), nb()
            mmacc(b1, [(onesd[:], R[:, c, :]) for c in range(8)], [B_od] + B_R)
            for c in range(8):
                sq, bsq = ntmp()
                self.A(lambda e: e.activation(out=sq[:], in_=R[:, c, :], func=AF.Square), reads=[B_R[c]], writes=[bsq])
                self.MM(self.ps[b2][:, :], lhsT=onesd[:], rhs=sq[:], start=(c == 0), stop=(c == 7), reads=[B_od, bsq], writes=[self.PB[b2]], inc=True)
            mean, rstd, nmr, t4 = st[:, 0, :], st[:, 1, :], st[:, 2, :], st[:, 3, :]
            self.V(lambda e: e.tensor_copy(out=mean, in_=self.ps[b1][:]), reads=[self.PB[b1]], writes=[B_st])
            self.V(lambda e: e.tensor_tensor(out=t4, in0=mean, in1=mean, op=ALU.mult), reads=[B_st], writes=[B_st])
            self.V(lambda e: e.tensor_tensor(out=t4, in0=self.ps[b2][:], in1=t4, op=ALU.subtract), reads=[B_st, self.PB[b2]], writes=[B_st])
            self.V(lambda e: e.tensor_scalar(out=t4, in0=t4, scalar1=LN_EPS, scalar2=None, op0=ALU.add), reads=[B_st], writes=[B_st])
            self.A(lambda e: e.activation(out=t4, in_=t4, func=AF.Sqrt), reads=[B_st], writes=[B_st])
            self.V(lambda e: e.reciprocal(out=rstd, in_=t4), reads=[B_st], writes=[B_st])
            self.V(lambda e: e.scalar_tensor_tensor(out=nmr, in0=mean, scalar=-1.0, in1=rstd, op0=ALU.mult, op1=ALU.mult), reads=[B_st], writes=[B_st])
            for c in range(8):
                self.V(lambda e: e.tensor_tensor(out=R[:, c, :], in0=R[:, c, :], in1=rstd, op=ALU.mult), reads=[B_R[c], B_st], writes=[B_R[c]])
                self.V(lambda e: e.tensor_tensor(out=R[:, c, :], in0=R[:, c, :], in1=nmr, op=ALU.add), reads=[B_R[c], B_st], writes=[B_R[c]])
                self.A(lambda e: e.activation(out=R[:, c, :], in_=R[:, c, :], func=AF.Identity, scale=vec[:, gcol + c:gcol + c + 1],
                                              bias=vec[:, bcol + c:bcol + c + 1]), reads=[B_R[c], self.B_cst], writes=[B_R[c]])
                if write_rb:
                    self.P(lambda e: e.tensor_copy(out=Rb[:, c, :], in_=R[:, c, :]), reads=[B_R[c]], writes=[B_Rb])

        def do_xT(blk):
            R, B_R = Rs[blk % 2], B_Rs[blk % 2]
            tok0 = blk * 512
            for tt in range(4):
                xs, bx = self.xs[self.xs_i % 2]
                self.xs_i += 1
                S.dma("sp", bx.name, xs[:], self.x_d[s, tok0 + tt * 128: tok0 + (tt + 1) * 128, :], writes=[bx])
                b1, b2 = nb(), nb()
                for c in range(8):
                    bk = b1 if c < 4 else b2
                    self.TR(self.ps[bk][:, (c % 4) * 128:(c % 4 + 1) * 128], xs[:, c * 128:(c + 1) * 128], self.ident,
                            reads=[bx, self.B_cst], writes=[self.PB[bk]], inc=(c % 4 == 3))
                self.V(lambda e: e.tensor_scalar(out=R[:, 0:4, tt * 128:(tt + 1) * 128], in0=self.ps[b1][:].rearrange("p (a b) -> p a b", a=4),
                                                 scalar1=ALPHA, scalar2=None, op0=ALU.mult), reads=[self.PB[b1]], writes=B_R[0:4])
                self.A(lambda e: e.activation(out=R[:, 4:8, tt * 128:(tt + 1) * 128], in_=self.ps[b2][:].rearrange("p (a b) -> p a b", a=4),
                                              func=AF.Copy, scale=ALPHA), reads=[self.PB[b2]], writes=B_R[4:8])

        def do_merge(blk):
            tsl = slice(blk * 512, blk * 512 + 512)
            for fc in range(8):
                ms = []
                for (src, B_src, nk) in ((yaT, B_ya, 4), (ybT, B_yb, 4)):
                    wg, bwg = nextw()
                    wu, bwu = nextw()
                    bg, bu = nb(), nb()
                    mmacc(bg, [(wg[:, kc, :], xT[:, kc, tsl]) for kc in range(8)], [bwg, B_xT])
                    mmacc(bu, [(wu[:, kc, :], src[:, kc, tsl]) for kc in range(nk)], [bwu, B_src])
                    sg, bsg = ntmp()
                    self.A(lambda e: e.activation(out=sg[:], in_=self.ps[bg][:], func=AF.Sigmoid), reads=[self.PB[bg]], writes=[bsg])
                    self.V(lambda e: e.tensor_tensor(out=sg[:], in0=sg[:], in1=self.ps[bu][:], op=ALU.mult), reads=[bsg, self.PB[bu]], writes=[bsg])
                    ms.append((sg, bsg))
                self.P(lambda e: e.tensor_tensor(out=mg[:, fc, :], in0=ms[0][0][:], in1=ms[1][0][:], op=ALU.add),
                       reads=[ms[0][1], ms[1][1]], writes=[B_mg])

        def do_wout(blk):
            R, B_R = Rs[blk % 2], B_Rs[blk % 2]
            for fc in range(8):
                ww, bww = nextw()
                bk = nb()
                mmacc(bk, [(ww[:, kc, :], mg[:, kc, :]) for kc in range(8)], [bww, B_mg])
                self.V(lambda e: e.tensor_tensor(out=R[:, fc, :], in0=self.ps[bk][:], in1=R[:, fc, :], op=ALU.add), reads=[self.PB[bk], B_R[fc]], writes=[B_R[fc]])

        def do_ffn(blk):
            R, B_R = Rs[blk % 2], B_Rs[blk % 2]
            for f in range(32):
                w1, bw1 = nextw()
                bk = nb()
                mmacc(bk, [(w1[:, kc, :], Rb[:, kc, :]) for kc in range(8)], [bw1, B_Rb])
                rl, brl = ntmp()
                self.A(lambda e: e.activation(out=rl[:], in_=self.ps[bk][:], func=AF.Relu), reads=[self.PB[bk]], writes=[brl])
                self.P(lambda e: e.tensor_tensor(out=uT[:, f, :], in0=rl[:], in1=rl[:], op=ALU.mult), reads=[brl], writes=[B_uT])
            for fc in range(8):
                w2, bw2 = nextw()
                bk = nb()
                mmacc(bk, [(w2[:, f, :], uT[:, f, :]) for f in range(32)], [bw2, B_uT])
                self.V(lambda e: e.scalar_tensor_tensor(out=R[:, fc, :], in0=R[:, fc, :], scalar=ALPHA, in1=self.ps[bk][:], op0=ALU.mult, op1=ALU.add),
                       reads=[B_R[fc], self.PB[bk]], writes=[B_R[fc]])
                self.P(lambda e: e.tensor_copy(out=Rb[:, fc, :], in_=R[:, fc, :]), reads=[B_R[fc]], writes=[B_Rb])

        def do_ple(blk):
            R, B_R = Rs[blk % 2], B_Rs[blk % 2]
            tok0 = blk * 512
            pb = [nb(), nb()]
            for tt in range(4):
                pt, bp = pl[tt]
                S.dma("sp", bp.name, pt[:], self.p_d[s, tok0 + tt * 128: tok0 + (tt + 1) * 128, :], writes=[bp])
                for c in range(2):
                    self.TR(self.ps[pb[c]][:, tt * 128:(tt + 1) * 128], pt[:, c * 128:(c + 1) * 128], self.ident, reads=[bp, self.B_cst],
                            writes=[self.PB[pb[c]]], inc=True)
            for c in range(2):
                self.V(lambda e: e.tensor_copy(out=pTb[:, c, :], in_=self.ps[pb[c]][:]), reads=[self.PB[pb[c]]], writes=[B_pT])
            for fc in range(8):
                wg, bwg = nextw()
                wp, bwp = nextw()
                bg, bp2 = nb(), nb()
                mmacc(bg, [(wg[:, kc, :], Rb[:, kc, :]) for kc in range(8)], [bwg, B_Rb])
                mmacc(bp2, [(wp[:, kc, :], pTb[:, kc, :]) for kc in range(2)], [bwp, B_pT])
                sg, bsg = ntmp()
                self.A(lambda e: e.activation(out=sg[:], in_=self.ps[bg][:], func=AF.Sigmoid), reads=[self.PB[bg]], writes=[bsg])
                self.V(lambda e: e.tensor_tensor(out=sg[:], in0=sg[:], in1=self.ps[bp2][:], op=ALU.mult), reads=[bsg, self.PB[bp2]], writes=[bsg])
                self.P(lambda e: e.tensor_tensor(out=R[:, fc, :], in0=R[:, fc, :], in1=sg[:], op=ALU.add), reads=[B_R[fc], bsg], writes=[B_R[fc]])

        def do_out(blk):
            R, B_R = Rs[blk % 2], B_Rs[blk % 2]
            tok0 = blk * 512
            for tt in range(4):
                o, bo = self.xs[self.xs_i % 2]
                self.xs_i += 1
                b1, b2 = nb(), nb()
                for c in range(8):
                    bk = b1 if c < 4 else b2
                    self.TR(self.ps[bk][:, (c % 4) * 128:(c % 4 + 1) * 128], R[:, c, tt * 128:(tt + 1) * 128], self.ident,
                            reads=[B_R[c], self.B_cst], writes=[self.PB[bk]], inc=(c % 4 == 3))
                self.V(lambda e: e.tensor_copy(out=o[:, 0:512], in_=self.ps[b1][:]), reads=[self.PB[b1]], writes=[bo])
                self.A(lambda e: e.copy(out=o[:, 512:1024], in_=self.ps[b2][:]), reads=[self.PB[b2]], writes=[bo])
                S.dma("pool", bo.name, self.out_d[s, tok0 + tt * 128: tok0 + (tt + 1) * 128, :], o[:], reads=[bo])

        do_xT(0)
        do_merge(0)
        do_wout(0)
        for blk in range(4):
            R, B_R = Rs[blk % 2], B_Rs[blk % 2]
            layernorm(R, B_R, V_L1G, V_L1B, True)
            if blk == 0:
                self.dump("h1T", R[:, 0, :], [B_R[0]])
            if blk + 1 < 4:
                do_merge(blk + 1)
            do_ffn(blk)
            do_ple(blk)
            layernorm(R, B_R, V_L2G, V_L2B, False)
            if blk + 1 < 4:
                do_xT(blk + 1)
                do_wout(blk + 1)
            do_out(blk)
        self.barrier()


K.post_phase = _post_phase


def build_program(nseq=NSEQ, dbg=None):
    k = K(nseq=nseq, dbg=dbg)
    k.setup()
    for s in range(nseq):
        with ExitStack() as c0:
            xT = k.sb("xT", [128, 8, 2048], BF16, c0)
            yaT = k.sb("yaT", [128, 4, 2048], BF16, c0)
            ybT = k.sb("ybT", [128, 4, 2048], BF16, c0)
            wifk = k.sb("wifk", [128, NT, 8], F32, c0)
            B_xT, B_ya, B_yb, B_w = Buf("xT"), Buf("yaT"), Buf("ybT"), Buf("wifk")
            k.phaseA(s, xT, B_xT)
            k.dsa_phase(s, xT, B_xT, yaT, B_ya, wifk, B_w)
            k.ml_phase(s, xT, B_xT, ybT, B_yb, wifk, B_w)
            k.post_phase(s, xT, B_xT, yaT, B_ya, ybT, B_yb)
            k.barrier()
    return k.finish(), k


_CACHE = {}


def kernel(x, p, positions, w_in, conv_w, conv_b, b_igate, b_fgate, ml_norm_g, w_up_a, w_up_b, w_out,
           ln1_g, ln1_b, w_ff1, w_ff2, w_ple_gate, w_ple_proj, ln2_g, ln2_b):
    f = lambda a: np.asarray(a, dtype=np.float32)
    wall = host_weights(f(w_in)[0], f(w_up_a)[0], f(w_up_b)[0], f(w_out)[0], f(w_ff1)[0], f(w_ff2)[0], f(w_ple_gate)[0], f(w_ple_proj)[0])
    cst, vec = host_consts(f(conv_w)[0], f(conv_b)[0], f(b_igate)[0], f(b_fgate)[0], f(ml_norm_g)[0], f(ln1_g)[0], f(ln1_b)[0],
                           f(ln2_g)[0], f(ln2_b)[0])
    x = f(x)
    p = f(p)[0]
    pos = np.asarray(positions, dtype=np.int32)
    if "nc" not in _CACHE:
        _CACHE["nc"] = build_program(NSEQ)[0]
    nc = _CACHE["nc"]
    in_maps = []
    for c in range(NCORES):
        sl = slice(c * NSEQ, (c + 1) * NSEQ)
        in_maps.append(dict(x=np.ascontiguousarray(x[sl]), p=np.ascontiguousarray(p[sl]), pos=np.ascontiguousarray(pos[sl]),
                            wall=wall, cst=cst, vec=vec))
    res = run_bass_kernel_spmd(nc, in_maps, core_ids=list(range(NCORES)))
    out = np.concatenate([np.asarray(r["out"], dtype=np.float32) for r in res.results], axis=0)
    return out
```
